# Optimizing a Trainium2 kernel written in Bass

```python
import jax, jax.numpy as jnp
from jax import lax
import numpy as np

D_MODEL = 1024
BATCH = 16
SEQ = 4096
DEPTH = 2

N_MIXERS = 2
N_GLA_LAYERS = (DEPTH + N_MIXERS - 1) // N_MIXERS
N_ATT_LAYERS = DEPTH // N_MIXERS
N_META = 16
GRID_W = 64
D_FF = 2816
NORM_EPS = 1e-6
MACARON_WEIGHT = 0.5

GLA_HEADS = 4
GLA_DK = D_MODEL // 2 // GLA_HEADS
GLA_DV = D_MODEL // GLA_HEADS
GLA_QK_W = GLA_HEADS * GLA_DK
GLA_V_W = GLA_HEADS * GLA_DV
GLA_GATE_RANK = 16
GLA_GATE_TAU = 16.0
GLA_CHUNK = 64
GLA_PAD = GLA_CHUNK - N_META

ATT_Q_HEADS = 8
ATT_KV_HEADS = 2
ATT_HEAD_DIM = D_MODEL // ATT_Q_HEADS
ATT_GROUP = ATT_Q_HEADS // ATT_KV_HEADS
ATT_Q_W = ATT_Q_HEADS * ATT_HEAD_DIM
ATT_KV_W = ATT_KV_HEADS * ATT_HEAD_DIM
ATT_BLOCK = 128
ROPE_THETA = 10000.0
ROPE_AXIS_DIM = ATT_HEAD_DIM // 2

kernel_name = "hybrid_gla_axial_gqa_macaron_encoder"


def rms_norm(x, gain):
    xf = x.astype(jnp.float32)
    y = xf * lax.rsqrt(jnp.mean(xf * xf, axis=-1, keepdims=True) + NORM_EPS)
    return (y * gain.astype(jnp.float32)).astype(x.dtype)


def swiglu(h, w_gate, w_up, w_down):
    return (jax.nn.silu(h @ w_gate) * (h @ w_up)) @ w_down


def gla_chunk_scan(q, k, v, logg):
    b = jnp.cumsum(logg, axis=3)
    b_last = b[:, :, :, -1:, :]
    q_dec = q * jnp.exp(b)
    k_inv = k * jnp.exp(-b)
    k_end = k * jnp.exp(b_last - b)
    t_len = q.shape[3]
    causal = jnp.tril(jnp.ones((t_len, t_len), dtype=bool))
    a = jnp.where(causal, jnp.einsum('bhctd,bhcsd->bhcts', q_dec, k_inv), 0.0)
    o_intra = jnp.einsum('bhcts,bhcsv->bhctv', a, v)
    decay = jnp.exp(b_last[:, :, :, 0, :])

    def step(state, xs):
        q_c, k_c, v_c, dec_c = xs
        out = jnp.einsum('bhtd,bhdv->bhtv', q_c, state)
        state = dec_c[..., None] * state + jnp.einsum('bhsd,bhsv->bhdv', k_c, v_c)
        return state, out

    xs = (jnp.moveaxis(q_dec, 2, 0), jnp.moveaxis(k_end, 2, 0),
          jnp.moveaxis(v, 2, 0), jnp.moveaxis(decay, 2, 0))
    state0 = jnp.zeros((q.shape[0], q.shape[1], q.shape[4], v.shape[4]), q.dtype)
    _, o_inter = lax.scan(step, state0, xs)
    return o_intra + jnp.moveaxis(o_inter, 0, 2)


def gla_mixer(h, w_in, gate_w1, gate_w2, gate_b, head_norm, w_out):
    bsz, seq_len, _ = h.shape
    q, k, v, r = jnp.split(h @ w_in, [GLA_QK_W, 2 * GLA_QK_W, 2 * GLA_QK_W + GLA_V_W], axis=-1)
    q = q * GLA_DK ** -0.5
    z = jnp.einsum('nblr,nrk->nblk', jnp.einsum('bld,ndr->nblr', h, gate_w1), gate_w2) + gate_b[:, None, None, :]
    logg = jax.nn.log_sigmoid(z.astype(jnp.float32)) / GLA_GATE_TAU

    def to_chunks(t, d):
        t = jnp.pad(t.astype(jnp.float32), ((0, 0), (GLA_PAD, 0), (0, 0)))
        n_chunks = t.shape[1] // GLA_CHUNK
        return t.reshape(bsz, n_chunks, GLA_CHUNK, GLA_HEADS, d).transpose(0, 3, 1, 2, 4)

    qc, kc, vc = to_chunks(q, GLA_DK), to_chunks(k, GLA_DK), to_chunks(v, GLA_DV)
    g_fwd, g_bwd = to_chunks(logg[0], GLA_DK), to_chunks(logg[1], GLA_DK)
    flip = lambda t: jnp.flip(t, axis=(2, 3))
    o_fwd = gla_chunk_scan(qc, kc, vc, g_fwd)
    o_bwd = flip(gla_chunk_scan(flip(qc), flip(kc), flip(vc), flip(g_bwd)))
    o = (o_fwd + o_bwd).transpose(0, 2, 3, 1, 4)
    o = o.reshape(bsz, -1, GLA_HEADS, GLA_DV)[:, GLA_PAD:]
    o = rms_norm(o, head_norm) * jax.nn.silu(r.astype(jnp.float32)).reshape(bsz, seq_len, GLA_HEADS, GLA_DV)
    return o.reshape(bsz, seq_len, GLA_V_W).astype(h.dtype) @ w_out


def axial_rope_tables(n_real):
    rows = n_real // GRID_W
    t_row = jnp.broadcast_to(jnp.arange(rows)[:, None], (rows, GRID_W)).reshape(-1)
    t_col = jnp.broadcast_to(jnp.arange(GRID_W)[None, :], (rows, GRID_W)).reshape(-1)
    meta = jnp.zeros((N_META,), t_row.dtype)
    t_row = jnp.concatenate([meta, t_row]).astype(jnp.float32)
    t_col = jnp.concatenate([meta, t_col]).astype(jnp.float32)
    inv_freq = ROPE_THETA ** (-jnp.arange(0, ROPE_AXIS_DIM, 2, dtype=jnp.float32) / ROPE_AXIS_DIM)
    ang_row = t_row[:, None] * inv_freq[None, :]
    ang_col = t_col[:, None] * inv_freq[None, :]
    return jnp.cos(ang_row), jnp.sin(ang_row), jnp.cos(ang_col), jnp.sin(ang_col)


def rotate_half_pairs(x, cos, sin):
    x1, x2 = jnp.split(x, 2, axis=-1)
    c, s = cos[None, :, None, :], sin[None, :, None, :]
    return jnp.concatenate([x1 * c - x2 * s, x2 * c + x1 * s], axis=-1)


def apply_axial_rope(x, tables):
    cos_r, sin_r, cos_c, sin_c = tables
    xf = x.astype(jnp.float32)
    out = jnp.concatenate([rotate_half_pairs(xf[..., :ROPE_AXIS_DIM], cos_r, sin_r),
                           rotate_half_pairs(xf[..., ROPE_AXIS_DIM:], cos_c, sin_c)], axis=-1)
    return out.astype(x.dtype)


def attn_mixer(h, w_in, q_norm, k_norm, w_out):
    bsz, seq_len, _ = h.shape
    n_real = seq_len - N_META
    q, k, v = jnp.split(h @ w_in, [ATT_Q_W, ATT_Q_W + ATT_KV_W], axis=-1)
    q = rms_norm(q.reshape(bsz, seq_len, ATT_Q_HEADS, ATT_HEAD_DIM), q_norm)
    k = rms_norm(k.reshape(bsz, seq_len, ATT_KV_HEADS, ATT_HEAD_DIM), k_norm)
    v = v.reshape(bsz, seq_len, ATT_KV_HEADS, ATT_HEAD_DIM)
    tables = axial_rope_tables(n_real)
    q = apply_axial_rope(q, tables) * ATT_HEAD_DIM ** -0.5
    k = apply_axial_rope(k, tables)
    q = q.reshape(bsz, seq_len, ATT_KV_HEADS, ATT_GROUP, ATT_HEAD_DIM)

    def attend(q_blk):
        s = jnp.einsum('bqkgd,bskd->bkgqs', q_blk, k).astype(jnp.float32)
        p = jax.nn.softmax(s, axis=-1).astype(v.dtype)
        return jnp.einsum('bkgqs,bskd->bqkgd', p, v)

    o_meta = attend(q[:, :N_META])
    n_blk = n_real // ATT_BLOCK
    q_blocks = jnp.moveaxis(q[:, N_META:].reshape(bsz, n_blk, ATT_BLOCK, ATT_KV_HEADS, ATT_GROUP, ATT_HEAD_DIM), 1, 0)
    o_real = jnp.moveaxis(lax.map(attend, q_blocks), 0, 1).reshape(bsz, n_real, ATT_KV_HEADS, ATT_GROUP, ATT_HEAD_DIM)
    o = jnp.concatenate([o_meta, o_real], axis=1).reshape(bsz, seq_len, ATT_Q_W)
    return o @ w_out


def setup_inputs(seed: int = 0) -> dict:
    key = jax.random.key(seed)
    ks = iter(jax.random.split(key, 32))
    nrm = lambda shape, fan_in: jax.random.normal(next(ks), shape, jnp.float32) * fan_in ** -0.5
    gain = lambda shape: 1.0 + 0.02 * jax.random.normal(next(ks), shape, jnp.float32)
    return {
        "x": jax.random.normal(next(ks), (BATCH, SEQ, D_MODEL), jnp.float32),
        "meta_tokens": jax.random.normal(next(ks), (N_META, D_MODEL), jnp.float32),
        "norm_ffn1": gain((DEPTH, D_MODEL)),
        "ffn1_w_gate": nrm((DEPTH, D_MODEL, D_FF), D_MODEL),
        "ffn1_w_up": nrm((DEPTH, D_MODEL, D_FF), D_MODEL),
        "ffn1_w_down": nrm((DEPTH, D_FF, D_MODEL), D_FF),
        "norm_mix": gain((DEPTH, D_MODEL)),
        "gla_w_in": nrm((N_GLA_LAYERS, D_MODEL, 2 * GLA_QK_W + 2 * GLA_V_W), D_MODEL),
        "gla_gate_w1": nrm((N_GLA_LAYERS, 2, D_MODEL, GLA_GATE_RANK), D_MODEL),
        "gla_gate_w2": nrm((N_GLA_LAYERS, 2, GLA_GATE_RANK, GLA_QK_W), GLA_GATE_RANK),
        "gla_gate_b": 0.1 * jax.random.normal(next(ks), (N_GLA_LAYERS, 2, GLA_QK_W), jnp.float32),
        "gla_head_norm": gain((N_GLA_LAYERS, GLA_DV)),
        "gla_w_out": nrm((N_GLA_LAYERS, GLA_V_W, D_MODEL), GLA_V_W),
        "attn_w_in": nrm((N_ATT_LAYERS, D_MODEL, ATT_Q_W + 2 * ATT_KV_W), D_MODEL),
        "attn_q_norm": gain((N_ATT_LAYERS, ATT_HEAD_DIM)),
        "attn_k_norm": gain((N_ATT_LAYERS, ATT_HEAD_DIM)),
        "attn_w_out": nrm((N_ATT_LAYERS, ATT_Q_W, D_MODEL), ATT_Q_W),
        "norm_ffn2": gain((DEPTH, D_MODEL)),
        "ffn2_w_gate": nrm((DEPTH, D_MODEL, D_FF), D_MODEL),
        "ffn2_w_up": nrm((DEPTH, D_MODEL, D_FF), D_MODEL),
        "ffn2_w_down": nrm((DEPTH, D_FF, D_MODEL), D_FF),
        "norm_final": gain((D_MODEL,)),
    }


def reference(x, meta_tokens, norm_ffn1, ffn1_w_gate, ffn1_w_up, ffn1_w_down, norm_mix,
              gla_w_in, gla_gate_w1, gla_gate_w2, gla_gate_b, gla_head_norm, gla_w_out,
              attn_w_in, attn_q_norm, attn_k_norm, attn_w_out,
              norm_ffn2, ffn2_w_gate, ffn2_w_up, ffn2_w_down, norm_final):
    bsz = x.shape[0]
    meta = jnp.broadcast_to(meta_tokens.astype(x.dtype)[None], (bsz, N_META, x.shape[-1]))
    h = jnp.concatenate([meta, x], axis=1)
    for i in range(DEPTH):
        h = h + MACARON_WEIGHT * swiglu(rms_norm(h, norm_ffn1[i]), ffn1_w_gate[i], ffn1_w_up[i], ffn1_w_down[i])
        hn = rms_norm(h, norm_mix[i])
        j = i // N_MIXERS
        if i % N_MIXERS == 0:
            h = h + gla_mixer(hn, gla_w_in[j], gla_gate_w1[j], gla_gate_w2[j], gla_gate_b[j], gla_head_norm[j], gla_w_out[j])
        else:
            h = h + attn_mixer(hn, attn_w_in[j], attn_q_norm[j], attn_k_norm[j], attn_w_out[j])
        h = h + MACARON_WEIGHT * swiglu(rms_norm(h, norm_ffn2[i]), ffn2_w_gate[i], ffn2_w_up[i], ffn2_w_down[i])
    return rms_norm(h[:, N_META:], norm_final)
```

```python
import numpy as np
import concourse.bass as bass
import concourse.mybir as mybir
from concourse.bass_utils import run_bass_kernel_spmd
from contextlib import ExitStack

F32 = mybir.dt.float32
BF16 = mybir.dt.bfloat16
AF = mybir.ActivationFunctionType
ALU = mybir.AluOpType

D = 1024
DFF = 2816
SEQ = 4096
NMETA = 16
NT = SEQ + NMETA
EPS = 1e-6
NSLOT = 4
TILES = [(SEQ, NMETA)] + [(i * 512, 512) for i in range(8)]
SUBS = [(SEQ, NMETA)] + [(i * 128, 128) for i in range(32)]


class Buf:
    __slots__ = ("name", "w", "r")

    def __init__(self, name):
        self.name = name
        self.w = None
        self.r = []


class DSem:
    __slots__ = ("sem", "cnt")

    def __init__(self, sem):
        self.sem = sem
        self.cnt = 0


class Sched:
    def __init__(self, nc, es):
        self.nc = nc
        self.es = es
        self.eng = {"pe": nc.tensor, "act": nc.scalar, "dve": nc.vector,
                    "pool": nc.gpsimd, "sp": nc.sync}
        self.esem = {k: es.enter_context(nc.semaphore("e_" + k)) for k in self.eng}
        self.ecnt = {k: 0 for k in self.eng}
        self.seen = {k: {} for k in self.eng}
        self.dsems = []
        self.n_ins = 0

    def dsem(self, name):
        d = DSem(self.es.enter_context(self.nc.semaphore("d_" + name)))
        self.dsems.append(d)
        return d

    def _wait(self, e, deps):
        best = {}
        for (sem, val) in deps:
            k = sem.num
            if k not in best or best[k][1] < val:
                best[k] = (sem, val)
        for k, (sem, val) in best.items():
            if e == "pe" and k == self.esem["pe"].num:
                continue
            if self.seen[e].get(k, 0) < val:
                self.eng[e].wait_ge(sem, val)
                self.seen[e][k] = val
                self.n_ins += 1

    @staticmethod
    def _deps(reads, writes):
        deps = []
        for b in reads:
            if b.w is not None:
                deps.append(b.w)
        for b in writes:
            if b.w is not None:
                deps.append(b.w)
            deps.extend(b.r)
        return deps

    @staticmethod
    def _compact(ticks):
        best = {}
        for (sem, val) in ticks:
            k = sem.num
            if k not in best or best[k][1] < val:
                best[k] = (sem, val)
        return list(best.values())

    def _mark(self, tick, reads, writes):
        for b in reads:
            b.r.append(tick)
            if len(b.r) > 32:
                b.r = self._compact(b.r)
        for b in writes:
            b.w = tick
            b.r = []

    def op(self, e, fn, reads=(), writes=(), inc=True):
        self._wait(e, self._deps(reads, writes))
        ins = fn(self.eng[e])
        self.n_ins += 1
        if inc:
            self.ecnt[e] += 1
            ins.then_inc(self.esem[e], 1)
            tick = (self.esem[e], self.ecnt[e])
        else:
            tick = (self.esem[e], self.ecnt[e] + 1)
        self._mark(tick, reads, writes)
        return ins

    def acquire(self, e, reads=(), writes=()):
        self._wait(e, self._deps(reads, writes))

    def dma(self, q, out, in_, ds, reads=(), writes=()):
        self._wait(q, self._deps(reads, writes))
        ins = self.eng[q].dma_start(out=out, in_=in_)
        ds.cnt += 16
        ins.then_inc(ds.sem, 16)
        self.n_ins += 1
        self._mark((ds.sem, ds.cnt), reads, writes)
        return ins

    def barrier(self):
        for e in self.eng:
            deps = [(self.esem[k], self.ecnt[k]) for k in self.eng if self.ecnt[k] > 0]
            deps += [(d.sem, d.cnt) for d in self.dsems if d.cnt > 0]
            for (sem, val) in deps:
                if self.seen[e].get(sem.num, 0) < val:
                    self.eng[e].wait_ge(sem, val)
                    self.seen[e][sem.num] = val
                    self.n_ins += 1


class TT:
    _cache = {}

    def __init__(self, S, es, name, shape, dt, nb=1, dsem=False, semname=None):
        self.t = es.enter_context(S.nc.sbuf_tensor(name, shape, dt))
        self.b = [Buf(f"{name}{i}") for i in range(nb)]
        semname = semname or name
        nsem = 0 if not dsem else (nb if dsem == "per" else 1)
        self.dsl = []
        for i in range(nsem):
            key = (id(S), f"{semname}{i}")
            if key not in TT._cache:
                TT._cache[key] = S.dsem(f"{semname}{i}")
            self.dsl.append(TT._cache[key])
        self.ds = self.dsl[0] if self.dsl else None
        self.ps = 1
        for s in shape[1:]:
            self.ps *= s


def weight_keys(n_seq):
    ks = []

    def ffn(l, f):
        for j in range(11):
            ks.append(("gu", l, f, j))
        for m in range(8):
            ks.append(("dn", l, f, m))

    for s in range(n_seq):
        for _ in TILES:
            ffn(0, 1)
            for j in range(6):
                ks.append(("m512", "gla_w_in", j))
        for _ in TILES:
            for j in range(2):
                ks.append(("m512", "gla_w_out", j))
            ffn(0, 2)
            ffn(1, 1)
            for j in range(3):
                ks.append(("m512", "attn_w_in", j))
        for _ in TILES[1:]:
            for j in range(2):
                ks.append(("m512", "attn_w_out", j))
            ffn(1, 2)
    return ks


def build(n_seq=2, stop_after="D", dbg=False):
    nc = bass.Bass("TRN2", target_bir_lowering=False)

    def din(name, shape):
        return nc.dram_tensor(name, list(shape), F32, kind="ExternalInput").ap()

    x = din("x", [n_seq, SEQ, D])
    meta_tokens = din("meta_tokens", [NMETA, D])
    norm_ffn1 = din("norm_ffn1", [2, D]); norm_mix = din("norm_mix", [2, D]); norm_ffn2 = din("norm_ffn2", [2, D])
    norm_final = din("norm_final", [D])
    Wg = {1: din("ffn1_w_gate", [2, D, DFF]), 2: din("ffn2_w_gate", [2, D, DFF])}
    Wu = {1: din("ffn1_w_up", [2, D, DFF]), 2: din("ffn2_w_up", [2, D, DFF])}
    Wd = {1: din("ffn1_w_down", [2, DFF, D]), 2: din("ffn2_w_down", [2, DFF, D])}
    Wm = {"gla_w_in": din("gla_w_in", [1, D, 3072])[0], "gla_w_out": din("gla_w_out", [1, D, D])[0],
          "attn_w_in": din("attn_w_in", [1, D, 1536])[0], "attn_w_out": din("attn_w_out", [1, D, D])[0]}
    gate_w1 = din("gla_gate_w1", [1, 2, D, 16])[0]
    gate_w2 = din("gla_gate_w2", [1, 2, 16, 512])[0]
    gate_b = din("gla_gate_b", [1, 2, 512])[0]
    head_norm = din("gla_head_norm", [1, 256])[0]
    q_norm = din("attn_q_norm", [1, 128])[0]
    k_norm = din("attn_k_norm", [1, 128])[0]
    c_ident = din("c_ident", [128, 128]); c_maskf = din("c_maskf", [128, 128]); c_maskb = din("c_maskb", [128, 128])
    c_scan = din("c_scan", [128, 512]); c_pmt = din("c_pmt", [128, 128])
    c_cos = din("c_cos", [128, NT]); c_sin = din("c_sin", [128, NT])

    out = nc.dram_tensor("out", [n_seq, SEQ, D], F32, kind="ExternalOutput").ap()
    WGU = {(l, f): nc.dram_tensor(f"wgu_{l}_{f}", [11, 128, 4096], BF16, kind="Internal").ap() for l in range(2) for f in (1, 2)}
    WDN = {(l, f): nc.dram_tensor(f"wdn_{l}_{f}", [8, 128, 2816], BF16, kind="Internal").ap() for l in range(2) for f in (1, 2)}
    WMB = {k: nc.dram_tensor(f"wmb_{k}", [v.shape[1] // 512, 128, 4096], BF16, kind="Internal").ap() for k, v in Wm.items()}
    wjob = {}
    skind = "ExternalOutput" if dbg else "Internal"

    def scr(name, shape, dt):
        return [nc.dram_tensor(f"{name}{s}", list(shape), dt, kind=skind).ap() for s in range(n_seq)]

    H1 = scr("H1_", [D, NT], F32); QD = scr("QD_", [2, 128, 33, 4, 128], BF16); KI = scr("KI_", [2, 128, 33, 4, 128], BF16)
    KE = scr("KE_", [2, NT, 512], BF16); VV = scr("VV_", [NT, 1024], BF16); SR = scr("SR_", [D, NT], F32)
    OO = scr("OO_", [2, 128, 33, 8, 128], F32)
    Q2 = scr("Q2_", [D, NT], BF16); H4 = scr("H4_", [D, NT], F32)

    with ExitStack() as es, nc.allow_non_contiguous_dma(reason="small param vectors"):
        S = Sched(nc, es)
        sb = lambda name, shape, dt, nb=1, dsem=False: TT(S, es, name, shape, dt, nb, dsem)

        ident_f = sb("ident_f", [128, 128], F32, dsem=True)
        ident_b = sb("ident_b", [128, 128], BF16)
        ones_b = sb("ones_b", [128, 128], BF16)
        maskf = sb("maskf", [128, 128], F32, dsem=True); maskb = sb("maskb", [128, 128], F32, dsem=True)
        scanm = sb("scanm", [128, 512], F32, dsem=True)
        pmt = sb("pmt", [128, 128], F32, dsem=True)
        neghalf = sb("neghalf", [128, 512], F32)
        ones_f = sb("ones_f", [128, 128], F32)
        gcol = sb("gcol", [128, 7, 8], F32, dsem=True)
        hncol = sb("hncol", [128, 2], F32, dsem=True)
        qkcol = sb("qkcol", [128, 2], F32, dsem=True)
        ngb = sb("ngb", [128, 2, 4], F32, dsem=True)
        w1 = sb("w1", [128, 2, 8, 16], BF16, dsem=True)
        w2 = sb("w2", [16, 2, 512], BF16, dsem=True)
        dec = sb("dec", [128, 2, 4, 65], F32)
        cst = [ident_f.b[0], ident_b.b[0], ones_b.b[0], maskf.b[0], maskb.b[0], scanm.b[0], pmt.b[0],
               neghalf.b[0], gcol.b[0], hncol.b[0], qkcol.b[0], ngb.b[0], w1.b[0], w2.b[0]]

        S.dma("sp", ident_f.t[:], c_ident, ident_f.ds, writes=ident_f.b)
        S.dma("sp", maskf.t[:], c_maskf, maskf.ds, writes=maskf.b)
        S.dma("sp", maskb.t[:], c_maskb, maskb.ds, writes=maskb.b)
        S.dma("sp", scanm.t[:], c_scan, scanm.ds, writes=scanm.b)
        S.dma("sp", pmt.t[:], c_pmt, pmt.ds, writes=pmt.b)
        gl = [norm_ffn1[0], norm_mix[0], norm_ffn2[0], norm_ffn1[1], norm_mix[1], norm_ffn2[1], norm_final]
        for i, g in enumerate(gl):
            S.dma("sp", gcol.t[:, i, :], g.rearrange("(c p) -> p c", p=128), gcol.ds, writes=gcol.b)
        S.dma("sp", hncol.t[:], head_norm.rearrange("(c p) -> p c", p=128), hncol.ds, writes=hncol.b)
        S.dma("sp", qkcol.t[:, 0:1], q_norm.rearrange("(c p) -> p c", p=128), qkcol.ds, writes=qkcol.b)
        S.dma("sp", qkcol.t[:, 1:2], k_norm.rearrange("(c p) -> p c", p=128), qkcol.ds, writes=qkcol.b)
        for n in range(2):
            S.dma("sp", ngb.t[:, n, :], gate_b[n].rearrange("(h p) -> p h", p=128), ngb.ds, writes=ngb.b)
            S.dma("pool", w1.t[:, n], gate_w1[n].rearrange("(kc p) r -> p kc r", p=128), w1.ds, writes=w1.b)
            S.dma("pool", w2.t[:, n, :], gate_w2[n], w2.ds, writes=w2.b)
        S.op("dve", lambda e: e.tensor_copy(ident_b.t[:], ident_f.t[:]), reads=ident_f.b, writes=ident_b.b)
        S.op("dve", lambda e: e.memset(ones_b.t[:], 1.0), writes=ones_b.b)
        S.op("dve", lambda e: e.memset(neghalf.t[:], -0.5), writes=neghalf.b)
        S.op("dve", lambda e: e.memset(ones_f.t[:], 1.0), writes=ones_f.b)
        S.op("dve", lambda e: e.tensor_scalar(out=ngb.t[:], in0=ngb.t[:], scalar1=-1.0, scalar2=None, op0=ALU.mult),
             reads=ngb.b, writes=ngb.b)
        S.op("dve", lambda e: e.tensor_scalar(out=qkcol.t[:, 0:1], in0=qkcol.t[:, 0:1], scalar1=128.0 ** -0.5,
                                              scalar2=None, op0=ALU.mult), reads=qkcol.b, writes=qkcol.b)

        class View:
            def __init__(self, t, b, ds):
                self.t, self.b, self.ds = t, b, ds

        def make_jobs(spec, gu_n, dn_n, m_n):
            jobs = []
            for it_ in spec:
                if it_[0] == "gu":
                    _, l, f = it_
                    for j0 in range(0, 11, gu_n):
                        nj = min(gu_n, 11 - j0)
                        jobs.append(dict(
                            keys=[("gu", l, f, j) for j in range(j0, j0 + nj)], rows=[(g, kc) for g in range(2) for kc in range(8)], ncols=nj * 256,
                            src=lambda r, l=l, f=f, j0=j0, nj=nj: (Wg[f][l] if r[0] == 0 else Wu[f][l])[r[1] * 128:(r[1] + 1) * 128, j0 * 256:(j0 + nj) * 256],
                            outv=lambda o, r, nj=nj: o.t[:, 0:nj * 4096].rearrange("p (j e) -> p j e", e=4096)[:, :, r[0] * 2048 + r[1] * 256:r[0] * 2048 + (r[1] + 1) * 256],
                            inv=lambda st, nj=nj: st.t[:, 0:nj * 256].rearrange("p (j c) -> p j c", c=256),
                            dst=WGU[(l, f)][j0:j0 + nj].rearrange("j p e -> p j e"), esz=(nj, 4096)))
                elif it_[0] == "dn":
                    _, l, f = it_
                    for m0 in range(0, 8, dn_n):
                        nm = min(dn_n, 8 - m0)
                        jobs.append(dict(
                            keys=[("dn", l, f, m) for m in range(m0, m0 + nm)], rows=list(range(22)), ncols=nm * 128,
                            src=lambda r, l=l, f=f, m0=m0, nm=nm: Wd[f][l][r * 128:(r + 1) * 128, m0 * 128:(m0 + nm) * 128],
                            outv=lambda o, r, nm=nm: o.t[:, 0:nm * 2816].rearrange("p (m e) -> p m e", e=2816)[:, :, r * 128:(r + 1) * 128],
                            inv=lambda st, nm=nm: st.t[:, 0:nm * 128].rearrange("p (m c) -> p m c", c=128),
                            dst=WDN[(l, f)][m0:m0 + nm].rearrange("m p e -> p m e"), esz=(nm, 2816)))
                else:
                    _, name = it_
                    npc_all = Wm[name].shape[1] // 512
                    for j0 in range(0, npc_all, m_n):
                        nj = min(m_n, npc_all - j0)
                        jobs.append(dict(
                            keys=[("m512", name, j) for j in range(j0, j0 + nj)], rows=list(range(8)), ncols=nj * 512,
                            src=lambda r, name=name, j0=j0, nj=nj: Wm[name][r * 128:(r + 1) * 128, j0 * 512:(j0 + nj) * 512],
                            outv=lambda o, r, nj=nj: o.t[:, 0:nj * 4096].rearrange("p (j e) -> p j e", e=4096)[:, :, r * 512:(r + 1) * 512],
                            inv=lambda st, nj=nj: st.t[:, 0:nj * 512].rearrange("p (j c) -> p j c", c=512),
                            dst=WMB[name][j0:j0 + nj].rearrange("j p e -> p j e"), esz=(nj, 4096)))
            return jobs

        def conv_gen(jobs, stages, outs, engs, store_q):
            items = [(ji, r) for ji, job in enumerate(jobs) for r in job["rows"]]
            ns = len(stages)
            ce = [0]

            def load(i):
                ji, r = items[i]
                st = stages[i % ns]
                S.dma("sp", st.t[:, 0:jobs[ji]["ncols"]], jobs[ji]["src"](r), st.ds, writes=st.b)

            for i in range(min(ns, len(items))):
                load(i)
            for i, (ji, r) in enumerate(items):
                job = jobs[ji]
                st = stages[i % ns]
                o = outs[ji % len(outs)]
                e = engs[ce[0] % len(engs)]
                ce[0] += 1
                if e == "act":
                    S.op("act", lambda en: en.copy(job["outv"](o, r), job["inv"](st)), reads=st.b, writes=o.b)
                else:
                    S.op(e, lambda en: en.tensor_copy(job["outv"](o, r), job["inv"](st)), reads=st.b, writes=o.b)
                if i + ns < len(items):
                    load(i + ns)
                if r == job["rows"][-1]:
                    jb = Buf("job")
                    nj, e_ = job["esz"]
                    S.dma(store_q, job["dst"], o.t[:, 0:nj * e_].rearrange("p (j e) -> p j e", e=e_), o.ds, reads=o.b, writes=[jb])
                    for k in job["keys"]:
                        wjob[k] = jb
                yield

        with ExitStack() as esP:
            stg = [TT(S, esP, f"cstg{i}", [128, 3072], F32, 1, True) for i in range(4)]
            ost_p = [TT(S, esP, f"cost{i}", [128, 24576], BF16, 1, True) for i in range(2)]
            for _ in conv_gen(make_jobs([("gu", 0, 1), ("dn", 0, 1), ("m", "gla_w_in"), ("m", "gla_w_out"), ("gu", 0, 2), ("dn", 0, 2),
                                         ("gu", 1, 1), ("dn", 1, 1), ("m", "attn_w_in"), ("m", "attn_w_out"), ("gu", 1, 2), ("dn", 1, 2)], 6, 8, 6),
                              stg, ost_p, ("act", "dve", "pool"), "pool"):
                pass
            S.barrier()
        deferred = []

        psb = [es.enter_context(nc.psum_tensor(f"ps{i}", [128, 512], F32)) for i in range(8)]
        psB = [Buf(f"ps{i}") for i in range(8)]
        pst = [0, 8]

        def psum():
            i = pst[0] % pst[1]
            pst[0] += 1
            return psb[i], psB[i]

        def psum_peek(n):
            return [psB[(pst[0] + k) % pst[1]] for k in range(min(n, pst[1]))]

        slots = [sb(f"wslot{i}", [128, 4096], BF16, dsem=True) for i in range(NSLOT)]
        wkeys = weight_keys(n_seq)
        wst = {"issued": 0, "next": 0}

        def issue_piece(i):
            key = wkeys[i]
            sl = slots[i % NSLOT]
            if key[0] == "gu":
                _, l, f, j = key
                S.dma("sp", sl.t[:, :], WGU[(l, f)][j], sl.ds, reads=[wjob[key]], writes=sl.b)
            elif key[0] == "dn":
                _, l, f, m = key
                S.dma("sp", sl.t[:, 0:2816], WDN[(l, f)][m], sl.ds, reads=[wjob[key]], writes=sl.b)
            else:
                _, name, j = key
                S.dma("sp", sl.t[:, :], WMB[name][j], sl.ds, reads=[wjob[key]], writes=sl.b)

        def wget(key):
            i = wst["next"]
            assert wkeys[i] == key, (i, wkeys[i], key)
            wst["next"] += 1
            while wst["issued"] < min(len(wkeys), i + NSLOT):
                if wkeys[wst["issued"]] not in wjob:
                    assert wst["issued"] > i
                    break
                issue_piece(wst["issued"])
                wst["issued"] += 1
            return slots[i % NSLOT]

        hT = sb("hT", [128, 8, 512], F32, nb=8)
        hn = sb("hn", [128, 8, 512], BF16, nb=8)
        act = sb("act", [128, 22, 512], BF16, nb=22)
        xin = sb("xin", [128, 4, 1024], F32, nb=8, dsem=True)
        ms = sb("ms", [128, 512], F32); rstd = sb("rstd", [128, 512], F32)
        sg = [sb(f"sg{i}", [128, 512], F32) for i in range(2)]
        hst_ds = S.dsem("hTst")
        rr = {"sg": 0, "cp": 0}

        def evac(out_ap, in_ap, reads, writes):
            rr["cp"] += 1
            if rr["cp"] % 2:
                S.op("act", lambda e: e.copy(out_ap, in_ap), reads=reads, writes=writes)
            else:
                S.op("dve", lambda e: e.tensor_copy(out_ap, in_ap), reads=reads, writes=writes)

        def rmsnorm(T, gi):
            for c in range(8):
                S.op("act", lambda e: e.activation(out=act.t[:, c, :T], in_=hT.t[:, c, :T], func=AF.Square),
                     reads=[hT.b[c]], writes=[act.b[c]])
            ps, pb = psum()
            for c in range(8):
                S.op("pe", lambda e: e.matmul(ps[:, :T], ones_b.t[:], act.t[:, c, :T], start=(c == 0), stop=(c == 7)),
                     reads=[ones_b.b[0], act.b[c]], writes=[pb], inc=(c == 7))
            S.op("act", lambda e: e.activation(out=ms.t[:, :T], in_=ps[:, :T], func=AF.Ln, scale=1.0 / D, bias=EPS), reads=[pb], writes=ms.b)
            S.op("act", lambda e: e.activation(out=rstd.t[:, :T], in_=ms.t[:, :T], func=AF.Exp, scale=-0.5), reads=ms.b, writes=rstd.b)
            for c in range(8):
                S.op("dve", lambda e: e.scalar_tensor_tensor(out=hn.t[:, c, :T], in0=hT.t[:, c, :T],
                                                             scalar=gcol.t[:, gi, c:c + 1], in1=rstd.t[:, :T],
                                                             op0=ALU.mult, op1=ALU.mult),
                     reads=[hT.b[c], gcol.b[0], rstd.b[0]], writes=[hn.b[c]])

        def ffn(T, l, f):
            rmsnorm(T, 3 * l + (0 if f == 1 else 2))
            for j in range(11):
                sl = wget(("gu", l, f, j))
                v = sl.t[:].rearrange("p (g k c) -> p g k c", g=2, k=8)
                S.acquire("pe", reads=sl.b, writes=psum_peek(4))
                for half in range(2):
                    fc = 2 * j + half
                    pg, pgb = psum()
                    pu, pub = psum()
                    for g, (pp, ppb) in enumerate(((pg, pgb), (pu, pub))):
                        for kc in range(8):
                            S.op("pe", lambda e: e.matmul(pp[:, :T], v[:, g, kc, half * 128:(half + 1) * 128],
                                                          hn.t[:, kc, :T], start=(kc == 0), stop=(kc == 7)),
                                 reads=[sl.b[0], hn.b[kc]], writes=[ppb], inc=(kc == 7))
                    s_ = sg[rr["sg"] % 2]
                    rr["sg"] += 1
                    S.op("act", lambda e: e.activation(out=s_.t[:, :T], in_=pg[:, :T], func=AF.Silu),
                         reads=[pgb], writes=s_.b)
                    S.op("dve", lambda e: e.tensor_tensor(out=act.t[:, fc, :T], in0=s_.t[:, :T], in1=pu[:, :T], op=ALU.mult),
                         reads=[s_.b[0], pub], writes=[act.b[fc]])
            for m in range(8):
                sl = wget(("dn", l, f, m))
                v = sl.t[:, 0:22 * 128].rearrange("p (k c) -> p k c", k=22)
                py, pyb = psum()
                for fc in range(22):
                    S.op("pe", lambda e: e.matmul(py[:, :T], v[:, fc, :], act.t[:, fc, :T], start=(fc == 0), stop=(fc == 21)),
                         reads=[sl.b[0], act.b[fc]], writes=[pyb], inc=(fc == 21))
                S.op("dve", lambda e: e.scalar_tensor_tensor(out=hT.t[:, m, :T], in0=py[:, :T], scalar=0.5,
                                                             in1=hT.t[:, m, :T], op0=ALU.mult, op1=ALU.add),
                     reads=[pyb, hT.b[m]], writes=[hT.b[m]])

        def proj_fm(T, sl, ncol, col0, consume):
            v = sl.t[:].rearrange("p (k c) -> p k c", k=8)
            S.acquire("pe", reads=sl.b, writes=psum_peek(ncol))
            for i in range(ncol):
                ps, pb = psum()
                for kc in range(8):
                    S.op("pe", lambda e: e.matmul(ps[:, :T], v[:, kc, col0 + i * 128:col0 + (i + 1) * 128], hn.t[:, kc, :T],
                                                  start=(kc == 0), stop=(kc == 7)),
                         reads=[sl.b[0], hn.b[kc]], writes=[pb], inc=(kc == 7))
                consume(i, ps, pb)

        def store_hT(T, t0, dst):
            S.dma("pool", dst.rearrange("(c p) t -> p c t", p=128)[:, :, t0:t0 + T], hT.t[:, :, :T], hst_ds, reads=hT.b)

        def load_hT(T, t0, src):
            S.dma("sp", hT.t[:, :, :T], src.rearrange("(c p) t -> p c t", p=128)[:, :, t0:t0 + T], hst_ds, writes=hT.b)

        def subs_of(T):
            return [(i * 128, 128) for i in range(T // 128)] if T >= 128 else [(0, T)]

        for s in range(n_seq):
            with ExitStack() as esA:
                sbA = lambda name, shape, dt, nb=1, dsem=False: TT(S, esA, f"{name}_{s}", shape, dt, nb, dsem, semname=name)
                qf = sbA("qf", [128, 4, 512], F32, nb=4); kf = sbA("kf", [128, 4, 512], F32, nb=4)
                ub = sbA("ub", [16, 2, 512], BF16, nb=2)
                tA = sbA("tA", [128, 512], F32); nl = sbA("nl", [128, 2, 512], F32, nb=2)
                Pc = sbA("Pc", [128, 2, 512], F32, nb=2); Ea = sbA("Ea", [128, 512], F32); Eb = sbA("Eb", [128, 512], F32)
                Pe = sbA("Pe", [128, 512], F32); t1 = sbA("t1", [128, 512], F32); t2 = sbA("t2", [128, 512], F32)
                Gs = sbA("Gs", [128, 2, 8], F32, nb=2)
                qdst = sbA("qdst", [128, 2, 4, 4, 128], BF16, nb=1, dsem=True)
                kist = sbA("kist", [128, 2, 4, 4, 128], BF16, nb=1, dsem=True)
                keT = sbA("keT", [128, 2, 2, 512], BF16, nb=2)
                ketok = sbA("ketok", [128, 2, 2, 4, 128], BF16, nb=2, dsem="per")
                vtok = sbA("vtok", [128, 2, 4, 512], BF16, nb=2, dsem="per")
                srst = sbA("srst", [128, 2, 4, 512], F32, nb=2, dsem="per")
                for ti, (t0, T) in enumerate(TILES):
                    subs = subs_of(T)
                    nch = max(1, T // 64)
                    clen = min(64, T)
                    ci0 = t0 // 64
                    def load_x(ti_):
                        t0_, T_ = TILES[ti_]
                        if T_ == NMETA:
                            S.dma("sp", xin.t[0:T_, 0, :], meta_tokens, xin.ds, writes=xin.b)
                        else:
                            S.dma("sp", xin.t[:, :, :], x[s, t0_:t0_ + T_, :].rearrange("(u p) d -> p u d", p=128), xin.ds,
                                  writes=xin.b)
                    if ti == 0:
                        load_x(0)
                    for c in range(8):
                        ps, pb = psum()
                        for u, (o, n) in enumerate(subs):
                            S.op("pe", lambda e: e.transpose(ps[:, o:o + n], xin.t[0:n, u, c * 128:(c + 1) * 128],
                                                             ident_f.t[0:n, 0:n]),
                                 reads=[xin.b[0], ident_f.b[0]], writes=[pb], inc=(u == len(subs) - 1))
                        evac(hT.t[:, c, :T], ps[:, :T], [pb], [hT.b[c]])
                    if ti + 1 < len(TILES):
                        load_x(ti + 1)
                    ffn(T, 0, 1)
                    store_hT(T, t0, H1[s])
                    rmsnorm(T, 1)
                    for n in range(2):
                        ps, pb = psum()
                        for kc in range(8):
                            S.op("pe", lambda e: e.matmul(ps[0:16, :T], w1.t[:, n, kc, :], hn.t[:, kc, :T],
                                                          start=(kc == 0), stop=(kc == 7)),
                                 reads=[w1.b[0], hn.b[kc]], writes=[pb], inc=(kc == 7))
                        evac(ub.t[0:16, n, :T], ps[0:16, :T], [pb], [ub.b[n]])
                    sl = wget(("m512", "gla_w_in", 0))
                    proj_fm(T, sl, 4, 0, lambda i, ps, pb: S.op(
                        "act", lambda e: e.mul(qf.t[:, i, :T], ps[:, :T], 128.0 ** -0.5), reads=[pb], writes=[qf.b[i]]))
                    sl = wget(("m512", "gla_w_in", 1))
                    proj_fm(T, sl, 4, 0, lambda i, ps, pb: evac(kf.t[:, i, :T], ps[:, :T], [pb], [kf.b[i]]))
                    def proj_piece(k):
                        if k < 2:
                            vp = k
                            sl = wget(("m512", "gla_w_in", 2 + vp))
                            v = sl.t[:].rearrange("p (k c) -> p k c", k=8)
                            for u, (o, n) in enumerate(subs):
                                ps, pb = psum()
                                for kc in range(8):
                                    S.op("pe", lambda e: e.matmul(ps[0:n, :], hn.t[:, kc, o:o + n], v[:, kc, :],
                                                                  start=(kc == 0), stop=(kc == 7)),
                                         reads=[sl.b[0], hn.b[kc]], writes=[pb], inc=(kc == 7))
                                evac(vtok.t[0:n, vp, u, :], ps[0:n, :], [pb], [vtok.b[vp]])
                            dst = VV[s][t0:t0 + T, vp * 512:(vp + 1) * 512]
                            if T >= 128:
                                S.dma("pool", dst.rearrange("(u p) c -> p u c", p=128), vtok.t[:, vp, :, :], vtok.dsl[vp], reads=[vtok.b[vp]])
                            else:
                                S.dma("pool", dst, vtok.t[0:T, vp, 0, :], vtok.dsl[vp], reads=[vtok.b[vp]])

                        else:
                            rp = k - 2
                            sl = wget(("m512", "gla_w_in", 4 + rp))
                            proj_fm(T, sl, 4, 0, lambda i, ps, pb: S.op(
                                "act", lambda e: e.activation(out=srst.t[:, rp, i, :T], in_=ps[:, :T], func=AF.Silu),
                                reads=[pb], writes=[srst.b[rp]]))
                            S.dma("pool", SR[s].rearrange("(c p) t -> p c t", p=128)[:, rp * 4:(rp + 1) * 4, t0:t0 + T],
                                  srst.t[:, rp, :, :T], srst.dsl[rp], reads=[srst.b[rp]])

                    def gate_math(h):
                        par = h % 2

                        def bc(tt, off):
                            return bass.AP(tt.t, off + clen - 1, [[tt.ps, 128], [clen, nch], [0, clen]])

                        def ch(tt, off):
                            return bass.AP(tt.t, off + clen - 1, [[tt.ps, 128], [clen, nch]])

                        def v3(ap):
                            return ap.rearrange("p (c j) -> p c j", j=clen)

                        wid = min(128, T)
                        nsb = len(subs)

                        def d4(tt, d_):
                            return tt.t[:, d_, 0:nsb, h, 0:wid]

                        def u3(ap):
                            return ap.rearrange("p (u j) -> p u j", j=wid)

                        for n in range(2):
                            ps, pb = psum()
                            S.op("pe", lambda e: e.matmul(ps[:, :T], w2.t[0:16, n, h * 128:(h + 1) * 128], ub.t[0:16, n, :T],
                                                          start=True, stop=True), reads=[w2.b[0], ub.b[n]], writes=[pb])
                            S.op("act", lambda e: e.activation(out=tA.t[:, :T], in_=ps[:, :T], func=AF.Exp, scale=-1.0,
                                                               bias=ngb.t[:, n, h:h + 1]),
                                 reads=[pb, ngb.b[0]], writes=tA.b)
                            S.op("act", lambda e: e.activation(out=nl.t[:, n, :T], in_=tA.t[:, :T], func=AF.Ln, bias=1.0),
                                 reads=tA.b, writes=[nl.b[n]])
                            S.op("dve", lambda e: e.tensor_tensor_scan(out=Pc.t[:, n, :T], data0=scanm.t[:, :T],
                                                                       data1=nl.t[:, n, :T], initial=0.0,
                                                                       op0=ALU.mult, op1=ALU.add),
                                 reads=[scanm.b[0], nl.b[n]], writes=[Pc.b[n]])
                        S.op("act", lambda e: e.activation(out=Ea.t[:, :T], in_=Pc.t[:, 0, :T], func=AF.Exp, scale=-1.0 / 16),
                             reads=[Pc.b[0]], writes=Ea.b)
                        S.op("act", lambda e: e.activation(out=Eb.t[:, :T], in_=Pc.t[:, 0, :T], func=AF.Exp, scale=1.0 / 16),
                             reads=[Pc.b[0]], writes=Eb.b)
                        S.op("dve", lambda e: e.tensor_tensor(out=d4(qdst, 0), in0=u3(qf.t[:, h, :T]), in1=u3(Ea.t[:, :T]), op=ALU.mult),
                             reads=[qf.b[h], Ea.b[0]], writes=qdst.b)
                        S.op("dve", lambda e: e.tensor_tensor(out=t1.t[:, :T], in0=kf.t[:, h, :T], in1=Eb.t[:, :T], op=ALU.mult),
                             reads=[kf.b[h], Eb.b[0]], writes=t1.b)
                        S.op("act", lambda e: e.copy(d4(kist, 0), u3(t1.t[:, :T])), reads=t1.b, writes=kist.b)
                        S.op("pool", lambda e: e.tensor_tensor(out=v3(keT.t[:, par, 0, :T]), in0=v3(t1.t[:, :T]), in1=bc(Ea, 0), op=ALU.mult),
                             reads=[t1.b[0], Ea.b[0]], writes=[keT.b[par]])
                        S.op("act", lambda e: e.copy(dec.t[:, 0, h, ci0:ci0 + nch], ch(Ea, 0)), reads=Ea.b, writes=dec.b)
                        S.op("dve", lambda e: e.tensor_tensor(out=Pe.t[:, :T], in0=Pc.t[:, 1, :T], in1=nl.t[:, 1, :T], op=ALU.subtract),
                             reads=[Pc.b[1], nl.b[1]], writes=Pe.b)
                        S.op("act", lambda e: e.activation(out=Ea.t[:, :T], in_=Pe.t[:, :T], func=AF.Exp, scale=-1.0 / 16),
                             reads=Pe.b, writes=Ea.b)
                        S.op("act", lambda e: e.activation(out=Eb.t[:, :T], in_=Pe.t[:, :T], func=AF.Exp, scale=1.0 / 16),
                             reads=Pe.b, writes=Eb.b)
                        S.op("act", lambda e: e.activation(out=Gs.t[:, 0, 0:nch], in_=ch(Pc, 512), func=AF.Exp, scale=-1.0 / 16),
                             reads=[Pc.b[1]], writes=[Gs.b[0]])
                        S.op("act", lambda e: e.activation(out=Gs.t[:, 1, 0:nch], in_=ch(Pc, 512), func=AF.Exp, scale=1.0 / 16),
                             reads=[Pc.b[1]], writes=[Gs.b[1]])
                        g1b = bass.AP(Gs.t, 0, [[16, 128], [1, nch], [0, clen]])
                        g2b = bass.AP(Gs.t, 8, [[16, 128], [1, nch], [0, clen]])
                        S.op("dve", lambda e: e.tensor_tensor(out=t1.t[:, :T], in0=qf.t[:, h, :T], in1=Eb.t[:, :T], op=ALU.mult),
                             reads=[qf.b[h], Eb.b[0]], writes=t1.b)
                        S.op("pool", lambda e: e.tensor_tensor(out=d4(qdst, 1), in0=v3(t1.t[:, :T]), in1=g1b, op=ALU.mult),
                             reads=[t1.b[0], Gs.b[0]], writes=qdst.b)
                        S.op("dve", lambda e: e.tensor_tensor(out=t2.t[:, :T], in0=kf.t[:, h, :T], in1=Ea.t[:, :T], op=ALU.mult),
                             reads=[kf.b[h], Ea.b[0]], writes=t2.b)
                        S.op("act", lambda e: e.copy(keT.t[:, par, 1, :T], t2.t[:, :T]), reads=t2.b, writes=[keT.b[par]])
                        S.op("pool", lambda e: e.tensor_tensor(out=d4(kist, 1), in0=v3(t2.t[:, :T]), in1=g2b, op=ALU.mult),
                             reads=[t2.b[0], Gs.b[1]], writes=kist.b)
                        S.op("act", lambda e: e.copy(dec.t[:, 1, h, ci0:ci0 + nch], Gs.t[:, 0, 0:nch]), reads=[Gs.b[0]], writes=dec.b)
                    def ke_transposes(h):
                        par = h % 2
                        for d_ in range(2):
                            ps, pb = psum()
                            pbf = ps[:].bitcast(BF16)
                            for u, (o, n) in enumerate(subs):
                                S.op("pe", lambda e: e.transpose(pbf[0:n, u * 128:(u + 1) * 128], keT.t[:, par, d_, o:o + n], ident_b.t[:, :]),
                                     reads=[keT.b[par], ident_b.b[0]], writes=[pb], inc=(u == len(subs) - 1))
                            nu = len(subs)
                            n0 = subs[0][1]
                            evac(ketok.t[0:n0, par, d_, 0:nu, :], pbf[0:n0, 0:nu * 128].rearrange("p (u c) -> p u c", c=128),
                                 [pb], [ketok.b[par]])
                            dst = KE[s][d_, t0:t0 + T, h * 128:(h + 1) * 128]
                            if T >= 128:
                                S.dma("pool", dst.rearrange("(u p) c -> p u c", p=128), ketok.t[:, par, d_, :, :], ketok.dsl[par], reads=[ketok.b[par]])
                            else:
                                S.dma("pool", dst, ketok.t[0:T, par, d_, 0, :], ketok.dsl[par], reads=[ketok.b[par]])
                    for h in range(4):
                        gate_math(h)
                        proj_piece(h)
                        ke_transposes(h)
                    for d_ in range(2):
                        for tt_, dst_ in ((qdst, QD), (kist, KI)):
                            if T >= 128:
                                S.dma("pool", dst_[s][d_, :, t0 // 128:t0 // 128 + 4, :, :], tt_.t[:, d_, :, :, :], tt_.ds, reads=tt_.b)
                            else:
                                S.dma("pool", dst_[s][d_, :, 32, :, 0:T], tt_.t[:, d_, 0, :, 0:T], tt_.ds, reads=tt_.b)
                S.barrier()
            if stop_after == "A":
                break

            with ExitStack() as esG:
                sbG = lambda name, shape, dt, nb=1, dsem=False: TT(S, esG, f"{name}_{s}", shape, dt, nb, dsem, semname=name)
                Sst = sbG("Sst", [128, 2, 4, 256], F32, nb=8)
                Sbf = sbG("Sbf", [128, 3, 4, 2, 256], BF16, nb=24)
                qdl = [sbG(f"qdl{i}", [128, 4, 128], BF16, dsem=True) for i in range(2)]
                kil = [sbG(f"kil{i}", [128, 4, 128], BF16, dsem=True) for i in range(2)]
                kel = [sbG(f"kel{i}", [128, 512], BF16, dsem=True) for i in range(2)]
                keh = [sbG(f"keh{i}", [128, 512], BF16, dsem=True) for i in range(2)]
                vl = [sbG(f"vl{i}", [128, 1024], BF16, dsem=True) for i in range(2)]
                Am = [sbG(f"Am{i}", [128, 4, 128], BF16, nb=4) for i in range(2)]
                ost = [sbG(f"ost{i}", [128, 8, 128], F32, dsem=True) for i in range(2)]
                ofl = [sbG(f"ofl{i}", [128, 8, 128], F32, dsem=True) for i in range(2)]
                cgen = None
                if s == 0 and deferred:
                    ost_g = [sbG(f"costg{i}", [128, 8448], BF16, dsem=True) for i in range(2)]
                    fx = xin.t[:].rearrange("p u d -> p (u d)")
                    fh = hT.t[:].rearrange("p c t -> p (c t)")
                    stg_g = [View(f_[:, i * 1024:(i + 1) * 1024], [Buf(f"cs{k}{i}")], S.dsem(f"cs{k}{i}"))
                             for k, f_ in enumerate((fx, fh)) for i in range(4)]
                    cgen = conv_gen(deferred, stg_g, ost_g, ("pool", "act"), "pool")

                def conv_step(k):
                    nonlocal_c = cgen
                    if nonlocal_c is None:
                        return
                    for _ in range(k):
                        if next(nonlocal_c, "done") == "done":
                            break
                for i in range(2):
                    S.op("dve", lambda e: e.memset(kel[i].t[:], 0.0), writes=kel[i].b)
                    S.op("dve", lambda e: e.memset(keh[i].t[:], 0.0), writes=keh[i].b)
                for d_ in range(2):
                    order = SUBS if d_ == 0 else SUBS[::-1]
                    mk = maskf if d_ == 0 else maskb
                    S.op("dve", lambda e: e.memset(Sst.t[:], 0.0), writes=Sst.b)
                    scur = [0, 0, 0, 0]
                    for h in range(4):
                        S.op("dve", lambda e: e.memset(Sbf.t[:, 0, h, 0, :], 0.0), writes=[Sbf.b[h * 2]])
                    def g_loads(it):
                        t0, n = order[it]
                        par = it % 2
                        S.dma("sp", qdl[par].t[:, :, 0:n], QD[s][d_, :, t0 // 128, :, 0:n], qdl[par].ds, writes=qdl[par].b)
                        S.dma("sp", kil[par].t[:, :, 0:n], KI[s][d_, :, t0 // 128, :, 0:n], kil[par].ds, writes=kil[par].b)
                        if n == 128:
                            S.dma("sp", kel[par].t[0:64, :], KE[s][d_, t0:t0 + 64, :], kel[par].ds, writes=kel[par].b)
                            S.dma("sp", keh[par].t[64:128, :], KE[s][d_, t0 + 64:t0 + 128, :], keh[par].ds, writes=keh[par].b)
                        else:
                            S.dma("sp", kel[par].t[0:n, :], KE[s][d_, t0:t0 + n, :], kel[par].ds, writes=kel[par].b)
                        S.dma("sp", vl[par].t[0:n, :], VV[s][t0:t0 + n, :], vl[par].ds, writes=vl[par].b)
                        if d_ == 1:
                            S.dma("sp", ofl[par].t[:, :, 0:n], OO[s][0, :, t0 // 128, :, 0:n], ofl[par].ds, writes=ofl[par].b)

                    gst = {}

                    def g_ctx(it):
                        t0, n = order[it]
                        par = it % 2
                        sbi = lambda p_, h_, k_: p_ * 8 + h_ * 2 + k_
                        p3 = it % 3
                        if n == 128:
                            chunks = [(0, 64, kel[par], t0 // 64), (64, 64, keh[par], t0 // 64 + 1)]
                            if d_ == 1:
                                chunks = chunks[::-1]
                            Kc = 128
                        else:
                            chunks = [(0, n, kel[par], 64)]
                            Kc = n
                        return t0, n, par, p3, sbi, chunks, Kc

                    def g_stage1(it):
                        t0, n, par, p3, sbi, chunks, Kc = g_ctx(it)
                        pkvs, pas = [], []
                        for h in range(4):
                            pkv, pkvb = psum()
                            for k, (co, cl, ket, ci) in enumerate(chunks):
                                S.op("pe", lambda e: e.matmul(pkv[:, k * 256:(k + 1) * 256], ket.t[0:Kc, h * 128:(h + 1) * 128],
                                                              vl[par].t[0:Kc, h * 256:(h + 1) * 256], start=True, stop=True),
                                     reads=[ket.b[0], vl[par].b[0]], writes=[pkvb])
                            pkvs.append((pkv, pkvb))
                        pa, pab = psum()
                        for h in range(4):
                            S.op("pe", lambda e: e.matmul(pa[0:n, h * 128:h * 128 + n], kil[par].t[:, h, 0:n], qdl[par].t[:, h, 0:n], start=True, stop=True),
                                 reads=[kil[par].b[0], qdl[par].b[0]], writes=[pab], inc=(h == 3))
                        for k, (co, cl, ket, ci) in enumerate(chunks):
                            for h in range(4):
                                pkv, pkvb = pkvs[h]
                                a_, b_ = scur[h], 1 - scur[h]
                                scur[h] = b_
                                S.op("dve", lambda e: e.scalar_tensor_tensor(out=Sst.t[:, b_, h, :], in0=Sst.t[:, a_, h, :],
                                                                             scalar=dec.t[:, d_, h, ci:ci + 1],
                                                                             in1=pkv[:, k * 256:(k + 1) * 256],
                                                                             op0=ALU.mult, op1=ALU.add),
                                     reads=[Sst.b[a_ * 4 + h], dec.b[0], pkvb], writes=[Sst.b[b_ * 4 + h]])
                                if k < len(chunks) - 1:
                                    tp, tk = p3, 1
                                else:
                                    tp, tk = (p3 + 1) % 3, 0
                                S.op("act", lambda e: e.copy(Sbf.t[:, tp, h, tk, :], Sst.t[:, b_, h, :]),
                                     reads=[Sst.b[b_ * 4 + h]], writes=[Sbf.b[sbi(tp, h, tk)]])
                            if k == 0:
                                for h in range(4):
                                    S.op("dve", lambda e: e.tensor_tensor(out=Am[par].t[0:n, h, 0:n], in0=pa[0:n, h * 128:h * 128 + n], in1=mk.t[0:n, 0:n], op=ALU.mult),
                                         reads=[pab, mk.b[0]], writes=[Am[par].b[h]])

                    def g_stage2(it):
                        t0, n, par, p3, sbi, chunks, Kc = g_ctx(it)
                        for hb in range(2):
                            po, pob = psum()
                            for h in (2 * hb, 2 * hb + 1):
                                for dvc in range(2):
                                    c0 = ((h % 2) * 2 + dvc) * 128
                                    S.op("pe", lambda e: e.matmul(po[:, c0:c0 + n], vl[par].t[0:n, h * 256 + dvc * 128:h * 256 + (dvc + 1) * 128],
                                                                  Am[par].t[0:n, h, 0:n], start=True, stop=False),
                                         reads=[vl[par].b[0], Am[par].b[h]], writes=[pob], inc=False)
                                    for k, (co, cl, ket, ci) in enumerate(chunks):
                                        last = (k == len(chunks) - 1)
                                        S.op("pe", lambda e: e.matmul(po[:, c0 + co:c0 + co + cl], Sbf.t[:, p3, h, k, dvc * 128:(dvc + 1) * 128],
                                                                      qdl[par].t[:, h, co:co + cl], start=False, stop=last),
                                             reads=[Sbf.b[sbi(p3, h, k)], qdl[par].b[0]], writes=[pob], inc=last)
                            for h in (2 * hb, 2 * hb + 1):
                                for dvc in range(2):
                                    c0 = ((h % 2) * 2 + dvc) * 128
                                    if d_ == 0:
                                        evac(ost[par].t[:, h * 2 + dvc, 0:n], po[:, c0:c0 + n], [pob], ost[par].b)
                                    else:
                                        S.op("dve", lambda e: e.tensor_tensor(out=ost[par].t[:, h * 2 + dvc, 0:n], in0=po[:, c0:c0 + n],
                                                                              in1=ofl[par].t[:, h * 2 + dvc, 0:n], op=ALU.add),
                                             reads=[pob, ofl[par].b[0]], writes=ost[par].b)
                        S.dma("pool", OO[s][d_, :, t0 // 128, :, 0:n], ost[par].t[:, :, 0:n], ost[par].ds, reads=ost[par].b)
                        if it + 2 < len(order):
                            g_loads(it + 2)

                    g_loads(0)
                    if len(order) > 1:
                        g_loads(1)
                    g_stage1(0)
                    for it in range(len(order)):
                        if it + 1 < len(order):
                            g_stage1(it + 1)
                        g_stage2(it)
                        conv_step(7)
                    if d_ == 1:
                        conv_step(10 ** 6)
                    S.barrier()
                S.barrier()
            if stop_after == "G":
                break

            esKV = ExitStack()
            KT = TT(S, esKV, f"KT_{s}", [128, 2, NT], BF16, 2)
            VR = TT(S, esKV, f"VR_{s}", [128, 33, 256], BF16, 1)
            with ExitStack() as esB:
                sbB = lambda name, shape, dt, nb=1, dsem=False: TT(S, esB, f"{name}_{s}", shape, dt, nb, dsem, semname=name)
                srt = sbB("srt", [128, 8, 512], F32, nb=8, dsem=True)
                oft = xin
                qkt = [dict(xs=sbB(f"xs{i}", [128, 512], F32), xn=sbB(f"xn{i}", [128, 512], F32), r1=sbB(f"r1{i}", [128, 512], F32),
                            r2=sbB(f"r2{i}", [128, 512], F32), sqh=sbB(f"sqh{i}", [128, 512], BF16), ms=sbB(f"msq{i}", [128, 512], F32),
                            rstd=sbB(f"rsq{i}", [128, 512], F32)) for i in range(2)]
                qki = [0]
                Ct = sbB("Ct", [128, 512], F32, dsem=True); St = sbB("St", [128, 512], F32, dsem=True)
                q2st = sbB("q2st", [128, 8, 512], BF16, nb=8, dsem=True)
                def load_os(ti_):
                    t0_, T_ = TILES[ti_]
                    if T_ >= 128:
                        S.dma("sp", xin.t[:, :, :], OO[s][1, :, t0_ // 128:t0_ // 128 + 4, :, :].rearrange("p u c j -> p u (c j)"), xin.ds, writes=xin.b)
                    else:
                        S.dma("sp", xin.t[:, 0, :].rearrange("p (c j) -> p c j", j=128)[:, :, 0:T_], OO[s][1, :, 32, :, 0:T_], xin.ds, writes=xin.b)
                    S.dma("sp", srt.t[:, :, :T_], SR[s].rearrange("(c p) t -> p c t", p=128)[:, :, t0_:t0_ + T_], srt.ds, writes=srt.b)

                for ti, (t0, T) in enumerate(TILES):
                    subs = subs_of(T)
                    if ti == 0:
                        load_os(0)
                    load_hT(T, t0, H1[s])
                    S.dma("sp", Ct.t[:, :T], c_cos[:, t0:t0 + T], Ct.ds, writes=Ct.b)
                    S.dma("sp", St.t[:, :T], c_sin[:, t0:t0 + T], St.ds, writes=St.b)
                    wid = min(128, T)
                    nsb = len(subs)
                    ofc = lambda c: xin.t[:, 0:nsb, c * 128:c * 128 + wid]
                    u3 = lambda ap: ap.rearrange("p (u j) -> p u j", j=wid)
                    for h in range(4):
                        ps, pb = psum()
                        for dvc in range(2):
                            c = 2 * h + dvc
                            S.op("act", lambda e: e.activation(out=u3(act.t[:, c, :T]), in_=ofc(c), func=AF.Square),
                                 reads=[xin.b[c]], writes=[act.b[c]])
                            S.op("pe", lambda e: e.matmul(ps[:, :T], ones_b.t[:], act.t[:, c, :T], start=(dvc == 0), stop=(dvc == 1)),
                                 reads=[ones_b.b[0], act.b[c]], writes=[pb], inc=(dvc == 1))
                        S.op("act", lambda e: e.activation(out=ms.t[:, :T], in_=ps[:, :T], func=AF.Ln, scale=1.0 / 256, bias=EPS), reads=[pb], writes=ms.b)
                        S.op("act", lambda e: e.activation(out=rstd.t[:, :T], in_=ms.t[:, :T], func=AF.Exp, scale=-0.5), reads=ms.b, writes=rstd.b)
                        for dvc in range(2):
                            c = 2 * h + dvc
                            S.op("dve", lambda e: e.scalar_tensor_tensor(out=ofc(c), in0=ofc(c),
                                                                         scalar=hncol.t[:, dvc:dvc + 1], in1=u3(rstd.t[:, :T]),
                                                                         op0=ALU.mult, op1=ALU.mult),
                                 reads=[xin.b[c], hncol.b[0], rstd.b[0]], writes=[xin.b[c]])
                            S.op("dve", lambda e: e.tensor_tensor(out=u3(hn.t[:, c, :T]), in0=ofc(c), in1=u3(srt.t[:, c, :T]), op=ALU.mult),
                                 reads=[xin.b[c], srt.b[c]], writes=[hn.b[c]])
                    if ti + 1 < len(TILES):
                        load_os(ti + 1)
                    for j in range(2):
                        sl = wget(("m512", "gla_w_out", j))
                        proj_fm(T, sl, 4, 0, lambda i, ps, pb: S.op(
                            "dve", lambda e: e.tensor_tensor(out=hT.t[:, j * 4 + i, :T], in0=ps[:, :T], in1=hT.t[:, j * 4 + i, :T], op=ALU.add),
                            reads=[pb, hT.b[j * 4 + i]], writes=[hT.b[j * 4 + i]]))
                    ffn(T, 0, 2)
                    ffn(T, 1, 1)
                    store_hT(T, t0, H4[s])
                    rmsnorm(T, 4)

                    def qk_head(ps, pb, gi, dst_ap, dst_b):
                        q_ = qkt[qki[0] % 2]
                        qki[0] += 1
                        xs, xn, r1, r2, sqh, ms_, rs_ = q_["xs"], q_["xn"], q_["r1"], q_["r2"], q_["sqh"], q_["ms"], q_["rstd"]
                        S.op("act", lambda e: e.copy(xs.t[:, :T], ps[:, :T]), reads=[pb], writes=xs.b)
                        S.op("act", lambda e: e.activation(out=sqh.t[:, :T], in_=ps[:, :T], func=AF.Square), reads=[pb], writes=sqh.b)
                        p2, p2b = psum()
                        S.op("pe", lambda e: e.matmul(p2[:, :T], ones_b.t[:], sqh.t[:, :T], start=True, stop=True),
                             reads=[ones_b.b[0], sqh.b[0]], writes=[p2b])
                        S.op("act", lambda e: e.activation(out=ms_.t[:, :T], in_=p2[:, :T], func=AF.Ln, scale=1.0 / 128, bias=EPS), reads=[p2b], writes=ms_.b)
                        S.op("act", lambda e: e.activation(out=rs_.t[:, :T], in_=ms_.t[:, :T], func=AF.Exp, scale=-0.5), reads=ms_.b, writes=rs_.b)
                        S.op("dve", lambda e: e.scalar_tensor_tensor(out=xn.t[:, :T], in0=xs.t[:, :T], scalar=qkcol.t[:, gi:gi + 1],
                                                                     in1=rs_.t[:, :T], op0=ALU.mult, op1=ALU.mult),
                             reads=[xs.b[0], qkcol.b[0], rs_.b[0]], writes=xn.b)
                        p3, p3b = psum()
                        S.op("pe", lambda e: e.matmul(p3[:, :T], pmt.t[:], xn.t[:, :T], start=True, stop=True),
                             reads=[pmt.b[0], xn.b[0]], writes=[p3b])
                        S.op("pool", lambda e: e.tensor_tensor(out=r1.t[:, :T], in0=xn.t[:, :T], in1=Ct.t[:, :T], op=ALU.mult),
                             reads=[xn.b[0], Ct.b[0]], writes=r1.b)
                        S.op("dve", lambda e: e.tensor_tensor(out=r2.t[:, :T], in0=p3[:, :T], in1=St.t[:, :T], op=ALU.mult),
                             reads=[p3b, St.b[0]], writes=r2.b)
                        S.op("dve", lambda e: e.tensor_tensor(out=dst_ap, in0=r1.t[:, :T], in1=r2.t[:, :T], op=ALU.add),
                             reads=[r1.b[0], r2.b[0]], writes=dst_b)

                    for j in range(2):
                        sl = wget(("m512", "attn_w_in", j))
                        proj_fm(T, sl, 4, 0, lambda i, ps, pb: qk_head(ps, pb, 0, q2st.t[:, j * 4 + i, :T], [q2st.b[j * 4 + i]]))
                    sl = wget(("m512", "attn_w_in", 2))
                    proj_fm(T, sl, 2, 0, lambda i, ps, pb: qk_head(ps, pb, 1, KT.t[:, i, t0:t0 + T], [KT.b[i]]))
                    v = sl.t[:].rearrange("p (k c) -> p k c", k=8)
                    for u, (o, n) in enumerate(subs):
                        ps, pb = psum()
                        for kc in range(8):
                            S.op("pe", lambda e: e.matmul(ps[0:n, 0:256], hn.t[:, kc, o:o + n], v[:, kc, 256:512],
                                                          start=(kc == 0), stop=(kc == 7)),
                                 reads=[sl.b[0], hn.b[kc]], writes=[pb], inc=(kc == 7))
                        evac(VR.t[0:n, (t0 + o) // 128, :], ps[0:n, 0:256], [pb], VR.b)
                    S.dma("pool", Q2[s].rearrange("(c p) t -> p c t", p=128)[:, :, t0:t0 + T], q2st.t[:, :, :T], q2st.ds, reads=q2st.b)
                S.barrier()
            if stop_after == "B":
                esKV.close()
                break

            with ExitStack() as esD:
                sbD = lambda name, shape, dt, nb=1, dsem=False: TT(S, esD, f"{name}_{s}", shape, dt, nb, dsem, semname=name)
                qts = [sbD(f"qt{i}", [128, 8, 512], BF16, dsem=True) for i in range(2)]
                pT = [sbD(f"pT{i}", [128, 512], BF16) for i in range(6)]
                rden = [sbD(f"rden{i}", [128, 512], F32) for i in range(2)]
                dacc = [[sbD(f"dacc{i}{j}", [128, 512], F32) for j in range(3)] for i in range(2)]
                yf = sbD("yf", [128, 8, 512], F32, nb=8)
                pst[1] = 4
                pti = 0
                def load_q(ti_):
                    t0_, T_ = TILES[1:][ti_]
                    q_ = qts[ti_ % 2]
                    S.dma("sp", q_.t[:, :, :T_], Q2[s].rearrange("(c p) t -> p c t", p=128)[:, :, t0_:t0_ + T_], q_.ds, writes=q_.b)

                load_q(0)
                for ti, (t0, T) in enumerate(TILES[1:]):
                    subs = subs_of(T)
                    qt = qts[ti % 2]
                    if ti + 1 < len(TILES) - 1:
                        load_q(ti + 1)
                    load_hT(T, t0, H4[s])
                    steps = [(head, blk) for head in range(8) for blk in range(33)]
                    LOOK = 3
                    sps = {}

                    def emit_s(i):
                        head, blk = steps[i]
                        kvh = head // 4
                        K_ = 128 if blk < 32 else NMETA
                        ps, pb = psum()
                        S.op("pe", lambda e: e.matmul(ps[0:K_, :T], KT.t[:, kvh, blk * 128:blk * 128 + K_], qt.t[:, head, :T],
                                                      start=True, stop=True), reads=[KT.b[kvh], qt.b[0]], writes=[pb])
                        sps[i] = (ps, pb)

                    for i in range(LOOK):
                        emit_s(i)
                    for i, (head, blk) in enumerate(steps):
                        kvh = head // 4
                        K_ = 128 if blk < 32 else NMETA
                        ps, pb = sps.pop(i)
                        a0 = 4 + 2 * (head % 2)
                        po, pob, pd, pdb = psb[a0], psB[a0], psb[a0 + 1], psB[a0 + 1]
                        p_ = pT[pti % 6]
                        pti += 1
                        S.op("act", lambda e: e.activation(out=p_.t[0:K_, :T], in_=ps[0:K_, :T], func=AF.Exp),
                             reads=[pb], writes=p_.b)
                        S.op("pe", lambda e: e.matmul(po[:, :T], VR.t[0:K_, blk, kvh * 128:(kvh + 1) * 128], p_.t[0:K_, :T],
                                                      start=(blk == 0), stop=(blk == 32)),
                             reads=[VR.b[0], p_.b[0]], writes=[pob], inc=True)
                        ai = blk % 3
                        if ai == 2:
                            S.op("pe", lambda e: e.matmul(pd[:, :T], ones_b.t[0:K_, :], p_.t[0:K_, :T], start=(blk == 2), stop=False),
                                 reads=[ones_b.b[0], p_.b[0]], writes=[pdb], inc=True)
                        else:
                            acc = dacc[head % 2][ai]
                            if blk < 2:
                                S.op("dve", lambda e: e.tensor_copy(acc.t[:, :T], p_.t[:, :T]), reads=p_.b, writes=acc.b)
                            else:
                                S.op("dve", lambda e: e.tensor_tensor(out=acc.t[0:K_, :T], in0=acc.t[0:K_, :T], in1=p_.t[0:K_, :T], op=ALU.add),
                                     reads=[acc.b[0], p_.b[0]], writes=acc.b)
                        if i + LOOK < len(steps):
                            emit_s(i + LOOK)
                        if blk == 32:
                            a_ = dacc[head % 2]
                            S.op("dve", lambda e: e.tensor_tensor(out=a_[0].t[:, :T], in0=a_[0].t[:, :T], in1=a_[1].t[:, :T], op=ALU.add),
                                 reads=[a_[0].b[0], a_[1].b[0]], writes=a_[0].b)
                            S.op("pe", lambda e: e.matmul(pd[:, :T], ones_f.t[:], a_[0].t[:, :T], start=False, stop=True),
                                 reads=[ones_f.b[0], a_[0].b[0]], writes=[pdb])
                            rd = rden[head % 2]
                            S.op("act", lambda e: e.activation(out=rd.t[:, :T], in_=pd[:, :T], func=AF.Ln), reads=[pdb], writes=rd.b)
                            S.op("act", lambda e: e.activation(out=rd.t[:, :T], in_=rd.t[:, :T], func=AF.Exp, scale=-1.0), reads=rd.b, writes=rd.b)
                            S.op("dve", lambda e: e.tensor_tensor(out=hn.t[:, head, :T], in0=po[:, :T], in1=rd.t[:, :T], op=ALU.mult),
                                 reads=[pob, rd.b[0]], writes=[hn.b[head]])
                    for j in range(2):
                        sl = wget(("m512", "attn_w_out", j))
                        proj_fm(T, sl, 4, 0, lambda i, ps, pb: S.op(
                            "dve", lambda e: e.tensor_tensor(out=hT.t[:, j * 4 + i, :T], in0=ps[:, :T], in1=hT.t[:, j * 4 + i, :T], op=ALU.add),
                            reads=[pb, hT.b[j * 4 + i]], writes=[hT.b[j * 4 + i]]))
                    ffn(T, 1, 2)
                    for c in range(8):
                        S.op("act", lambda e: e.activation(out=act.t[:, c, :T], in_=hT.t[:, c, :T], func=AF.Square),
                             reads=[hT.b[c]], writes=[act.b[c]])
                    ps, pb = psum()
                    for c in range(8):
                        S.op("pe", lambda e: e.matmul(ps[:, :T], ones_b.t[:], act.t[:, c, :T], start=(c == 0), stop=(c == 7)),
                             reads=[ones_b.b[0], act.b[c]], writes=[pb], inc=(c == 7))
                    S.op("act", lambda e: e.activation(out=ms.t[:, :T], in_=ps[:, :T], func=AF.Ln, scale=1.0 / D, bias=EPS), reads=[pb], writes=ms.b)
                    S.op("act", lambda e: e.activation(out=rstd.t[:, :T], in_=ms.t[:, :T], func=AF.Exp, scale=-0.5), reads=ms.b, writes=rstd.b)
                    for c in range(8):
                        S.op("dve", lambda e: e.scalar_tensor_tensor(out=yf.t[:, c, :T], in0=hT.t[:, c, :T], scalar=gcol.t[:, 6, c:c + 1],
                                                                     in1=rstd.t[:, :T], op0=ALU.mult, op1=ALU.mult),
                             reads=[hT.b[c], gcol.b[0], rstd.b[0]], writes=[yf.b[c]])
                    for u, (o, n) in enumerate(subs):
                        for cg in range(2):
                            ps, pb = psum()
                            for c4 in range(4):
                                c = cg * 4 + c4
                                S.op("pe", lambda e: e.transpose(ps[0:n, c4 * 128:(c4 + 1) * 128], yf.t[:, c, o:o + n], ident_f.t[:, :]),
                                     reads=[yf.b[c], ident_f.b[0]], writes=[pb], inc=(c4 == 3))
                            evac(xin.t[0:n, u, cg * 512:(cg + 1) * 512], ps[0:n, :], [pb], xin.b)
                    S.dma("pool", out[s, t0:t0 + T, :].rearrange("(u p) d -> p u d", p=128), xin.t[:, :, :], xin.ds, reads=xin.b)
                pst[1] = 8
                S.barrier()
            esKV.close()

        S.barrier()
        print("instructions:", S.n_ins, "weights consumed", wst["next"], "/", len(wkeys))
    return nc


def make_consts():
    c = {}
    c["c_ident"] = np.eye(128, dtype=np.float32)
    s_ = np.arange(128)[:, None]
    t_ = np.arange(128)[None, :]
    same = (s_ // 64) == (t_ // 64)
    c["c_maskf"] = (same & (s_ <= t_)).astype(np.float32)
    c["c_maskb"] = (same & (s_ >= t_)).astype(np.float32)
    m = np.ones((128, 512), np.float32)
    m[:, ::64] = 0.0
    c["c_scan"] = m
    pm = np.zeros((128, 128), np.float32)
    for d in range(128):
        if d % 64 < 32:
            pm[d, d + 32] = -1.0
        else:
            pm[d, d - 32] = 1.0
    c["c_pmt"] = np.ascontiguousarray(pm.T)
    inv = (10000.0 ** (-np.arange(0, 64, 2, dtype=np.float32) / 64.0)).astype(np.float32)
    pos = np.arange(SEQ)
    trow = np.concatenate([pos // 64, np.zeros(NMETA, np.int64)]).astype(np.float32)
    tcol = np.concatenate([pos % 64, np.zeros(NMETA, np.int64)]).astype(np.float32)
    ang = np.zeros((128, NT), np.float32)
    for d in range(128):
        ang[d] = (trow if d < 64 else tcol) * inv[d % 32]
    c["c_cos"] = np.cos(ang).astype(np.float32)
    c["c_sin"] = np.sin(ang).astype(np.float32)
    return c


_NC_CACHE = {}


def kernel(**inputs):
    n_cores = 8
    n_seq = 2
    if "nc" not in _NC_CACHE:
        _NC_CACHE["nc"] = build(n_seq=n_seq)
    nc = _NC_CACHE["nc"]
    consts = make_consts()
    x = np.ascontiguousarray(np.asarray(inputs["x"], dtype=np.float32))
    shared = {k: np.ascontiguousarray(np.asarray(v, dtype=np.float32)) for k, v in inputs.items() if k != "x"}
    shared.update(consts)
    in_maps = []
    for c in range(n_cores):
        m = dict(shared)
        m["x"] = x[c * n_seq:(c + 1) * n_seq]
        in_maps.append(m)
    res = run_bass_kernel_spmd(nc, in_maps, core_ids=list(range(n_cores)))
    return np.concatenate([r["out"] for r in res.results], axis=0).astype(np.float32)
```

```python
import numpy as np
import concourse.bass as bass
import concourse.mybir as mybir
from concourse.bass_utils import run_bass_kernel_spmd
from contextlib import ExitStack

F32 = mybir.dt.float32
BF16 = mybir.dt.bfloat16
AF = mybir.ActivationFunctionType
ALU = mybir.AluOpType

D = 1024
DFF = 2816
SEQ = 4096
NMETA = 16
NT = SEQ + NMETA
EPS = 1e-6
NSLOT = 4
TILES = [(SEQ, NMETA)] + [(i * 512, 512) for i in range(8)]
SUBS = [(SEQ, NMETA)] + [(i * 128, 128) for i in range(32)]


class Buf:
    __slots__ = ("name", "w", "r")

    def __init__(self, name):
        self.name = name
        self.w = None
        self.r = []


class DSem:
    __slots__ = ("sem", "cnt")

    def __init__(self, sem):
        self.sem = sem
        self.cnt = 0


class Sched:
    def __init__(self, nc, es):
        self.nc = nc
        self.es = es
        self.eng = {"pe": nc.tensor, "act": nc.scalar, "dve": nc.vector,
                    "pool": nc.gpsimd, "sp": nc.sync}
        self.esem = {k: es.enter_context(nc.semaphore("e_" + k)) for k in self.eng}
        self.ecnt = {k: 0 for k in self.eng}
        self.seen = {k: {} for k in self.eng}
        self.dsems = []
        self.n_ins = 0

    def dsem(self, name):
        d = DSem(self.es.enter_context(self.nc.semaphore("d_" + name)))
        self.dsems.append(d)
        return d

    def _wait(self, e, deps):
        best = {}
        for (sem, val) in deps:
            k = sem.num
            if k not in best or best[k][1] < val:
                best[k] = (sem, val)
        for k, (sem, val) in best.items():
            if e == "pe" and k == self.esem["pe"].num:
                continue
            if self.seen[e].get(k, 0) < val:
                self.eng[e].wait_ge(sem, val)
                self.seen[e][k] = val
                self.n_ins += 1

    @staticmethod
    def _deps(reads, writes):
        deps = []
        for b in reads:
            if b.w is not None:
                deps.append(b.w)
        for b in writes:
            if b.w is not None:
                deps.append(b.w)
            deps.extend(b.r)
        return deps

    @staticmethod
    def _compact(ticks):
        best = {}
        for (sem, val) in ticks:
            k = sem.num
            if k not in best or best[k][1] < val:
                best[k] = (sem, val)
        return list(best.values())

    def _mark(self, tick, reads, writes):
        for b in reads:
            b.r.append(tick)
            if len(b.r) > 32:
                b.r = self._compact(b.r)
        for b in writes:
            b.w = tick
            b.r = []

    def op(self, e, fn, reads=(), writes=(), inc=True):
        self._wait(e, self._deps(reads, writes))
        ins = fn(self.eng[e])
        self.n_ins += 1
        if inc:
            self.ecnt[e] += 1
            ins.then_inc(self.esem[e], 1)
            tick = (self.esem[e], self.ecnt[e])
        else:
            tick = (self.esem[e], self.ecnt[e] + 1)
        self._mark(tick, reads, writes)
        return ins

    def acquire(self, e, reads=(), writes=()):
        self._wait(e, self._deps(reads, writes))

    def dma(self, q, out, in_, ds, reads=(), writes=()):
        self._wait(q, self._deps(reads, writes))
        ins = self.eng[q].dma_start(out=out, in_=in_)
        ds.cnt += 16
        ins.then_inc(ds.sem, 16)
        self.n_ins += 1
        self._mark((ds.sem, ds.cnt), reads, writes)
        return ins

    def barrier(self):
        for e in self.eng:
            deps = [(self.esem[k], self.ecnt[k]) for k in self.eng if self.ecnt[k] > 0]
            deps += [(d.sem, d.cnt) for d in self.dsems if d.cnt > 0]
            for (sem, val) in deps:
                if self.seen[e].get(sem.num, 0) < val:
                    self.eng[e].wait_ge(sem, val)
                    self.seen[e][sem.num] = val
                    self.n_ins += 1


class TT:
    _cache = {}

    def __init__(self, S, es, name, shape, dt, nb=1, dsem=False, semname=None):
        self.t = es.enter_context(S.nc.sbuf_tensor(name, shape, dt))
        self.b = [Buf(f"{name}{i}") for i in range(nb)]
        semname = semname or name
        nsem = 0 if not dsem else (nb if dsem == "per" else 1)
        self.dsl = []
        for i in range(nsem):
            key = (id(S), f"{semname}{i}")
            if key not in TT._cache:
                TT._cache[key] = S.dsem(f"{semname}{i}")
            self.dsl.append(TT._cache[key])
        self.ds = self.dsl[0] if self.dsl else None
        self.ps = 1
        for s in shape[1:]:
            self.ps *= s


def weight_keys(n_seq):
    ks = []

    def ffn(l, f):
        for j in range(11):
            ks.append(("gu", l, f, j))
        for m in range(8):
            ks.append(("dn", l, f, m))

    for s in range(n_seq):
        for _ in TILES:
            ffn(0, 1)
            for j in range(6):
                ks.append(("m512", "gla_w_in", j))
        for _ in TILES:
            for j in range(2):
                ks.append(("m512", "gla_w_out", j))
            ffn(0, 2)
            ffn(1, 1)
            for j in range(3):
                ks.append(("m512", "attn_w_in", j))
        for _ in TILES[1:]:
            for j in range(2):
                ks.append(("m512", "attn_w_out", j))
            ffn(1, 2)
    return ks


def build(n_seq=2, stop_after="D", dbg=False):
    nc = bass.Bass("TRN2", target_bir_lowering=False)

    def din(name, shape):
        return nc.dram_tensor(name, list(shape), F32, kind="ExternalInput").ap()

    x = din("x", [n_seq, SEQ, D])
    meta_tokens = din("meta_tokens", [NMETA, D])
    norm_ffn1 = din("norm_ffn1", [2, D]); norm_mix = din("norm_mix", [2, D]); norm_ffn2 = din("norm_ffn2", [2, D])
    norm_final = din("norm_final", [D])
    Wg = {1: din("ffn1_w_gate", [2, D, DFF]), 2: din("ffn2_w_gate", [2, D, DFF])}
    Wu = {1: din("ffn1_w_up", [2, D, DFF]), 2: din("ffn2_w_up", [2, D, DFF])}
    Wd = {1: din("ffn1_w_down", [2, DFF, D]), 2: din("ffn2_w_down", [2, DFF, D])}
    Wm = {"gla_w_in": din("gla_w_in", [1, D, 3072])[0], "gla_w_out": din("gla_w_out", [1, D, D])[0],
          "attn_w_in": din("attn_w_in", [1, D, 1536])[0], "attn_w_out": din("attn_w_out", [1, D, D])[0]}
    gate_w1 = din("gla_gate_w1", [1, 2, D, 16])[0]
    gate_w2 = din("gla_gate_w2", [1, 2, 16, 512])[0]
    gate_b = din("gla_gate_b", [1, 2, 512])[0]
    head_norm = din("gla_head_norm", [1, 256])[0]
    q_norm = din("attn_q_norm", [1, 128])[0]
    k_norm = din("attn_k_norm", [1, 128])[0]
    c_ident = din("c_ident", [128, 128]); c_maskf = din("c_maskf", [128, 128]); c_maskb = din("c_maskb", [128, 128])
    c_scan = din("c_scan", [128, 512]); c_pmt = din("c_pmt", [128, 128])
    c_cos = din("c_cos", [128, NT]); c_sin = din("c_sin", [128, NT])

    out = nc.dram_tensor("out", [n_seq, SEQ, D], F32, kind="ExternalOutput").ap()
    WGU = {(l, f): nc.dram_tensor(f"wgu_{l}_{f}", [11, 128, 4096], BF16, kind="Internal").ap() for l in range(2) for f in (1, 2)}
    WDN = {(l, f): nc.dram_tensor(f"wdn_{l}_{f}", [8, 128, 2816], BF16, kind="Internal").ap() for l in range(2) for f in (1, 2)}
    WMB = {k: nc.dram_tensor(f"wmb_{k}", [v.shape[1] // 512, 128, 4096], BF16, kind="Internal").ap() for k, v in Wm.items()}
    wjob = {}
    skind = "ExternalOutput" if dbg else "Internal"

    def scr(name, shape, dt):
        return [nc.dram_tensor(f"{name}{s}", list(shape), dt, kind=skind).ap() for s in range(n_seq)]

    H1 = scr("H1_", [D, NT], F32); QD = scr("QD_", [2, 128, 33, 4, 128], BF16); KI = scr("KI_", [2, 128, 33, 4, 128], BF16)
    KE = scr("KE_", [2, NT, 512], BF16); VV = scr("VV_", [NT, 1024], BF16); SR = scr("SR_", [D, NT], F32)
    OO = scr("OO_", [2, 128, 33, 8, 128], F32)
    Q2 = scr("Q2_", [D, NT], BF16); H4 = scr("H4_", [D, NT], F32)

    with ExitStack() as es, nc.allow_non_contiguous_dma(reason="small param vectors"):
        S = Sched(nc, es)
        sb = lambda name, shape, dt, nb=1, dsem=False: TT(S, es, name, shape, dt, nb, dsem)

        ident_f = sb("ident_f", [128, 128], F32, dsem=True)
        ident_b = sb("ident_b", [128, 128], BF16)
        ones_b = sb("ones_b", [128, 128], BF16)
        maskf = sb("maskf", [128, 128], F32, dsem=True); maskb = sb("maskb", [128, 128], F32, dsem=True)
        scanm = sb("scanm", [128, 512], F32, dsem=True)
        pmt = sb("pmt", [128, 128], F32, dsem=True)
        neghalf = sb("neghalf", [128, 512], F32)
        ones_f = sb("ones_f", [128, 128], F32)
        gcol = sb("gcol", [128, 7, 8], F32, dsem=True)
        hncol = sb("hncol", [128, 2], F32, dsem=True)
        qkcol = sb("qkcol", [128, 2], F32, dsem=True)
        ngb = sb("ngb", [128, 2, 4], F32, dsem=True)
        w1 = sb("w1", [128, 2, 8, 16], BF16, dsem=True)
        w2 = sb("w2", [16, 2, 512], BF16, dsem=True)
        dec = sb("dec", [128, 2, 4, 65], F32)
        cst = [ident_f.b[0], ident_b.b[0], ones_b.b[0], maskf.b[0], maskb.b[0], scanm.b[0], pmt.b[0],
               neghalf.b[0], gcol.b[0], hncol.b[0], qkcol.b[0], ngb.b[0], w1.b[0], w2.b[0]]

        S.dma("sp", ident_f.t[:], c_ident, ident_f.ds, writes=ident_f.b)
        S.dma("sp", maskf.t[:], c_maskf, maskf.ds, writes=maskf.b)
        S.dma("sp", maskb.t[:], c_maskb, maskb.ds, writes=maskb.b)
        S.dma("sp", scanm.t[:], c_scan, scanm.ds, writes=scanm.b)
        S.dma("sp", pmt.t[:], c_pmt, pmt.ds, writes=pmt.b)
        gl = [norm_ffn1[0], norm_mix[0], norm_ffn2[0], norm_ffn1[1], norm_mix[1], norm_ffn2[1], norm_final]
        for i, g in enumerate(gl):
            S.dma("sp", gcol.t[:, i, :], g.rearrange("(c p) -> p c", p=128), gcol.ds, writes=gcol.b)
        S.dma("sp", hncol.t[:], head_norm.rearrange("(c p) -> p c", p=128), hncol.ds, writes=hncol.b)
        S.dma("sp", qkcol.t[:, 0:1], q_norm.rearrange("(c p) -> p c", p=128), qkcol.ds, writes=qkcol.b)
        S.dma("sp", qkcol.t[:, 1:2], k_norm.rearrange("(c p) -> p c", p=128), qkcol.ds, writes=qkcol.b)
        for n in range(2):
            S.dma("sp", ngb.t[:, n, :], gate_b[n].rearrange("(h p) -> p h", p=128), ngb.ds, writes=ngb.b)
            S.dma("pool", w1.t[:, n], gate_w1[n].rearrange("(kc p) r -> p kc r", p=128), w1.ds, writes=w1.b)
            S.dma("pool", w2.t[:, n, :], gate_w2[n], w2.ds, writes=w2.b)
        S.op("dve", lambda e: e.tensor_copy(ident_b.t[:], ident_f.t[:]), reads=ident_f.b, writes=ident_b.b)
        S.op("dve", lambda e: e.memset(ones_b.t[:], 1.0), writes=ones_b.b)
        S.op("dve", lambda e: e.memset(neghalf.t[:], -0.5), writes=neghalf.b)
        S.op("dve", lambda e: e.memset(ones_f.t[:], 1.0), writes=ones_f.b)
        S.op("dve", lambda e: e.tensor_scalar(out=ngb.t[:], in0=ngb.t[:], scalar1=-1.0, scalar2=None, op0=ALU.mult),
             reads=ngb.b, writes=ngb.b)
        S.op("dve", lambda e: e.tensor_scalar(out=qkcol.t[:, 0:1], in0=qkcol.t[:, 0:1], scalar1=128.0 ** -0.5,
                                              scalar2=None, op0=ALU.mult), reads=qkcol.b, writes=qkcol.b)

        class View:
            def __init__(self, t, b, ds):
                self.t, self.b, self.ds = t, b, ds

        def make_jobs(spec, gu_n, dn_n, m_n):
            jobs = []
            for it_ in spec:
                if it_[0] == "gu":
                    _, l, f = it_
                    for j0 in range(0, 11, gu_n):
                        nj = min(gu_n, 11 - j0)
                        jobs.append(dict(
                            keys=[("gu", l, f, j) for j in range(j0, j0 + nj)], rows=[(g, kc) for g in range(2) for kc in range(8)], ncols=nj * 256,
                            src=lambda r, l=l, f=f, j0=j0, nj=nj: (Wg[f][l] if r[0] == 0 else Wu[f][l])[r[1] * 128:(r[1] + 1) * 128, j0 * 256:(j0 + nj) * 256],
                            outv=lambda o, r, nj=nj: o.t[:, 0:nj * 4096].rearrange("p (j e) -> p j e", e=4096)[:, :, r[0] * 2048 + r[1] * 256:r[0] * 2048 + (r[1] + 1) * 256],
                            inv=lambda st, nj=nj: st.t[:, 0:nj * 256].rearrange("p (j c) -> p j c", c=256),
                            dst=WGU[(l, f)][j0:j0 + nj].rearrange("j p e -> p j e"), esz=(nj, 4096)))
                elif it_[0] == "dn":
                    _, l, f = it_
                    for m0 in range(0, 8, dn_n):
                        nm = min(dn_n, 8 - m0)
                        jobs.append(dict(
                            keys=[("dn", l, f, m) for m in range(m0, m0 + nm)], rows=list(range(22)), ncols=nm * 128,
                            src=lambda r, l=l, f=f, m0=m0, nm=nm: Wd[f][l][r * 128:(r + 1) * 128, m0 * 128:(m0 + nm) * 128],
                            outv=lambda o, r, nm=nm: o.t[:, 0:nm * 2816].rearrange("p (m e) -> p m e", e=2816)[:, :, r * 128:(r + 1) * 128],
                            inv=lambda st, nm=nm: st.t[:, 0:nm * 128].rearrange("p (m c) -> p m c", c=128),
                            dst=WDN[(l, f)][m0:m0 + nm].rearrange("m p e -> p m e"), esz=(nm, 2816)))
                else:
                    _, name = it_
                    npc_all = Wm[name].shape[1] // 512
                    for j0 in range(0, npc_all, m_n):
                        nj = min(m_n, npc_all - j0)
                        jobs.append(dict(
                            keys=[("m512", name, j) for j in range(j0, j0 + nj)], rows=list(range(8)), ncols=nj * 512,
                            src=lambda r, name=name, j0=j0, nj=nj: Wm[name][r * 128:(r + 1) * 128, j0 * 512:(j0 + nj) * 512],
                            outv=lambda o, r, nj=nj: o.t[:, 0:nj * 4096].rearrange("p (j e) -> p j e", e=4096)[:, :, r * 512:(r + 1) * 512],
                            inv=lambda st, nj=nj: st.t[:, 0:nj * 512].rearrange("p (j c) -> p j c", c=512),
                            dst=WMB[name][j0:j0 + nj].rearrange("j p e -> p j e"), esz=(nj, 4096)))
            return jobs

        def conv_gen(jobs, stages, outs, engs, store_q):
            items = [(ji, r) for ji, job in enumerate(jobs) for r in job["rows"]]
            ns = len(stages)
            ce = [0]

            def load(i):
                ji, r = items[i]
                st = stages[i % ns]
                S.dma("sp", st.t[:, 0:jobs[ji]["ncols"]], jobs[ji]["src"](r), st.ds, writes=st.b)

            for i in range(min(ns, len(items))):
                load(i)
            for i, (ji, r) in enumerate(items):
                job = jobs[ji]
                st = stages[i % ns]
                o = outs[ji % len(outs)]
                e = engs[ce[0] % len(engs)]
                ce[0] += 1
                if e == "act":
                    S.op("act", lambda en: en.copy(job["outv"](o, r), job["inv"](st)), reads=st.b, writes=o.b)
                else:
                    S.op(e, lambda en: en.tensor_copy(job["outv"](o, r), job["inv"](st)), reads=st.b, writes=o.b)
                if i + ns < len(items):
                    load(i + ns)
                if r == job["rows"][-1]:
                    jb = Buf("job")
                    nj, e_ = job["esz"]
                    S.dma(store_q, job["dst"], o.t[:, 0:nj * e_].rearrange("p (j e) -> p j e", e=e_), o.ds, reads=o.b, writes=[jb])
                    for k in job["keys"]:
                        wjob[k] = jb
                yield

        with ExitStack() as esP:
            stg = [TT(S, esP, f"cstg{i}", [128, 3072], F32, 1, True) for i in range(4)]
            ost_p = [TT(S, esP, f"cost{i}", [128, 24576], BF16, 1, True) for i in range(2)]
            for _ in conv_gen(make_jobs([("gu", 0, 1), ("dn", 0, 1), ("m", "gla_w_in"), ("m", "gla_w_out"), ("gu", 0, 2), ("dn", 0, 2),
                                         ("gu", 1, 1), ("dn", 1, 1), ("m", "attn_w_in"), ("m", "attn_w_out"), ("gu", 1, 2), ("dn", 1, 2)], 6, 8, 6),
                              stg, ost_p, ("act", "dve", "pool"), "pool"):
                pass
            S.barrier()
        deferred = []

        psb = [es.enter_context(nc.psum_tensor(f"ps{i}", [128, 512], F32)) for i in range(8)]
        psB = [Buf(f"ps{i}") for i in range(8)]
        pst = [0, 8]

        def psum():
            i = pst[0] % pst[1]
            pst[0] += 1
            return psb[i], psB[i]

        def psum_peek(n):
            return [psB[(pst[0] + k) % pst[1]] for k in range(min(n, pst[1]))]

        slots = [sb(f"wslot{i}", [128, 4096], BF16, dsem=True) for i in range(NSLOT)]
        wkeys = weight_keys(n_seq)
        wst = {"issued": 0, "next": 0}

        def issue_piece(i):
            key = wkeys[i]
            sl = slots[i % NSLOT]
            if key[0] == "gu":
                _, l, f, j = key
                S.dma("sp", sl.t[:, :], WGU[(l, f)][j], sl.ds, reads=[wjob[key]], writes=sl.b)
            elif key[0] == "dn":
                _, l, f, m = key
                S.dma("sp", sl.t[:, 0:2816], WDN[(l, f)][m], sl.ds, reads=[wjob[key]], writes=sl.b)
            else:
                _, name, j = key
                S.dma("sp", sl.t[:, :], WMB[name][j], sl.ds, reads=[wjob[key]], writes=sl.b)

        def wget(key):
            i = wst["next"]
            assert wkeys[i] == key, (i, wkeys[i], key)
            wst["next"] += 1
            while wst["issued"] < min(len(wkeys), i + NSLOT):
                if wkeys[wst["issued"]] not in wjob:
                    assert wst["issued"] > i
                    break
                issue_piece(wst["issued"])
                wst["issued"] += 1
            return slots[i % NSLOT]

        hT = sb("hT", [128, 8, 512], F32, nb=8)
        hn = sb("hn", [128, 8, 512], BF16, nb=8)
        act = sb("act", [128, 22, 512], BF16, nb=22)
        xin = sb("xin", [128, 4, 1024], F32, nb=8, dsem=True)
        ms = sb("ms", [128, 512], F32); rstd = sb("rstd", [128, 512], F32)
        sg = [sb(f"sg{i}", [128, 512], F32) for i in range(2)]
        hst_ds = S.dsem("hTst")
        rr = {"sg": 0, "cp": 0}

        def evac(out_ap, in_ap, reads, writes):
            rr["cp"] += 1
            if rr["cp"] % 2:
                S.op("act", lambda e: e.copy(out_ap, in_ap), reads=reads, writes=writes)
            else:
                S.op("dve", lambda e: e.tensor_copy(out_ap, in_ap), reads=reads, writes=writes)

        def rmsnorm(T, gi):
            for c in range(8):
                S.op("act", lambda e: e.activation(out=act.t[:, c, :T], in_=hT.t[:, c, :T], func=AF.Square),
                     reads=[hT.b[c]], writes=[act.b[c]])
            ps, pb = psum()
            for c in range(8):
                S.op("pe", lambda e: e.matmul(ps[:, :T], ones_b.t[:], act.t[:, c, :T], start=(c == 0), stop=(c == 7)),
                     reads=[ones_b.b[0], act.b[c]], writes=[pb], inc=(c == 7))
            S.op("act", lambda e: e.activation(out=ms.t[:, :T], in_=ps[:, :T], func=AF.Ln, scale=1.0 / D, bias=EPS), reads=[pb], writes=ms.b)
            S.op("act", lambda e: e.activation(out=rstd.t[:, :T], in_=ms.t[:, :T], func=AF.Exp, scale=-0.5), reads=ms.b, writes=rstd.b)
            for c in range(8):
                S.op("dve", lambda e: e.scalar_tensor_tensor(out=hn.t[:, c, :T], in0=hT.t[:, c, :T],
                                                             scalar=gcol.t[:, gi, c:c + 1], in1=rstd.t[:, :T],
                                                             op0=ALU.mult, op1=ALU.mult),
                     reads=[hT.b[c], gcol.b[0], rstd.b[0]], writes=[hn.b[c]])

        def ffn(T, l, f):
            rmsnorm(T, 3 * l + (0 if f == 1 else 2))
            for j in range(11):
                sl = wget(("gu", l, f, j))
                v = sl.t[:].rearrange("p (g k c) -> p g k c", g=2, k=8)
                S.acquire("pe", reads=sl.b, writes=psum_peek(4))
                for half in range(2):
                    fc = 2 * j + half
                    pg, pgb = psum()
                    pu, pub = psum()
                    for g, (pp, ppb) in enumerate(((pg, pgb), (pu, pub))):
                        for kc in range(8):
                            S.op("pe", lambda e: e.matmul(pp[:, :T], v[:, g, kc, half * 128:(half + 1) * 128],
                                                          hn.t[:, kc, :T], start=(kc == 0), stop=(kc == 7)),
                                 reads=[sl.b[0], hn.b[kc]], writes=[ppb], inc=(kc == 7))
                    s_ = sg[rr["sg"] % 2]
                    rr["sg"] += 1
                    S.op("act", lambda e: e.activation(out=s_.t[:, :T], in_=pg[:, :T], func=AF.Silu),
                         reads=[pgb], writes=s_.b)
                    S.op("dve", lambda e: e.tensor_tensor(out=act.t[:, fc, :T], in0=s_.t[:, :T], in1=pu[:, :T], op=ALU.mult),
                         reads=[s_.b[0], pub], writes=[act.b[fc]])
            for m in range(8):
                sl = wget(("dn", l, f, m))
                v = sl.t[:, 0:22 * 128].rearrange("p (k c) -> p k c", k=22)
                py, pyb = psum()
                for fc in range(22):
                    S.op("pe", lambda e: e.matmul(py[:, :T], v[:, fc, :], act.t[:, fc, :T], start=(fc == 0), stop=(fc == 21)),
                         reads=[sl.b[0], act.b[fc]], writes=[pyb], inc=(fc == 21))
                S.op("dve", lambda e: e.scalar_tensor_tensor(out=hT.t[:, m, :T], in0=py[:, :T], scalar=0.5,
                                                             in1=hT.t[:, m, :T], op0=ALU.mult, op1=ALU.add),
                     reads=[pyb, hT.b[m]], writes=[hT.b[m]])

        def proj_fm(T, sl, ncol, col0, consume):
            v = sl.t[:].rearrange("p (k c) -> p k c", k=8)
            S.acquire("pe", reads=sl.b, writes=psum_peek(ncol))
            for i in range(ncol):
                ps, pb = psum()
                for kc in range(8):
                    S.op("pe", lambda e: e.matmul(ps[:, :T], v[:, kc, col0 + i * 128:col0 + (i + 1) * 128], hn.t[:, kc, :T],
                                                  start=(kc == 0), stop=(kc == 7)),
                         reads=[sl.b[0], hn.b[kc]], writes=[pb], inc=(kc == 7))
                consume(i, ps, pb)

        def store_hT(T, t0, dst):
            S.dma("pool", dst.rearrange("(c p) t -> p c t", p=128)[:, :, t0:t0 + T], hT.t[:, :, :T], hst_ds, reads=hT.b)

        def load_hT(T, t0, src):
            S.dma("sp", hT.t[:, :, :T], src.rearrange("(c p) t -> p c t", p=128)[:, :, t0:t0 + T], hst_ds, writes=hT.b)

        def subs_of(T):
            return [(i * 128, 128) for i in range(T // 128)] if T >= 128 else [(0, T)]

        for s in range(n_seq):
            with ExitStack() as esA:
                sbA = lambda name, shape, dt, nb=1, dsem=False: TT(S, esA, f"{name}_{s}", shape, dt, nb, dsem, semname=name)
                qf = sbA("qf", [128, 4, 512], F32, nb=4); kf = sbA("kf", [128, 4, 512], F32, nb=4)
                ub = sbA("ub", [16, 2, 512], BF16, nb=2)
                tA = sbA("tA", [128, 512], F32); nl = sbA("nl", [128, 2, 512], F32, nb=2)
                Pc = sbA("Pc", [128, 2, 512], F32, nb=2); Ea = sbA("Ea", [128, 512], F32); Eb = sbA("Eb", [128, 512], F32)
                Pe = sbA("Pe", [128, 512], F32); t1 = sbA("t1", [128, 512], F32); t2 = sbA("t2", [128, 512], F32)
                Gs = sbA("Gs", [128, 2, 8], F32, nb=2)
                qdst = sbA("qdst", [128, 2, 4, 4, 128], BF16, nb=1, dsem=True)
                kist = sbA("kist", [128, 2, 4, 4, 128], BF16, nb=1, dsem=True)
                keT = sbA("keT", [128, 2, 2, 512], BF16, nb=2)
                ketok = sbA("ketok", [128, 2, 2, 4, 128], BF16, nb=2, dsem="per")
                vtok = sbA("vtok", [128, 2, 4, 512], BF16, nb=2, dsem="per")
                srst = sbA("srst", [128, 2, 4, 512], F32, nb=2, dsem="per")
                for ti, (t0, T) in enumerate(TILES):
                    subs = subs_of(T)
                    nch = max(1, T // 64)
                    clen = min(64, T)
                    ci0 = t0 // 64
                    def load_x(ti_):
                        t0_, T_ = TILES[ti_]
                        if T_ == NMETA:
                            S.dma("sp", xin.t[0:T_, 0, :], meta_tokens, xin.ds, writes=xin.b)
                        else:
                            S.dma("sp", xin.t[:, :, :], x[s, t0_:t0_ + T_, :].rearrange("(u p) d -> p u d", p=128), xin.ds,
                                  writes=xin.b)
                    if ti == 0:
                        load_x(0)
                    for c in range(8):
                        ps, pb = psum()
                        for u, (o, n) in enumerate(subs):
                            S.op("pe", lambda e: e.transpose(ps[:, o:o + n], xin.t[0:n, u, c * 128:(c + 1) * 128],
                                                             ident_f.t[0:n, 0:n]),
                                 reads=[xin.b[0], ident_f.b[0]], writes=[pb], inc=(u == len(subs) - 1))
                        evac(hT.t[:, c, :T], ps[:, :T], [pb], [hT.b[c]])
                    if ti + 1 < len(TILES):
                        load_x(ti + 1)
                    ffn(T, 0, 1)
                    store_hT(T, t0, H1[s])
                    rmsnorm(T, 1)
                    for n in range(2):
                        ps, pb = psum()
                        for kc in range(8):
                            S.op("pe", lambda e: e.matmul(ps[0:16, :T], w1.t[:, n, kc, :], hn.t[:, kc, :T],
                                                          start=(kc == 0), stop=(kc == 7)),
                                 reads=[w1.b[0], hn.b[kc]], writes=[pb], inc=(kc == 7))
                        evac(ub.t[0:16, n, :T], ps[0:16, :T], [pb], [ub.b[n]])
                    sl = wget(("m512", "gla_w_in", 0))
                    proj_fm(T, sl, 4, 0, lambda i, ps, pb: S.op(
                        "act", lambda e: e.mul(qf.t[:, i, :T], ps[:, :T], 128.0 ** -0.5), reads=[pb], writes=[qf.b[i]]))
                    sl = wget(("m512", "gla_w_in", 1))
                    proj_fm(T, sl, 4, 0, lambda i, ps, pb: evac(kf.t[:, i, :T], ps[:, :T], [pb], [kf.b[i]]))
                    def proj_piece(k):
                        if k < 2:
                            vp = k
                            sl = wget(("m512", "gla_w_in", 2 + vp))
                            v = sl.t[:].rearrange("p (k c) -> p k c", k=8)
                            for u, (o, n) in enumerate(subs):
                                ps, pb = psum()
                                for kc in range(8):
                                    S.op("pe", lambda e: e.matmul(ps[0:n, :], hn.t[:, kc, o:o + n], v[:, kc, :],
                                                                  start=(kc == 0), stop=(kc == 7)),
                                         reads=[sl.b[0], hn.b[kc]], writes=[pb], inc=(kc == 7))
                                evac(vtok.t[0:n, vp, u, :], ps[0:n, :], [pb], [vtok.b[vp]])
                            dst = VV[s][t0:t0 + T, vp * 512:(vp + 1) * 512]
                            if T >= 128:
                                S.dma("pool", dst.rearrange("(u p) c -> p u c", p=128), vtok.t[:, vp, :, :], vtok.dsl[vp], reads=[vtok.b[vp]])
                            else:
                                S.dma("pool", dst, vtok.t[0:T, vp, 0, :], vtok.dsl[vp], reads=[vtok.b[vp]])

                        else:
                            rp = k - 2
                            sl = wget(("m512", "gla_w_in", 4 + rp))
                            proj_fm(T, sl, 4, 0, lambda i, ps, pb: S.op(
                                "act", lambda e: e.activation(out=srst.t[:, rp, i, :T], in_=ps[:, :T], func=AF.Silu),
                                reads=[pb], writes=[srst.b[rp]]))
                            S.dma("pool", SR[s].rearrange("(c p) t -> p c t", p=128)[:, rp * 4:(rp + 1) * 4, t0:t0 + T],
                                  srst.t[:, rp, :, :T], srst.dsl[rp], reads=[srst.b[rp]])

                    def gate_math(h):
                        par = h % 2

                        def bc(tt, off):
                            return bass.AP(tt.t, off + clen - 1, [[tt.ps, 128], [clen, nch], [0, clen]])

                        def ch(tt, off):
                            return bass.AP(tt.t, off + clen - 1, [[tt.ps, 128], [clen, nch]])

                        def v3(ap):
                            return ap.rearrange("p (c j) -> p c j", j=clen)

                        wid = min(128, T)
                        nsb = len(subs)

                        def d4(tt, d_):
                            return tt.t[:, d_, 0:nsb, h, 0:wid]

                        def u3(ap):
                            return ap.rearrange("p (u j) -> p u j", j=wid)

                        for n in range(2):
                            ps, pb = psum()
                            S.op("pe", lambda e: e.matmul(ps[:, :T], w2.t[0:16, n, h * 128:(h + 1) * 128], ub.t[0:16, n, :T],
                                                          start=True, stop=True), reads=[w2.b[0], ub.b[n]], writes=[pb])
                            S.op("act", lambda e: e.activation(out=tA.t[:, :T], in_=ps[:, :T], func=AF.Exp, scale=-1.0,
                                                               bias=ngb.t[:, n, h:h + 1]),
                                 reads=[pb, ngb.b[0]], writes=tA.b)
                            S.op("act", lambda e: e.activation(out=nl.t[:, n, :T], in_=tA.t[:, :T], func=AF.Ln, bias=1.0),
                                 reads=tA.b, writes=[nl.b[n]])
                            S.op("dve", lambda e: e.tensor_tensor_scan(out=Pc.t[:, n, :T], data0=scanm.t[:, :T],
                                                                       data1=nl.t[:, n, :T], initial=0.0,
                                                                       op0=ALU.mult, op1=ALU.add),
                                 reads=[scanm.b[0], nl.b[n]], writes=[Pc.b[n]])
                        S.op("act", lambda e: e.activation(out=Ea.t[:, :T], in_=Pc.t[:, 0, :T], func=AF.Exp, scale=-1.0 / 16),
                             reads=[Pc.b[0]], writes=Ea.b)
                        S.op("act", lambda e: e.activation(out=Eb.t[:, :T], in_=Pc.t[:, 0, :T], func=AF.Exp, scale=1.0 / 16),
                             reads=[Pc.b[0]], writes=Eb.b)
                        S.op("dve", lambda e: e.tensor_tensor(out=d4(qdst, 0), in0=u3(qf.t[:, h, :T]), in1=u3(Ea.t[:, :T]), op=ALU.mult),
                             reads=[qf.b[h], Ea.b[0]], writes=qdst.b)
                        S.op("dve", lambda e: e.tensor_tensor(out=t1.t[:, :T], in0=kf.t[:, h, :T], in1=Eb.t[:, :T], op=ALU.mult),
                             reads=[kf.b[h], Eb.b[0]], writes=t1.b)
                        S.op("act", lambda e: e.copy(d4(kist, 0), u3(t1.t[:, :T])), reads=t1.b, writes=kist.b)
                        S.op("pool", lambda e: e.tensor_tensor(out=v3(keT.t[:, par, 0, :T]), in0=v3(t1.t[:, :T]), in1=bc(Ea, 0), op=ALU.mult),
                             reads=[t1.b[0], Ea.b[0]], writes=[keT.b[par]])
                        S.op("act", lambda e: e.copy(dec.t[:, 0, h, ci0:ci0 + nch], ch(Ea, 0)), reads=Ea.b, writes=dec.b)
                        S.op("dve", lambda e: e.tensor_tensor(out=Pe.t[:, :T], in0=Pc.t[:, 1, :T], in1=nl.t[:, 1, :T], op=ALU.subtract),
                             reads=[Pc.b[1], nl.b[1]], writes=Pe.b)
                        S.op("act", lambda e: e.activation(out=Ea.t[:, :T], in_=Pe.t[:, :T], func=AF.Exp, scale=-1.0 / 16),
                             reads=Pe.b, writes=Ea.b)
                        S.op("act", lambda e: e.activation(out=Eb.t[:, :T], in_=Pe.t[:, :T], func=AF.Exp, scale=1.0 / 16),
                             reads=Pe.b, writes=Eb.b)
                        S.op("act", lambda e: e.activation(out=Gs.t[:, 0, 0:nch], in_=ch(Pc, 512), func=AF.Exp, scale=-1.0 / 16),
                             reads=[Pc.b[1]], writes=[Gs.b[0]])
                        S.op("act", lambda e: e.activation(out=Gs.t[:, 1, 0:nch], in_=ch(Pc, 512), func=AF.Exp, scale=1.0 / 16),
                             reads=[Pc.b[1]], writes=[Gs.b[1]])
                        g1b = bass.AP(Gs.t, 0, [[16, 128], [1, nch], [0, clen]])
                        g2b = bass.AP(Gs.t, 8, [[16, 128], [1, nch], [0, clen]])
                        S.op("dve", lambda e: e.tensor_tensor(out=t1.t[:, :T], in0=qf.t[:, h, :T], in1=Eb.t[:, :T], op=ALU.mult),
                             reads=[qf.b[h], Eb.b[0]], writes=t1.b)
                        S.op("pool", lambda e: e.tensor_tensor(out=d4(qdst, 1), in0=v3(t1.t[:, :T]), in1=g1b, op=ALU.mult),
                             reads=[t1.b[0], Gs.b[0]], writes=qdst.b)
                        S.op("dve", lambda e: e.tensor_tensor(out=t2.t[:, :T], in0=kf.t[:, h, :T], in1=Ea.t[:, :T], op=ALU.mult),
                             reads=[kf.b[h], Ea.b[0]], writes=t2.b)
                        S.op("act", lambda e: e.copy(keT.t[:, par, 1, :T], t2.t[:, :T]), reads=t2.b, writes=[keT.b[par]])
                        S.op("pool", lambda e: e.tensor_tensor(out=d4(kist, 1), in0=v3(t2.t[:, :T]), in1=g2b, op=ALU.mult),
                             reads=[t2.b[0], Gs.b[1]], writes=kist.b)
                        S.op("act", lambda e: e.copy(dec.t[:, 1, h, ci0:ci0 + nch], Gs.t[:, 0, 0:nch]), reads=[Gs.b[0]], writes=dec.b)
                    def ke_transposes(h):
                        par = h % 2
                        for d_ in range(2):
                            ps, pb = psum()
                            pbf = ps[:].bitcast(BF16)
                            for u, (o, n) in enumerate(subs):
                                S.op("pe", lambda e: e.transpose(pbf[0:n, u * 128:(u + 1) * 128], keT.t[:, par, d_, o:o + n], ident_b.t[:, :]),
                                     reads=[keT.b[par], ident_b.b[0]], writes=[pb], inc=(u == len(subs) - 1))
                            nu = len(subs)
                            n0 = subs[0][1]
                            evac(ketok.t[0:n0, par, d_, 0:nu, :], pbf[0:n0, 0:nu * 128].rearrange("p (u c) -> p u c", c=128),
                                 [pb], [ketok.b[par]])
                            dst = KE[s][d_, t0:t0 + T, h * 128:(h + 1) * 128]
                            if T >= 128:
                                S.dma("pool", dst.rearrange("(u p) c -> p u c", p=128), ketok.t[:, par, d_, :, :], ketok.dsl[par], reads=[ketok.b[par]])
                            else:
                                S.dma("pool", dst, ketok.t[0:T, par, d_, 0, :], ketok.dsl[par], reads=[ketok.b[par]])
                    for h in range(4):
                        gate_math(h)
                        proj_piece(h)
                        ke_transposes(h)
                    for d_ in range(2):
                        for tt_, dst_ in ((qdst, QD), (kist, KI)):
                            if T >= 128:
                                S.dma("pool", dst_[s][d_, :, t0 // 128:t0 // 128 + 4, :, :], tt_.t[:, d_, :, :, :], tt_.ds, reads=tt_.b)
                            else:
                                S.dma("pool", dst_[s][d_, :, 32, :, 0:T], tt_.t[:, d_, 0, :, 0:T], tt_.ds, reads=tt_.b)
                S.barrier()
            if stop_after == "A":
                break

            with ExitStack() as esG:
                sbG = lambda name, shape, dt, nb=1, dsem=False: TT(S, esG, f"{name}_{s}", shape, dt, nb, dsem, semname=name)
                Sst = sbG("Sst", [128, 2, 4, 256], F32, nb=8)
                Sbf = sbG("Sbf", [128, 3, 4, 2, 256], BF16, nb=24)
                qdl = [sbG(f"qdl{i}", [128, 4, 128], BF16, dsem=True) for i in range(2)]
                kil = [sbG(f"kil{i}", [128, 4, 128], BF16, dsem=True) for i in range(2)]
                kel = [sbG(f"kel{i}", [128, 512], BF16, dsem=True) for i in range(2)]
                keh = [sbG(f"keh{i}", [128, 512], BF16, dsem=True) for i in range(2)]
                vl = [sbG(f"vl{i}", [128, 1024], BF16, dsem=True) for i in range(2)]
                Am = [sbG(f"Am{i}", [128, 4, 128], BF16, nb=4) for i in range(2)]
                ost = [sbG(f"ost{i}", [128, 8, 128], F32, dsem=True) for i in range(2)]
                ofl = [sbG(f"ofl{i}", [128, 8, 128], F32, dsem=True) for i in range(2)]
                cgen = None
                if s == 0 and deferred:
                    ost_g = [sbG(f"costg{i}", [128, 8448], BF16, dsem=True) for i in range(2)]
                    fx = xin.t[:].rearrange("p u d -> p (u d)")
                    fh = hT.t[:].rearrange("p c t -> p (c t)")
                    stg_g = [View(f_[:, i * 1024:(i + 1) * 1024], [Buf(f"cs{k}{i}")], S.dsem(f"cs{k}{i}"))
                             for k, f_ in enumerate((fx, fh)) for i in range(4)]
                    cgen = conv_gen(deferred, stg_g, ost_g, ("pool", "act"), "pool")

                def conv_step(k):
                    nonlocal_c = cgen
                    if nonlocal_c is None:
                        return
                    for _ in range(k):
                        if next(nonlocal_c, "done") == "done":
                            break
                for i in range(2):
                    S.op("dve", lambda e: e.memset(kel[i].t[:], 0.0), writes=kel[i].b)
                    S.op("dve", lambda e: e.memset(keh[i].t[:], 0.0), writes=keh[i].b)
                for d_ in range(2):
                    order = SUBS if d_ == 0 else SUBS[::-1]
                    mk = maskf if d_ == 0 else maskb
                    S.op("dve", lambda e: e.memset(Sst.t[:], 0.0), writes=Sst.b)
                    scur = [0, 0, 0, 0]
                    for h in range(4):
                        S.op("dve", lambda e: e.memset(Sbf.t[:, 0, h, 0, :], 0.0), writes=[Sbf.b[h * 2]])
                    def g_loads(it):
                        t0, n = order[it]
                        par = it % 2
                        S.dma("sp", qdl[par].t[:, :, 0:n], QD[s][d_, :, t0 // 128, :, 0:n], qdl[par].ds, writes=qdl[par].b)
                        S.dma("sp", kil[par].t[:, :, 0:n], KI[s][d_, :, t0 // 128, :, 0:n], kil[par].ds, writes=kil[par].b)
                        if n == 128:
                            S.dma("sp", kel[par].t[0:64, :], KE[s][d_, t0:t0 + 64, :], kel[par].ds, writes=kel[par].b)
                            S.dma("sp", keh[par].t[64:128, :], KE[s][d_, t0 + 64:t0 + 128, :], keh[par].ds, writes=keh[par].b)
                        else:
                            S.dma("sp", kel[par].t[0:n, :], KE[s][d_, t0:t0 + n, :], kel[par].ds, writes=kel[par].b)
                        S.dma("sp", vl[par].t[0:n, :], VV[s][t0:t0 + n, :], vl[par].ds, writes=vl[par].b)
                        if d_ == 1:
                            S.dma("sp", ofl[par].t[:, :, 0:n], OO[s][0, :, t0 // 128, :, 0:n], ofl[par].ds, writes=ofl[par].b)

                    gst = {}

                    def g_ctx(it):
                        t0, n = order[it]
                        par = it % 2
                        sbi = lambda p_, h_, k_: p_ * 8 + h_ * 2 + k_
                        p3 = it % 3
                        if n == 128:
                            chunks = [(0, 64, kel[par], t0 // 64), (64, 64, keh[par], t0 // 64 + 1)]
                            if d_ == 1:
                                chunks = chunks[::-1]
                            Kc = 128
                        else:
                            chunks = [(0, n, kel[par], 64)]
                            Kc = n
                        return t0, n, par, p3, sbi, chunks, Kc

                    def g_stage1(it):
                        t0, n, par, p3, sbi, chunks, Kc = g_ctx(it)
                        pkvs, pas = [], []
                        for h in range(4):
                            pkv, pkvb = psum()
                            for k, (co, cl, ket, ci) in enumerate(chunks):
                                S.op("pe", lambda e: e.matmul(pkv[:, k * 256:(k + 1) * 256], ket.t[0:Kc, h * 128:(h + 1) * 128],
                                                              vl[par].t[0:Kc, h * 256:(h + 1) * 256], start=True, stop=True),
                                     reads=[ket.b[0], vl[par].b[0]], writes=[pkvb])
                            pkvs.append((pkv, pkvb))
                        pa, pab = psum()
                        for h in range(4):
                            S.op("pe", lambda e: e.matmul(pa[0:n, h * 128:h * 128 + n], kil[par].t[:, h, 0:n], qdl[par].t[:, h, 0:n], start=True, stop=True),
                                 reads=[kil[par].b[0], qdl[par].b[0]], writes=[pab], inc=(h == 3))
                        for k, (co, cl, ket, ci) in enumerate(chunks):
                            for h in range(4):
                                pkv, pkvb = pkvs[h]
                                a_, b_ = scur[h], 1 - scur[h]
                                scur[h] = b_
                                S.op("dve", lambda e: e.scalar_tensor_tensor(out=Sst.t[:, b_, h, :], in0=Sst.t[:, a_, h, :],
                                                                             scalar=dec.t[:, d_, h, ci:ci + 1],
                                                                             in1=pkv[:, k * 256:(k + 1) * 256],
                                                                             op0=ALU.mult, op1=ALU.add),
                                     reads=[Sst.b[a_ * 4 + h], dec.b[0], pkvb], writes=[Sst.b[b_ * 4 + h]])
                                if k < len(chunks) - 1:
                                    tp, tk = p3, 1
                                else:
                                    tp, tk = (p3 + 1) % 3, 0
                                S.op("act", lambda e: e.copy(Sbf.t[:, tp, h, tk, :], Sst.t[:, b_, h, :]),
                                     reads=[Sst.b[b_ * 4 + h]], writes=[Sbf.b[sbi(tp, h, tk)]])
                            if k == 0:
                                for h in range(4):
                                    S.op("dve", lambda e: e.tensor_tensor(out=Am[par].t[0:n, h, 0:n], in0=pa[0:n, h * 128:h * 128 + n], in1=mk.t[0:n, 0:n], op=ALU.mult),
                                         reads=[pab, mk.b[0]], writes=[Am[par].b[h]])

                    def g_stage2(it):
                        t0, n, par, p3, sbi, chunks, Kc = g_ctx(it)
                        for hb in range(2):
                            po, pob = psum()
                            for h in (2 * hb, 2 * hb + 1):
                                for dvc in range(2):
                                    c0 = ((h % 2) * 2 + dvc) * 128
                                    S.op("pe", lambda e: e.matmul(po[:, c0:c0 + n], vl[par].t[0:n, h * 256 + dvc * 128:h * 256 + (dvc + 1) * 128],
                                                                  Am[par].t[0:n, h, 0:n], start=True, stop=False),
                                         reads=[vl[par].b[0], Am[par].b[h]], writes=[pob], inc=False)
                                    for k, (co, cl, ket, ci) in enumerate(chunks):
                                        last = (k == len(chunks) - 1)
                                        S.op("pe", lambda e: e.matmul(po[:, c0 + co:c0 + co + cl], Sbf.t[:, p3, h, k, dvc * 128:(dvc + 1) * 128],
                                                                      qdl[par].t[:, h, co:co + cl], start=False, stop=last),
                                             reads=[Sbf.b[sbi(p3, h, k)], qdl[par].b[0]], writes=[pob], inc=last)
                            for h in (2 * hb, 2 * hb + 1):
                                for dvc in range(2):
                                    c0 = ((h % 2) * 2 + dvc) * 128
                                    if d_ == 0:
                                        evac(ost[par].t[:, h * 2 + dvc, 0:n], po[:, c0:c0 + n], [pob], ost[par].b)
                                    else:
                                        S.op("dve", lambda e: e.tensor_tensor(out=ost[par].t[:, h * 2 + dvc, 0:n], in0=po[:, c0:c0 + n],
                                                                              in1=ofl[par].t[:, h * 2 + dvc, 0:n], op=ALU.add),
                                             reads=[pob, ofl[par].b[0]], writes=ost[par].b)
                        S.dma("pool", OO[s][d_, :, t0 // 128, :, 0:n], ost[par].t[:, :, 0:n], ost[par].ds, reads=ost[par].b)
                        if it + 2 < len(order):
                            g_loads(it + 2)

                    g_loads(0)
                    if len(order) > 1:
                        g_loads(1)
                    g_stage1(0)
                    for it in range(len(order)):
                        if it + 1 < len(order):
                            g_stage1(it + 1)
                        g_stage2(it)
                        conv_step(7)
                    if d_ == 1:
                        conv_step(10 ** 6)
                    S.barrier()
                S.barrier()
            if stop_after == "G":
                break

            esKV = ExitStack()
            KT = TT(S, esKV, f"KT_{s}", [128, 2, NT], BF16, 2)
            VR = TT(S, esKV, f"VR_{s}", [128, 33, 256], BF16, 1)
            with ExitStack() as esB:
                sbB = lambda name, shape, dt, nb=1, dsem=False: TT(S, esB, f"{name}_{s}", shape, dt, nb, dsem, semname=name)
                srt = sbB("srt", [128, 8, 512], F32, nb=8, dsem=True)
                oft = xin
                qkt = []
                for i in range(3):
                    xs_ = sbB(f"xs{i}", [128, 512], F32)
                    ms_ = sbB(f"msq{i}", [128, 512], F32)
                    qkt.append(dict(xs=xs_, r2=xs_, xn=sbB(f"xn{i}", [128, 512], F32), r1=sbB(f"r1{i}", [128, 512], F32),
                                    sqh=sbB(f"sqh{i}", [128, 512], BF16), ms=ms_, rstd=ms_))
                qki = [0]
                Ct = sbB("Ct", [128, 512], F32, dsem=True); St = sbB("St", [128, 512], F32, dsem=True)
                q2st = sbB("q2st", [128, 8, 512], BF16, nb=8, dsem=True)
                def load_os(ti_):
                    t0_, T_ = TILES[ti_]
                    if T_ >= 128:
                        S.dma("sp", xin.t[:, :, :], OO[s][1, :, t0_ // 128:t0_ // 128 + 4, :, :].rearrange("p u c j -> p u (c j)"), xin.ds, writes=xin.b)
                    else:
                        S.dma("sp", xin.t[:, 0, :].rearrange("p (c j) -> p c j", j=128)[:, :, 0:T_], OO[s][1, :, 32, :, 0:T_], xin.ds, writes=xin.b)
                    S.dma("sp", srt.t[:, :, :T_], SR[s].rearrange("(c p) t -> p c t", p=128)[:, :, t0_:t0_ + T_], srt.ds, writes=srt.b)

                for ti, (t0, T) in enumerate(TILES):
                    subs = subs_of(T)
                    if ti == 0:
                        load_os(0)
                    load_hT(T, t0, H1[s])
                    S.dma("sp", Ct.t[:, :T], c_cos[:, t0:t0 + T], Ct.ds, writes=Ct.b)
                    S.dma("sp", St.t[:, :T], c_sin[:, t0:t0 + T], St.ds, writes=St.b)
                    wid = min(128, T)
                    nsb = len(subs)
                    ofc = lambda c: xin.t[:, 0:nsb, c * 128:c * 128 + wid]
                    u3 = lambda ap: ap.rearrange("p (u j) -> p u j", j=wid)
                    for h in range(4):
                        ps, pb = psum()
                        for dvc in range(2):
                            c = 2 * h + dvc
                            S.op("act", lambda e: e.activation(out=u3(act.t[:, c, :T]), in_=ofc(c), func=AF.Square),
                                 reads=[xin.b[c]], writes=[act.b[c]])
                            S.op("pe", lambda e: e.matmul(ps[:, :T], ones_b.t[:], act.t[:, c, :T], start=(dvc == 0), stop=(dvc == 1)),
                                 reads=[ones_b.b[0], act.b[c]], writes=[pb], inc=(dvc == 1))
                        S.op("act", lambda e: e.activation(out=ms.t[:, :T], in_=ps[:, :T], func=AF.Ln, scale=1.0 / 256, bias=EPS), reads=[pb], writes=ms.b)
                        S.op("act", lambda e: e.activation(out=rstd.t[:, :T], in_=ms.t[:, :T], func=AF.Exp, scale=-0.5), reads=ms.b, writes=rstd.b)
                        for dvc in range(2):
                            c = 2 * h + dvc
                            S.op("dve", lambda e: e.scalar_tensor_tensor(out=ofc(c), in0=ofc(c),
                                                                         scalar=hncol.t[:, dvc:dvc + 1], in1=u3(rstd.t[:, :T]),
                                                                         op0=ALU.mult, op1=ALU.mult),
                                 reads=[xin.b[c], hncol.b[0], rstd.b[0]], writes=[xin.b[c]])
                            S.op("dve", lambda e: e.tensor_tensor(out=u3(hn.t[:, c, :T]), in0=ofc(c), in1=u3(srt.t[:, c, :T]), op=ALU.mult),
                                 reads=[xin.b[c], srt.b[c]], writes=[hn.b[c]])
                    if ti + 1 < len(TILES):
                        load_os(ti + 1)
                    for j in range(2):
                        sl = wget(("m512", "gla_w_out", j))
                        proj_fm(T, sl, 4, 0, lambda i, ps, pb: S.op(
                            "dve", lambda e: e.tensor_tensor(out=hT.t[:, j * 4 + i, :T], in0=ps[:, :T], in1=hT.t[:, j * 4 + i, :T], op=ALU.add),
                            reads=[pb, hT.b[j * 4 + i]], writes=[hT.b[j * 4 + i]]))
                    ffn(T, 0, 2)
                    ffn(T, 1, 1)
                    store_hT(T, t0, H4[s])
                    rmsnorm(T, 4)

                    pend = []

                    def qk_stage0(ps, pb, gi, dst_ap, dst_b):
                        q_ = qkt[qki[0] % 3]
                        qki[0] += 1
                        S.op("act", lambda e: e.copy(q_["xs"].t[:, :T], ps[:, :T]), reads=[pb], writes=q_["xs"].b)
                        S.op("act", lambda e: e.activation(out=q_["sqh"].t[:, :T], in_=ps[:, :T], func=AF.Square), reads=[pb], writes=q_["sqh"].b)
                        pend.append([q_, gi, dst_ap, dst_b, 0])
                        qk_advance(2)

                    def qk_stage1(it_):
                        q_, gi = it_[0], it_[1]
                        xs, xn, sqh, ms_, rs_ = q_["xs"], q_["xn"], q_["sqh"], q_["ms"], q_["rstd"]
                        p2, p2b = psum()
                        S.op("pe", lambda e: e.matmul(p2[:, :T], ones_b.t[:], sqh.t[:, :T], start=True, stop=True),
                             reads=[ones_b.b[0], sqh.b[0]], writes=[p2b])
                        S.op("act", lambda e: e.activation(out=ms_.t[:, :T], in_=p2[:, :T], func=AF.Ln, scale=1.0 / 128, bias=EPS), reads=[p2b], writes=ms_.b)
                        S.op("act", lambda e: e.activation(out=rs_.t[:, :T], in_=ms_.t[:, :T], func=AF.Exp, scale=-0.5), reads=ms_.b, writes=rs_.b)
                        S.op("dve", lambda e: e.scalar_tensor_tensor(out=xn.t[:, :T], in0=xs.t[:, :T], scalar=qkcol.t[:, gi:gi + 1],
                                                                     in1=rs_.t[:, :T], op0=ALU.mult, op1=ALU.mult),
                             reads=[xs.b[0], qkcol.b[0], rs_.b[0]], writes=xn.b)

                    def qk_stage2(it_):
                        q_, gi, dst_ap, dst_b = it_[0], it_[1], it_[2], it_[3]
                        xn, r1, r2 = q_["xn"], q_["r1"], q_["r2"]
                        p3, p3b = psum()
                        S.op("pe", lambda e: e.matmul(p3[:, :T], pmt.t[:], xn.t[:, :T], start=True, stop=True),
                             reads=[pmt.b[0], xn.b[0]], writes=[p3b])
                        S.op("pool", lambda e: e.tensor_tensor(out=r1.t[:, :T], in0=xn.t[:, :T], in1=Ct.t[:, :T], op=ALU.mult),
                             reads=[xn.b[0], Ct.b[0]], writes=r1.b)
                        S.op("dve", lambda e: e.tensor_tensor(out=r2.t[:, :T], in0=p3[:, :T], in1=St.t[:, :T], op=ALU.mult),
                             reads=[p3b, St.b[0]], writes=r2.b)
                        S.op("dve", lambda e: e.tensor_tensor(out=dst_ap, in0=r1.t[:, :T], in1=r2.t[:, :T], op=ALU.add),
                             reads=[r1.b[0], r2.b[0]], writes=dst_b)

                    def qk_advance(lag):
                        n_ = len(pend)
                        for k_, it_ in enumerate(pend):
                            age = n_ - 1 - k_
                            if it_[4] == 0 and age >= lag - 1:
                                qk_stage1(it_)
                                it_[4] = 1
                            elif it_[4] == 1 and age >= lag:
                                qk_stage2(it_)
                                it_[4] = 2
                        while pend and pend[0][4] == 2:
                            pend.pop(0)

                    def qk_flush():
                        while pend:
                            for it_ in pend:
                                if it_[4] == 0:
                                    qk_stage1(it_)
                                    it_[4] = 1
                                elif it_[4] == 1:
                                    qk_stage2(it_)
                                    it_[4] = 2
                            while pend and pend[0][4] == 2:
                                pend.pop(0)

                    for j in range(2):
                        sl = wget(("m512", "attn_w_in", j))
                        proj_fm(T, sl, 4, 0, lambda i, ps, pb: qk_stage0(ps, pb, 0, q2st.t[:, j * 4 + i, :T], [q2st.b[j * 4 + i]]))
                    sl = wget(("m512", "attn_w_in", 2))
                    proj_fm(T, sl, 2, 0, lambda i, ps, pb: qk_stage0(ps, pb, 1, KT.t[:, i, t0:t0 + T], [KT.b[i]]))
                    v = sl.t[:].rearrange("p (k c) -> p k c", k=8)
                    for u, (o, n) in enumerate(subs):
                        ps, pb = psum()
                        for kc in range(8):
                            S.op("pe", lambda e: e.matmul(ps[0:n, 0:256], hn.t[:, kc, o:o + n], v[:, kc, 256:512],
                                                          start=(kc == 0), stop=(kc == 7)),
                                 reads=[sl.b[0], hn.b[kc]], writes=[pb], inc=(kc == 7))
                        evac(VR.t[0:n, (t0 + o) // 128, :], ps[0:n, 0:256], [pb], VR.b)
                        if u == 0:
                            qk_advance(1)
                    qk_flush()
                    S.dma("pool", Q2[s].rearrange("(c p) t -> p c t", p=128)[:, :, t0:t0 + T], q2st.t[:, :, :T], q2st.ds, reads=q2st.b)
                S.barrier()
            if stop_after == "B":
                esKV.close()
                break

            with ExitStack() as esD:
                sbD = lambda name, shape, dt, nb=1, dsem=False: TT(S, esD, f"{name}_{s}", shape, dt, nb, dsem, semname=name)
                qts = [sbD(f"qt{i}", [128, 8, 512], BF16, dsem=True) for i in range(2)]
                pT = [sbD(f"pT{i}", [128, 512], BF16) for i in range(6)]
                rden = [sbD(f"rden{i}", [128, 512], F32) for i in range(2)]
                dacc = [[sbD(f"dacc{i}{j}", [128, 512], F32) for j in range(3)] for i in range(2)]
                yf = sbD("yf", [128, 8, 512], F32, nb=8)
                pst[1] = 4
                pti = 0
                def load_q(ti_):
                    t0_, T_ = TILES[1:][ti_]
                    q_ = qts[ti_ % 2]
                    S.dma("sp", q_.t[:, :, :T_], Q2[s].rearrange("(c p) t -> p c t", p=128)[:, :, t0_:t0_ + T_], q_.ds, writes=q_.b)

                load_q(0)
                for ti, (t0, T) in enumerate(TILES[1:]):
                    subs = subs_of(T)
                    qt = qts[ti % 2]
                    if ti + 1 < len(TILES) - 1:
                        load_q(ti + 1)
                    load_hT(T, t0, H4[s])
                    steps = [(head, blk) for head in range(8) for blk in range(33)]
                    LOOK = 3
                    sps = {}

                    def emit_s(i):
                        head, blk = steps[i]
                        kvh = head // 4
                        K_ = 128 if blk < 32 else NMETA
                        ps, pb = psum()
                        S.op("pe", lambda e: e.matmul(ps[0:K_, :T], KT.t[:, kvh, blk * 128:blk * 128 + K_], qt.t[:, head, :T],
                                                      start=True, stop=True), reads=[KT.b[kvh], qt.b[0]], writes=[pb])
                        sps[i] = (ps, pb)

                    for i in range(LOOK):
                        emit_s(i)
                    for i, (head, blk) in enumerate(steps):
                        kvh = head // 4
                        K_ = 128 if blk < 32 else NMETA
                        ps, pb = sps.pop(i)
                        a0 = 4 + 2 * (head % 2)
                        po, pob, pd, pdb = psb[a0], psB[a0], psb[a0 + 1], psB[a0 + 1]
                        p_ = pT[pti % 6]
                        pti += 1
                        S.op("act", lambda e: e.activation(out=p_.t[0:K_, :T], in_=ps[0:K_, :T], func=AF.Exp),
                             reads=[pb], writes=p_.b)
                        S.op("pe", lambda e: e.matmul(po[:, :T], VR.t[0:K_, blk, kvh * 128:(kvh + 1) * 128], p_.t[0:K_, :T],
                                                      start=(blk == 0), stop=(blk == 32)),
                             reads=[VR.b[0], p_.b[0]], writes=[pob], inc=True)
                        ai = blk % 3
                        if ai == 2:
                            S.op("pe", lambda e: e.matmul(pd[:, :T], ones_b.t[0:K_, :], p_.t[0:K_, :T], start=(blk == 2), stop=False),
                                 reads=[ones_b.b[0], p_.b[0]], writes=[pdb], inc=True)
                        else:
                            acc = dacc[head % 2][ai]
                            if blk < 2:
                                S.op("dve", lambda e: e.tensor_copy(acc.t[:, :T], p_.t[:, :T]), reads=p_.b, writes=acc.b)
                            else:
                                S.op("dve", lambda e: e.tensor_tensor(out=acc.t[0:K_, :T], in0=acc.t[0:K_, :T], in1=p_.t[0:K_, :T], op=ALU.add),
                                     reads=[acc.b[0], p_.b[0]], writes=acc.b)
                        if i + LOOK < len(steps):
                            emit_s(i + LOOK)
                        if blk == 32:
                            a_ = dacc[head % 2]
                            S.op("dve", lambda e: e.tensor_tensor(out=a_[0].t[:, :T], in0=a_[0].t[:, :T], in1=a_[1].t[:, :T], op=ALU.add),
                                 reads=[a_[0].b[0], a_[1].b[0]], writes=a_[0].b)
                            S.op("pe", lambda e: e.matmul(pd[:, :T], ones_f.t[:], a_[0].t[:, :T], start=False, stop=True),
                                 reads=[ones_f.b[0], a_[0].b[0]], writes=[pdb])
                            rd = rden[head % 2]
                            S.op("act", lambda e: e.activation(out=rd.t[:, :T], in_=pd[:, :T], func=AF.Ln), reads=[pdb], writes=rd.b)
                            S.op("act", lambda e: e.activation(out=rd.t[:, :T], in_=rd.t[:, :T], func=AF.Exp, scale=-1.0), reads=rd.b, writes=rd.b)
                            S.op("dve", lambda e: e.tensor_tensor(out=hn.t[:, head, :T], in0=po[:, :T], in1=rd.t[:, :T], op=ALU.mult),
                                 reads=[pob, rd.b[0]], writes=[hn.b[head]])
                    for j in range(2):
                        sl = wget(("m512", "attn_w_out", j))
                        proj_fm(T, sl, 4, 0, lambda i, ps, pb: S.op(
                            "dve", lambda e: e.tensor_tensor(out=hT.t[:, j * 4 + i, :T], in0=ps[:, :T], in1=hT.t[:, j * 4 + i, :T], op=ALU.add),
                            reads=[pb, hT.b[j * 4 + i]], writes=[hT.b[j * 4 + i]]))
                    ffn(T, 1, 2)
                    for c in range(8):
                        S.op("act", lambda e: e.activation(out=act.t[:, c, :T], in_=hT.t[:, c, :T], func=AF.Square),
                             reads=[hT.b[c]], writes=[act.b[c]])
                    ps, pb = psum()
                    for c in range(8):
                        S.op("pe", lambda e: e.matmul(ps[:, :T], ones_b.t[:], act.t[:, c, :T], start=(c == 0), stop=(c == 7)),
                             reads=[ones_b.b[0], act.b[c]], writes=[pb], inc=(c == 7))
                    S.op("act", lambda e: e.activation(out=ms.t[:, :T], in_=ps[:, :T], func=AF.Ln, scale=1.0 / D, bias=EPS), reads=[pb], writes=ms.b)
                    S.op("act", lambda e: e.activation(out=rstd.t[:, :T], in_=ms.t[:, :T], func=AF.Exp, scale=-0.5), reads=ms.b, writes=rstd.b)
                    for c in range(8):
                        S.op("dve", lambda e: e.scalar_tensor_tensor(out=yf.t[:, c, :T], in0=hT.t[:, c, :T], scalar=gcol.t[:, 6, c:c + 1],
                                                                     in1=rstd.t[:, :T], op0=ALU.mult, op1=ALU.mult),
                             reads=[hT.b[c], gcol.b[0], rstd.b[0]], writes=[yf.b[c]])
                    for u, (o, n) in enumerate(subs):
                        for cg in range(2):
                            ps, pb = psum()
                            for c4 in range(4):
                                c = cg * 4 + c4
                                S.op("pe", lambda e: e.transpose(ps[0:n, c4 * 128:(c4 + 1) * 128], yf.t[:, c, o:o + n], ident_f.t[:, :]),
                                     reads=[yf.b[c], ident_f.b[0]], writes=[pb], inc=(c4 == 3))
                            evac(xin.t[0:n, u, cg * 512:(cg + 1) * 512], ps[0:n, :], [pb], xin.b)
                    S.dma("pool", out[s, t0:t0 + T, :].rearrange("(u p) d -> p u d", p=128), xin.t[:, :, :], xin.ds, reads=xin.b)
                pst[1] = 8
                S.barrier()
            esKV.close()

        S.barrier()
        print("instructions:", S.n_ins, "weights consumed", wst["next"], "/", len(wkeys))
    return nc


def make_consts():
    c = {}
    c["c_ident"] = np.eye(128, dtype=np.float32)
    s_ = np.arange(128)[:, None]
    t_ = np.arange(128)[None, :]
    same = (s_ // 64) == (t_ // 64)
    c["c_maskf"] = (same & (s_ <= t_)).astype(np.float32)
    c["c_maskb"] = (same & (s_ >= t_)).astype(np.float32)
    m = np.ones((128, 512), np.float32)
    m[:, ::64] = 0.0
    c["c_scan"] = m
    pm = np.zeros((128, 128), np.float32)
    for d in range(128):
        if d % 64 < 32:
            pm[d, d + 32] = -1.0
        else:
            pm[d, d - 32] = 1.0
    c["c_pmt"] = np.ascontiguousarray(pm.T)
    inv = (10000.0 ** (-np.arange(0, 64, 2, dtype=np.float32) / 64.0)).astype(np.float32)
    pos = np.arange(SEQ)
    trow = np.concatenate([pos // 64, np.zeros(NMETA, np.int64)]).astype(np.float32)
    tcol = np.concatenate([pos % 64, np.zeros(NMETA, np.int64)]).astype(np.float32)
    ang = np.zeros((128, NT), np.float32)
    for d in range(128):
        ang[d] = (trow if d < 64 else tcol) * inv[d % 32]
    c["c_cos"] = np.cos(ang).astype(np.float32)
    c["c_sin"] = np.sin(ang).astype(np.float32)
    return c


_NC_CACHE = {}


def kernel(**inputs):
    n_cores = 8
    n_seq = 2
    if "nc" not in _NC_CACHE:
        _NC_CACHE["nc"] = build(n_seq=n_seq)
    nc = _NC_CACHE["nc"]
    consts = make_consts()
    x = np.ascontiguousarray(np.asarray(inputs["x"], dtype=np.float32))
    shared = {k: np.ascontiguousarray(np.asarray(v, dtype=np.float32)) for k, v in inputs.items() if k != "x"}
    shared.update(consts)
    in_maps = []
    for c in range(n_cores):
        m = dict(shared)
        m["x"] = x[c * n_seq:(c + 1) * n_seq]
        in_maps.append(m)
    res = run_bass_kernel_spmd(nc, in_maps, core_ids=list(range(n_cores)))
    return np.concatenate([r["out"] for r in res.results], axis=0).astype(np.float32)
```

```python
import numpy as np
import concourse.bass as bass
import concourse.mybir as mybir
from concourse.bass_utils import run_bass_kernel_spmd
from contextlib import ExitStack

F32 = mybir.dt.float32
BF16 = mybir.dt.bfloat16
AF = mybir.ActivationFunctionType
ALU = mybir.AluOpType

D = 1024
DFF = 2816
SEQ = 4096
NMETA = 16
NT = SEQ + NMETA
EPS = 1e-6
NSLOT = 4
TILES = [(SEQ, NMETA)] + [(i * 512, 512) for i in range(8)]
SUBS = [(SEQ, NMETA)] + [(i * 128, 128) for i in range(32)]


class Buf:
    __slots__ = ("name", "w", "r")

    def __init__(self, name):
        self.name = name
        self.w = None
        self.r = []


class DSem:
    __slots__ = ("sem", "cnt")

    def __init__(self, sem):
        self.sem = sem
        self.cnt = 0


class Sched:
    def __init__(self, nc, es):
        self.nc = nc
        self.es = es
        self.eng = {"pe": nc.tensor, "act": nc.scalar, "dve": nc.vector,
                    "pool": nc.gpsimd, "sp": nc.sync}
        self.esem = {k: es.enter_context(nc.semaphore("e_" + k)) for k in self.eng}
        self.ecnt = {k: 0 for k in self.eng}
        self.seen = {k: {} for k in self.eng}
        self.dsems = []
        self.n_ins = 0

    def dsem(self, name):
        d = DSem(self.es.enter_context(self.nc.semaphore("d_" + name)))
        self.dsems.append(d)
        return d

    def _wait(self, e, deps):
        best = {}
        for (sem, val) in deps:
            k = sem.num
            if k not in best or best[k][1] < val:
                best[k] = (sem, val)
        for k, (sem, val) in best.items():
            if e == "pe" and k == self.esem["pe"].num:
                continue
            if self.seen[e].get(k, 0) < val:
                self.eng[e].wait_ge(sem, val)
                self.seen[e][k] = val
                self.n_ins += 1

    @staticmethod
    def _deps(reads, writes):
        deps = []
        for b in reads:
            if b.w is not None:
                deps.append(b.w)
        for b in writes:
            if b.w is not None:
                deps.append(b.w)
            deps.extend(b.r)
        return deps

    @staticmethod
    def _compact(ticks):
        best = {}
        for (sem, val) in ticks:
            k = sem.num
            if k not in best or best[k][1] < val:
                best[k] = (sem, val)
        return list(best.values())

    def _mark(self, tick, reads, writes):
        for b in reads:
            b.r.append(tick)
            if len(b.r) > 32:
                b.r = self._compact(b.r)
        for b in writes:
            b.w = tick
            b.r = []

    def op(self, e, fn, reads=(), writes=(), inc=True):
        self._wait(e, self._deps(reads, writes))
        ins = fn(self.eng[e])
        self.n_ins += 1
        if inc:
            self.ecnt[e] += 1
            ins.then_inc(self.esem[e], 1)
            tick = (self.esem[e], self.ecnt[e])
        else:
            tick = (self.esem[e], self.ecnt[e] + 1)
        self._mark(tick, reads, writes)
        return ins

    def acquire(self, e, reads=(), writes=()):
        self._wait(e, self._deps(reads, writes))

    def dma(self, q, out, in_, ds, reads=(), writes=()):
        self._wait(q, self._deps(reads, writes))
        ins = self.eng[q].dma_start(out=out, in_=in_)
        ds.cnt += 16
        ins.then_inc(ds.sem, 16)
        self.n_ins += 1
        self._mark((ds.sem, ds.cnt), reads, writes)
        return ins

    def barrier(self):
        for e in self.eng:
            deps = [(self.esem[k], self.ecnt[k]) for k in self.eng if self.ecnt[k] > 0]
            deps += [(d.sem, d.cnt) for d in self.dsems if d.cnt > 0]
            for (sem, val) in deps:
                if self.seen[e].get(sem.num, 0) < val:
                    self.eng[e].wait_ge(sem, val)
                    self.seen[e][sem.num] = val
                    self.n_ins += 1


class TT:
    _cache = {}

    def __init__(self, S, es, name, shape, dt, nb=1, dsem=False, semname=None):
        self.t = es.enter_context(S.nc.sbuf_tensor(name, shape, dt))
        self.b = [Buf(f"{name}{i}") for i in range(nb)]
        semname = semname or name
        nsem = 0 if not dsem else (nb if dsem == "per" else 1)
        self.dsl = []
        for i in range(nsem):
            key = (id(S), f"{semname}{i}")
            if key not in TT._cache:
                TT._cache[key] = S.dsem(f"{semname}{i}")
            self.dsl.append(TT._cache[key])
        self.ds = self.dsl[0] if self.dsl else None
        self.ps = 1
        for s in shape[1:]:
            self.ps *= s


def weight_keys(n_seq):
    ks = []

    def ffn(l, f):
        for j in range(11):
            ks.append(("gu", l, f, j))
        for m in range(8):
            ks.append(("dn", l, f, m))

    for s in range(n_seq):
        for _ in TILES:
            ffn(0, 1)
            for j in range(6):
                ks.append(("m512", "gla_w_in", j))
        for _ in TILES:
            for j in range(2):
                ks.append(("m512", "gla_w_out", j))
            ffn(0, 2)
            ffn(1, 1)
            for j in range(3):
                ks.append(("m512", "attn_w_in", j))
        for _ in TILES[1:]:
            for j in range(2):
                ks.append(("m512", "attn_w_out", j))
            ffn(1, 2)
    return ks


def build(n_seq=2, stop_after="D", dbg=False):
    nc = bass.Bass("TRN2", target_bir_lowering=False)

    def din(name, shape):
        return nc.dram_tensor(name, list(shape), F32, kind="ExternalInput").ap()

    x = din("x", [n_seq, SEQ, D])
    meta_tokens = din("meta_tokens", [NMETA, D])
    norm_ffn1 = din("norm_ffn1", [2, D]); norm_mix = din("norm_mix", [2, D]); norm_ffn2 = din("norm_ffn2", [2, D])
    norm_final = din("norm_final", [D])
    Wg = {1: din("ffn1_w_gate", [2, D, DFF]), 2: din("ffn2_w_gate", [2, D, DFF])}
    Wu = {1: din("ffn1_w_up", [2, D, DFF]), 2: din("ffn2_w_up", [2, D, DFF])}
    Wd = {1: din("ffn1_w_down", [2, DFF, D]), 2: din("ffn2_w_down", [2, DFF, D])}
    Wm = {"gla_w_in": din("gla_w_in", [1, D, 3072])[0], "gla_w_out": din("gla_w_out", [1, D, D])[0],
          "attn_w_in": din("attn_w_in", [1, D, 1536])[0], "attn_w_out": din("attn_w_out", [1, D, D])[0]}
    gate_w1 = din("gla_gate_w1", [1, 2, D, 16])[0]
    gate_w2 = din("gla_gate_w2", [1, 2, 16, 512])[0]
    gate_b = din("gla_gate_b", [1, 2, 512])[0]
    head_norm = din("gla_head_norm", [1, 256])[0]
    q_norm = din("attn_q_norm", [1, 128])[0]
    k_norm = din("attn_k_norm", [1, 128])[0]
    c_ident = din("c_ident", [128, 128]); c_maskf = din("c_maskf", [128, 128]); c_maskb = din("c_maskb", [128, 128])
    c_scan = din("c_scan", [128, 512]); c_pmt = din("c_pmt", [128, 128])
    c_cos = din("c_cos", [128, NT]); c_sin = din("c_sin", [128, NT])

    out = nc.dram_tensor("out", [n_seq, SEQ, D], F32, kind="ExternalOutput").ap()
    WGU = {(l, f): nc.dram_tensor(f"wgu_{l}_{f}", [11, 128, 4096], BF16, kind="Internal").ap() for l in range(2) for f in (1, 2)}
    WDN = {(l, f): nc.dram_tensor(f"wdn_{l}_{f}", [8, 128, 2816], BF16, kind="Internal").ap() for l in range(2) for f in (1, 2)}
    WMB = {k: nc.dram_tensor(f"wmb_{k}", [v.shape[1] // 512, 128, 4096], BF16, kind="Internal").ap() for k, v in Wm.items()}
    wjob = {}
    skind = "ExternalOutput" if dbg else "Internal"

    def scr(name, shape, dt):
        return [nc.dram_tensor(f"{name}{s}", list(shape), dt, kind=skind).ap() for s in range(n_seq)]

    H1 = scr("H1_", [D, NT], F32); QD = scr("QD_", [2, 128, 33, 4, 128], BF16); KI = scr("KI_", [2, 128, 33, 4, 128], BF16)
    KE = scr("KE_", [2, NT, 512], BF16); VV = scr("VV_", [NT, 1024], BF16); SR = scr("SR_", [D, NT], F32)
    OO = scr("OO_", [2, 128, 33, 8, 128], F32)
    Q2 = scr("Q2_", [D, NT], BF16); H4 = scr("H4_", [D, NT], F32)

    with ExitStack() as es, nc.allow_non_contiguous_dma(reason="small param vectors"):
        S = Sched(nc, es)
        sb = lambda name, shape, dt, nb=1, dsem=False: TT(S, es, name, shape, dt, nb, dsem)

        ident_f = sb("ident_f", [128, 128], F32, dsem=True)
        ident_b = sb("ident_b", [128, 128], BF16)
        ones_b = sb("ones_b", [128, 128], BF16)
        maskf = sb("maskf", [128, 128], F32, dsem=True); maskb = sb("maskb", [128, 128], F32, dsem=True)
        scanm = sb("scanm", [128, 512], F32, dsem=True)
        pmt = sb("pmt", [128, 128], F32, dsem=True)
        neghalf = sb("neghalf", [128, 512], F32)
        ones_f = sb("ones_f", [128, 128], F32)
        gcol = sb("gcol", [128, 7, 8], F32, dsem=True)
        hncol = sb("hncol", [128, 2], F32, dsem=True)
        qkcol = sb("qkcol", [128, 2], F32, dsem=True)
        ngb = sb("ngb", [128, 2, 4], F32, dsem=True)
        w1 = sb("w1", [128, 2, 8, 16], BF16, dsem=True)
        w2 = sb("w2", [16, 2, 512], BF16, dsem=True)
        dec = sb("dec", [128, 2, 4, 65], F32)
        cst = [ident_f.b[0], ident_b.b[0], ones_b.b[0], maskf.b[0], maskb.b[0], scanm.b[0], pmt.b[0],
               neghalf.b[0], gcol.b[0], hncol.b[0], qkcol.b[0], ngb.b[0], w1.b[0], w2.b[0]]

        S.dma("sp", ident_f.t[:], c_ident, ident_f.ds, writes=ident_f.b)
        S.dma("sp", maskf.t[:], c_maskf, maskf.ds, writes=maskf.b)
        S.dma("sp", maskb.t[:], c_maskb, maskb.ds, writes=maskb.b)
        S.dma("sp", scanm.t[:], c_scan, scanm.ds, writes=scanm.b)
        S.dma("sp", pmt.t[:], c_pmt, pmt.ds, writes=pmt.b)
        gl = [norm_ffn1[0], norm_mix[0], norm_ffn2[0], norm_ffn1[1], norm_mix[1], norm_ffn2[1], norm_final]
        for i, g in enumerate(gl):
            S.dma("sp", gcol.t[:, i, :], g.rearrange("(c p) -> p c", p=128), gcol.ds, writes=gcol.b)
        S.dma("sp", hncol.t[:], head_norm.rearrange("(c p) -> p c", p=128), hncol.ds, writes=hncol.b)
        S.dma("sp", qkcol.t[:, 0:1], q_norm.rearrange("(c p) -> p c", p=128), qkcol.ds, writes=qkcol.b)
        S.dma("sp", qkcol.t[:, 1:2], k_norm.rearrange("(c p) -> p c", p=128), qkcol.ds, writes=qkcol.b)
        for n in range(2):
            S.dma("sp", ngb.t[:, n, :], gate_b[n].rearrange("(h p) -> p h", p=128), ngb.ds, writes=ngb.b)
            S.dma("pool", w1.t[:, n], gate_w1[n].rearrange("(kc p) r -> p kc r", p=128), w1.ds, writes=w1.b)
            S.dma("pool", w2.t[:, n, :], gate_w2[n], w2.ds, writes=w2.b)
        S.op("dve", lambda e: e.tensor_copy(ident_b.t[:], ident_f.t[:]), reads=ident_f.b, writes=ident_b.b)
        S.op("dve", lambda e: e.memset(ones_b.t[:], 1.0), writes=ones_b.b)
        S.op("dve", lambda e: e.memset(neghalf.t[:], -0.5), writes=neghalf.b)
        S.op("dve", lambda e: e.memset(ones_f.t[:], 1.0), writes=ones_f.b)
        S.op("dve", lambda e: e.tensor_scalar(out=ngb.t[:], in0=ngb.t[:], scalar1=-1.0, scalar2=None, op0=ALU.mult),
             reads=ngb.b, writes=ngb.b)
        S.op("dve", lambda e: e.tensor_scalar(out=qkcol.t[:, 0:1], in0=qkcol.t[:, 0:1], scalar1=128.0 ** -0.5,
                                              scalar2=None, op0=ALU.mult), reads=qkcol.b, writes=qkcol.b)

        class View:
            def __init__(self, t, b, ds):
                self.t, self.b, self.ds = t, b, ds

        def make_jobs(spec, gu_n, dn_n, m_n):
            jobs = []
            for it_ in spec:
                if it_[0] == "gu":
                    _, l, f = it_
                    for j0 in range(0, 11, gu_n):
                        nj = min(gu_n, 11 - j0)
                        jobs.append(dict(
                            keys=[("gu", l, f, j) for j in range(j0, j0 + nj)], rows=[(g, kc) for g in range(2) for kc in range(8)], ncols=nj * 256,
                            src=lambda r, l=l, f=f, j0=j0, nj=nj: (Wg[f][l] if r[0] == 0 else Wu[f][l])[r[1] * 128:(r[1] + 1) * 128, j0 * 256:(j0 + nj) * 256],
                            outv=lambda o, r, nj=nj: o.t[:, 0:nj * 4096].rearrange("p (j e) -> p j e", e=4096)[:, :, r[0] * 2048 + r[1] * 256:r[0] * 2048 + (r[1] + 1) * 256],
                            inv=lambda st, nj=nj: st.t[:, 0:nj * 256].rearrange("p (j c) -> p j c", c=256),
                            dst=WGU[(l, f)][j0:j0 + nj].rearrange("j p e -> p j e"), esz=(nj, 4096)))
                elif it_[0] == "dn":
                    _, l, f = it_
                    for m0 in range(0, 8, dn_n):
                        nm = min(dn_n, 8 - m0)
                        jobs.append(dict(
                            keys=[("dn", l, f, m) for m in range(m0, m0 + nm)], rows=list(range(22)), ncols=nm * 128,
                            src=lambda r, l=l, f=f, m0=m0, nm=nm: Wd[f][l][r * 128:(r + 1) * 128, m0 * 128:(m0 + nm) * 128],
                            outv=lambda o, r, nm=nm: o.t[:, 0:nm * 2816].rearrange("p (m e) -> p m e", e=2816)[:, :, r * 128:(r + 1) * 128],
                            inv=lambda st, nm=nm: st.t[:, 0:nm * 128].rearrange("p (m c) -> p m c", c=128),
                            dst=WDN[(l, f)][m0:m0 + nm].rearrange("m p e -> p m e"), esz=(nm, 2816)))
                else:
                    _, name = it_
                    npc_all = Wm[name].shape[1] // 512
                    for j0 in range(0, npc_all, m_n):
                        nj = min(m_n, npc_all - j0)
                        jobs.append(dict(
                            keys=[("m512", name, j) for j in range(j0, j0 + nj)], rows=list(range(8)), ncols=nj * 512,
                            src=lambda r, name=name, j0=j0, nj=nj: Wm[name][r * 128:(r + 1) * 128, j0 * 512:(j0 + nj) * 512],
                            outv=lambda o, r, nj=nj: o.t[:, 0:nj * 4096].rearrange("p (j e) -> p j e", e=4096)[:, :, r * 512:(r + 1) * 512],
                            inv=lambda st, nj=nj: st.t[:, 0:nj * 512].rearrange("p (j c) -> p j c", c=512),
                            dst=WMB[name][j0:j0 + nj].rearrange("j p e -> p j e"), esz=(nj, 4096)))
            return jobs

        def conv_gen(jobs, stages, outs, engs, store_q):
            items = [(ji, r) for ji, job in enumerate(jobs) for r in job["rows"]]
            ns = len(stages)
            ce = [0]

            def load(i):
                ji, r = items[i]
                st = stages[i % ns]
                S.dma("sp", st.t[:, 0:jobs[ji]["ncols"]], jobs[ji]["src"](r), st.ds, writes=st.b)

            for i in range(min(ns, len(items))):
                load(i)
            for i, (ji, r) in enumerate(items):
                job = jobs[ji]
                st = stages[i % ns]
                o = outs[ji % len(outs)]
                e = engs[ce[0] % len(engs)]
                ce[0] += 1
                if e == "act":
                    S.op("act", lambda en: en.copy(job["outv"](o, r), job["inv"](st)), reads=st.b, writes=o.b)
                else:
                    S.op(e, lambda en: en.tensor_copy(job["outv"](o, r), job["inv"](st)), reads=st.b, writes=o.b)
                if i + ns < len(items):
                    load(i + ns)
                if r == job["rows"][-1]:
                    jb = Buf("job")
                    nj, e_ = job["esz"]
                    S.dma(store_q, job["dst"], o.t[:, 0:nj * e_].rearrange("p (j e) -> p j e", e=e_), o.ds, reads=o.b, writes=[jb])
                    for k in job["keys"]:
                        wjob[k] = jb
                yield

        with ExitStack() as esP:
            stg = [TT(S, esP, f"cstg{i}", [128, 3072], F32, 1, True) for i in range(4)]
            ost_p = [TT(S, esP, f"cost{i}", [128, 24576], BF16, 1, True) for i in range(2)]
            for _ in conv_gen(make_jobs([("gu", 0, 1), ("dn", 0, 1), ("m", "gla_w_in"), ("m", "gla_w_out"), ("gu", 0, 2), ("dn", 0, 2),
                                         ("gu", 1, 1), ("dn", 1, 1), ("m", "attn_w_in"), ("m", "attn_w_out"), ("gu", 1, 2), ("dn", 1, 2)], 6, 8, 6),
                              stg, ost_p, ("act", "dve", "pool"), "pool"):
                pass
            S.barrier()
        deferred = []

        psb = [es.enter_context(nc.psum_tensor(f"ps{i}", [128, 512], F32)) for i in range(8)]
        psB = [Buf(f"ps{i}") for i in range(8)]
        pst = [0, 8]

        def psum():
            i = pst[0] % pst[1]
            pst[0] += 1
            return psb[i], psB[i]

        def psum_peek(n):
            return [psB[(pst[0] + k) % pst[1]] for k in range(min(n, pst[1]))]

        slots = [sb(f"wslot{i}", [128, 4096], BF16, dsem=True) for i in range(NSLOT)]
        wkeys = weight_keys(n_seq)
        wst = {"issued": 0, "next": 0}

        def issue_piece(i):
            key = wkeys[i]
            sl = slots[i % NSLOT]
            if key[0] == "gu":
                _, l, f, j = key
                S.dma("sp", sl.t[:, :], WGU[(l, f)][j], sl.ds, reads=[wjob[key]], writes=sl.b)
            elif key[0] == "dn":
                _, l, f, m = key
                S.dma("sp", sl.t[:, 0:2816], WDN[(l, f)][m], sl.ds, reads=[wjob[key]], writes=sl.b)
            else:
                _, name, j = key
                S.dma("sp", sl.t[:, :], WMB[name][j], sl.ds, reads=[wjob[key]], writes=sl.b)

        def wget(key):
            i = wst["next"]
            assert wkeys[i] == key, (i, wkeys[i], key)
            wst["next"] += 1
            while wst["issued"] < min(len(wkeys), i + NSLOT):
                if wkeys[wst["issued"]] not in wjob:
                    assert wst["issued"] > i
                    break
                issue_piece(wst["issued"])
                wst["issued"] += 1
            return slots[i % NSLOT]

        hT = sb("hT", [128, 8, 512], F32, nb=8)
        hn = sb("hn", [128, 8, 512], BF16, nb=8)
        act = sb("act", [128, 22, 512], BF16, nb=22)
        xin = sb("xin", [128, 4, 1024], F32, nb=8, dsem=True)
        ms = sb("ms", [128, 512], F32); rstd = sb("rstd", [128, 512], F32)
        sg = [sb(f"sg{i}", [128, 512], F32) for i in range(2)]
        hst_ds = S.dsem("hTst")
        hld_ds = S.dsem("hTld")
        xst_ds = S.dsem("xinst")
        rr = {"sg": 0, "cp": 0}

        def evac(out_ap, in_ap, reads, writes):
            rr["cp"] += 1
            if rr["cp"] % 2:
                S.op("act", lambda e: e.copy(out_ap, in_ap), reads=reads, writes=writes)
            else:
                S.op("dve", lambda e: e.tensor_copy(out_ap, in_ap), reads=reads, writes=writes)

        def rmsnorm(T, gi):
            for c in range(8):
                S.op("act", lambda e: e.activation(out=act.t[:, c, :T], in_=hT.t[:, c, :T], func=AF.Square),
                     reads=[hT.b[c]], writes=[act.b[c]])
            ps, pb = psum()
            for c in range(8):
                S.op("pe", lambda e: e.matmul(ps[:, :T], ones_b.t[:], act.t[:, c, :T], start=(c == 0), stop=(c == 7)),
                     reads=[ones_b.b[0], act.b[c]], writes=[pb], inc=(c == 7))
            S.op("act", lambda e: e.activation(out=ms.t[:, :T], in_=ps[:, :T], func=AF.Ln, scale=1.0 / D, bias=EPS), reads=[pb], writes=ms.b)
            S.op("act", lambda e: e.activation(out=rstd.t[:, :T], in_=ms.t[:, :T], func=AF.Exp, scale=-0.5), reads=ms.b, writes=rstd.b)
            for c in range(8):
                S.op("dve", lambda e: e.scalar_tensor_tensor(out=hn.t[:, c, :T], in0=hT.t[:, c, :T],
                                                             scalar=gcol.t[:, gi, c:c + 1], in1=rstd.t[:, :T],
                                                             op0=ALU.mult, op1=ALU.mult),
                     reads=[hT.b[c], gcol.b[0], rstd.b[0]], writes=[hn.b[c]])

        def ffn(T, l, f):
            rmsnorm(T, 3 * l + (0 if f == 1 else 2))
            for j in range(11):
                sl = wget(("gu", l, f, j))
                v = sl.t[:].rearrange("p (g k c) -> p g k c", g=2, k=8)
                S.acquire("pe", reads=sl.b, writes=psum_peek(4))
                for half in range(2):
                    fc = 2 * j + half
                    pg, pgb = psum()
                    pu, pub = psum()
                    for g, (pp, ppb) in enumerate(((pg, pgb), (pu, pub))):
                        for kc in range(8):
                            S.op("pe", lambda e: e.matmul(pp[:, :T], v[:, g, kc, half * 128:(half + 1) * 128],
                                                          hn.t[:, kc, :T], start=(kc == 0), stop=(kc == 7)),
                                 reads=[sl.b[0], hn.b[kc]], writes=[ppb], inc=(kc == 7))
                    s_ = sg[rr["sg"] % 2]
                    rr["sg"] += 1
                    S.op("act", lambda e: e.activation(out=s_.t[:, :T], in_=pg[:, :T], func=AF.Silu),
                         reads=[pgb], writes=s_.b)
                    S.op("dve", lambda e: e.tensor_tensor(out=act.t[:, fc, :T], in0=s_.t[:, :T], in1=pu[:, :T], op=ALU.mult),
                         reads=[s_.b[0], pub], writes=[act.b[fc]])
            for m in range(8):
                sl = wget(("dn", l, f, m))
                v = sl.t[:, 0:22 * 128].rearrange("p (k c) -> p k c", k=22)
                py, pyb = psum()
                for fc in range(22):
                    S.op("pe", lambda e: e.matmul(py[:, :T], v[:, fc, :], act.t[:, fc, :T], start=(fc == 0), stop=(fc == 21)),
                         reads=[sl.b[0], act.b[fc]], writes=[pyb], inc=(fc == 21))
                S.op("dve", lambda e: e.scalar_tensor_tensor(out=hT.t[:, m, :T], in0=py[:, :T], scalar=0.5,
                                                             in1=hT.t[:, m, :T], op0=ALU.mult, op1=ALU.add),
                     reads=[pyb, hT.b[m]], writes=[hT.b[m]])

        def proj_fm(T, sl, ncol, col0, consume):
            v = sl.t[:].rearrange("p (k c) -> p k c", k=8)
            S.acquire("pe", reads=sl.b, writes=psum_peek(ncol))
            for i in range(ncol):
                ps, pb = psum()
                for kc in range(8):
                    S.op("pe", lambda e: e.matmul(ps[:, :T], v[:, kc, col0 + i * 128:col0 + (i + 1) * 128], hn.t[:, kc, :T],
                                                  start=(kc == 0), stop=(kc == 7)),
                         reads=[sl.b[0], hn.b[kc]], writes=[pb], inc=(kc == 7))
                consume(i, ps, pb)

        def store_hT(T, t0, dst):
            S.dma("pool", dst.rearrange("(c p) t -> p c t", p=128)[:, :, t0:t0 + T], hT.t[:, :, :T], hst_ds, reads=hT.b)

        def load_hT(T, t0, src):
            S.dma("sp", hT.t[:, :, :T], src.rearrange("(c p) t -> p c t", p=128)[:, :, t0:t0 + T], hld_ds, writes=hT.b)

        def subs_of(T):
            return [(i * 128, 128) for i in range(T // 128)] if T >= 128 else [(0, T)]

        for s in range(n_seq):
            with ExitStack() as esA:
                sbA = lambda name, shape, dt, nb=1, dsem=False: TT(S, esA, f"{name}_{s}", shape, dt, nb, dsem, semname=name)
                qf = sbA("qf", [128, 4, 512], F32, nb=4); kf = sbA("kf", [128, 4, 512], F32, nb=4)
                ub = sbA("ub", [16, 2, 512], BF16, nb=2)
                tA = sbA("tA", [128, 512], F32); nl = sbA("nl", [128, 2, 512], F32, nb=2)
                Pc = sbA("Pc", [128, 2, 512], F32, nb=2); Ea = sbA("Ea", [128, 512], F32); Eb = sbA("Eb", [128, 512], F32)
                Pe = sbA("Pe", [128, 512], F32); t1 = sbA("t1", [128, 512], F32); t2 = sbA("t2", [128, 512], F32)
                Gs = sbA("Gs", [128, 2, 8], F32, nb=2)
                qdst = sbA("qdst", [128, 2, 4, 4, 128], BF16, nb=1, dsem=True)
                kist = sbA("kist", [128, 2, 4, 4, 128], BF16, nb=1, dsem=True)
                keT = sbA("keT", [128, 2, 2, 512], BF16, nb=2)
                ketok = sbA("ketok", [128, 2, 2, 4, 128], BF16, nb=2, dsem="per")
                vtok = sbA("vtok", [128, 2, 4, 512], BF16, nb=2, dsem="per")
                srst = sbA("srst", [128, 2, 4, 512], F32, nb=2, dsem="per")
                for ti, (t0, T) in enumerate(TILES):
                    subs = subs_of(T)
                    nch = max(1, T // 64)
                    clen = min(64, T)
                    ci0 = t0 // 64
                    def load_x(ti_):
                        t0_, T_ = TILES[ti_]
                        if T_ == NMETA:
                            S.dma("sp", xin.t[0:T_, 0, :], meta_tokens, xin.ds, writes=xin.b)
                        else:
                            S.dma("sp", xin.t[:, :, :], x[s, t0_:t0_ + T_, :].rearrange("(u p) d -> p u d", p=128), xin.ds,
                                  writes=xin.b)
                    if ti == 0:
                        load_x(0)
                    for c in range(8):
                        ps, pb = psum()
                        for u, (o, n) in enumerate(subs):
                            S.op("pe", lambda e: e.transpose(ps[:, o:o + n], xin.t[0:n, u, c * 128:(c + 1) * 128],
                                                             ident_f.t[0:n, 0:n]),
                                 reads=[xin.b[0], ident_f.b[0]], writes=[pb], inc=(u == len(subs) - 1))
                        evac(hT.t[:, c, :T], ps[:, :T], [pb], [hT.b[c]])
                    if ti + 1 < len(TILES):
                        load_x(ti + 1)
                    ffn(T, 0, 1)
                    store_hT(T, t0, H1[s])
                    rmsnorm(T, 1)
                    for n in range(2):
                        ps, pb = psum()
                        for kc in range(8):
                            S.op("pe", lambda e: e.matmul(ps[0:16, :T], w1.t[:, n, kc, :], hn.t[:, kc, :T],
                                                          start=(kc == 0), stop=(kc == 7)),
                                 reads=[w1.b[0], hn.b[kc]], writes=[pb], inc=(kc == 7))
                        evac(ub.t[0:16, n, :T], ps[0:16, :T], [pb], [ub.b[n]])
                    sl = wget(("m512", "gla_w_in", 0))
                    proj_fm(T, sl, 4, 0, lambda i, ps, pb: S.op(
                        "act", lambda e: e.mul(qf.t[:, i, :T], ps[:, :T], 128.0 ** -0.5), reads=[pb], writes=[qf.b[i]]))
                    sl = wget(("m512", "gla_w_in", 1))
                    proj_fm(T, sl, 4, 0, lambda i, ps, pb: evac(kf.t[:, i, :T], ps[:, :T], [pb], [kf.b[i]]))
                    def proj_piece(k):
                        if k < 2:
                            vp = k
                            sl = wget(("m512", "gla_w_in", 2 + vp))
                            v = sl.t[:].rearrange("p (k c) -> p k c", k=8)
                            for u, (o, n) in enumerate(subs):
                                ps, pb = psum()
                                for kc in range(8):
                                    S.op("pe", lambda e: e.matmul(ps[0:n, :], hn.t[:, kc, o:o + n], v[:, kc, :],
                                                                  start=(kc == 0), stop=(kc == 7)),
                                         reads=[sl.b[0], hn.b[kc]], writes=[pb], inc=(kc == 7))
                                evac(vtok.t[0:n, vp, u, :], ps[0:n, :], [pb], [vtok.b[vp]])
                            dst = VV[s][t0:t0 + T, vp * 512:(vp + 1) * 512]
                            if T >= 128:
                                S.dma("pool", dst.rearrange("(u p) c -> p u c", p=128), vtok.t[:, vp, :, :], vtok.dsl[vp], reads=[vtok.b[vp]])
                            else:
                                S.dma("pool", dst, vtok.t[0:T, vp, 0, :], vtok.dsl[vp], reads=[vtok.b[vp]])

                        else:
                            rp = k - 2
                            sl = wget(("m512", "gla_w_in", 4 + rp))
                            proj_fm(T, sl, 4, 0, lambda i, ps, pb: S.op(
                                "act", lambda e: e.activation(out=srst.t[:, rp, i, :T], in_=ps[:, :T], func=AF.Silu),
                                reads=[pb], writes=[srst.b[rp]]))
                            S.dma("pool", SR[s].rearrange("(c p) t -> p c t", p=128)[:, rp * 4:(rp + 1) * 4, t0:t0 + T],
                                  srst.t[:, rp, :, :T], srst.dsl[rp], reads=[srst.b[rp]])

                    def gate_math(h):
                        par = h % 2

                        def bc(tt, off):
                            return bass.AP(tt.t, off + clen - 1, [[tt.ps, 128], [clen, nch], [0, clen]])

                        def ch(tt, off):
                            return bass.AP(tt.t, off + clen - 1, [[tt.ps, 128], [clen, nch]])

                        def v3(ap):
                            return ap.rearrange("p (c j) -> p c j", j=clen)

                        wid = min(128, T)
                        nsb = len(subs)

                        def d4(tt, d_):
                            return tt.t[:, d_, 0:nsb, h, 0:wid]

                        def u3(ap):
                            return ap.rearrange("p (u j) -> p u j", j=wid)

                        for n in range(2):
                            ps, pb = psum()
                            S.op("pe", lambda e: e.matmul(ps[:, :T], w2.t[0:16, n, h * 128:(h + 1) * 128], ub.t[0:16, n, :T],
                                                          start=True, stop=True), reads=[w2.b[0], ub.b[n]], writes=[pb])
                            S.op("act", lambda e: e.activation(out=tA.t[:, :T], in_=ps[:, :T], func=AF.Exp, scale=-1.0,
                                                               bias=ngb.t[:, n, h:h + 1]),
                                 reads=[pb, ngb.b[0]], writes=tA.b)
                            S.op("act", lambda e: e.activation(out=nl.t[:, n, :T], in_=tA.t[:, :T], func=AF.Ln, bias=1.0),
                                 reads=tA.b, writes=[nl.b[n]])
                            S.op("dve", lambda e: e.tensor_tensor_scan(out=Pc.t[:, n, :T], data0=scanm.t[:, :T],
                                                                       data1=nl.t[:, n, :T], initial=0.0,
                                                                       op0=ALU.mult, op1=ALU.add),
                                 reads=[scanm.b[0], nl.b[n]], writes=[Pc.b[n]])
                        S.op("act", lambda e: e.activation(out=Ea.t[:, :T], in_=Pc.t[:, 0, :T], func=AF.Exp, scale=-1.0 / 16),
                             reads=[Pc.b[0]], writes=Ea.b)
                        S.op("act", lambda e: e.activation(out=Eb.t[:, :T], in_=Pc.t[:, 0, :T], func=AF.Exp, scale=1.0 / 16),
                             reads=[Pc.b[0]], writes=Eb.b)
                        S.op("dve", lambda e: e.tensor_tensor(out=d4(qdst, 0), in0=u3(qf.t[:, h, :T]), in1=u3(Ea.t[:, :T]), op=ALU.mult),
                             reads=[qf.b[h], Ea.b[0]], writes=qdst.b)
                        S.op("dve", lambda e: e.tensor_tensor(out=t1.t[:, :T], in0=kf.t[:, h, :T], in1=Eb.t[:, :T], op=ALU.mult),
                             reads=[kf.b[h], Eb.b[0]], writes=t1.b)
                        S.op("act", lambda e: e.copy(d4(kist, 0), u3(t1.t[:, :T])), reads=t1.b, writes=kist.b)
                        S.op("pool", lambda e: e.tensor_tensor(out=v3(keT.t[:, par, 0, :T]), in0=v3(t1.t[:, :T]), in1=bc(Ea, 0), op=ALU.mult),
                             reads=[t1.b[0], Ea.b[0]], writes=[keT.b[par]])
                        S.op("act", lambda e: e.copy(dec.t[:, 0, h, ci0:ci0 + nch], ch(Ea, 0)), reads=Ea.b, writes=dec.b)
                        S.op("dve", lambda e: e.tensor_tensor(out=Pe.t[:, :T], in0=Pc.t[:, 1, :T], in1=nl.t[:, 1, :T], op=ALU.subtract),
                             reads=[Pc.b[1], nl.b[1]], writes=Pe.b)
                        S.op("act", lambda e: e.activation(out=Ea.t[:, :T], in_=Pe.t[:, :T], func=AF.Exp, scale=-1.0 / 16),
                             reads=Pe.b, writes=Ea.b)
                        S.op("act", lambda e: e.activation(out=Eb.t[:, :T], in_=Pe.t[:, :T], func=AF.Exp, scale=1.0 / 16),
                             reads=Pe.b, writes=Eb.b)
                        S.op("act", lambda e: e.activation(out=Gs.t[:, 0, 0:nch], in_=ch(Pc, 512), func=AF.Exp, scale=-1.0 / 16),
                             reads=[Pc.b[1]], writes=[Gs.b[0]])
                        S.op("act", lambda e: e.activation(out=Gs.t[:, 1, 0:nch], in_=ch(Pc, 512), func=AF.Exp, scale=1.0 / 16),
                             reads=[Pc.b[1]], writes=[Gs.b[1]])
                        g1b = bass.AP(Gs.t, 0, [[16, 128], [1, nch], [0, clen]])
                        g2b = bass.AP(Gs.t, 8, [[16, 128], [1, nch], [0, clen]])
                        S.op("dve", lambda e: e.tensor_tensor(out=t1.t[:, :T], in0=qf.t[:, h, :T], in1=Eb.t[:, :T], op=ALU.mult),
                             reads=[qf.b[h], Eb.b[0]], writes=t1.b)
                        S.op("pool", lambda e: e.tensor_tensor(out=d4(qdst, 1), in0=v3(t1.t[:, :T]), in1=g1b, op=ALU.mult),
                             reads=[t1.b[0], Gs.b[0]], writes=qdst.b)
                        S.op("dve", lambda e: e.tensor_tensor(out=t2.t[:, :T], in0=kf.t[:, h, :T], in1=Ea.t[:, :T], op=ALU.mult),
                             reads=[kf.b[h], Ea.b[0]], writes=t2.b)
                        S.op("act", lambda e: e.copy(keT.t[:, par, 1, :T], t2.t[:, :T]), reads=t2.b, writes=[keT.b[par]])
                        S.op("pool", lambda e: e.tensor_tensor(out=d4(kist, 1), in0=v3(t2.t[:, :T]), in1=g2b, op=ALU.mult),
                             reads=[t2.b[0], Gs.b[1]], writes=kist.b)
                        S.op("act", lambda e: e.copy(dec.t[:, 1, h, ci0:ci0 + nch], Gs.t[:, 0, 0:nch]), reads=[Gs.b[0]], writes=dec.b)
                    def ke_transposes(h):
                        par = h % 2
                        for d_ in range(2):
                            ps, pb = psum()
                            pbf = ps[:].bitcast(BF16)
                            for u, (o, n) in enumerate(subs):
                                S.op("pe", lambda e: e.transpose(pbf[0:n, u * 128:(u + 1) * 128], keT.t[:, par, d_, o:o + n], ident_b.t[:, :]),
                                     reads=[keT.b[par], ident_b.b[0]], writes=[pb], inc=(u == len(subs) - 1))
                            nu = len(subs)
                            n0 = subs[0][1]
                            evac(ketok.t[0:n0, par, d_, 0:nu, :], pbf[0:n0, 0:nu * 128].rearrange("p (u c) -> p u c", c=128),
                                 [pb], [ketok.b[par]])
                            dst = KE[s][d_, t0:t0 + T, h * 128:(h + 1) * 128]
                            if T >= 128:
                                S.dma("pool", dst.rearrange("(u p) c -> p u c", p=128), ketok.t[:, par, d_, :, :], ketok.dsl[par], reads=[ketok.b[par]])
                            else:
                                S.dma("pool", dst, ketok.t[0:T, par, d_, 0, :], ketok.dsl[par], reads=[ketok.b[par]])
                    for h in range(4):
                        gate_math(h)
                        proj_piece(h)
                        ke_transposes(h)
                    for d_ in range(2):
                        for tt_, dst_ in ((qdst, QD), (kist, KI)):
                            if T >= 128:
                                S.dma("pool", dst_[s][d_, :, t0 // 128:t0 // 128 + 4, :, :], tt_.t[:, d_, :, :, :], tt_.ds, reads=tt_.b)
                            else:
                                S.dma("pool", dst_[s][d_, :, 32, :, 0:T], tt_.t[:, d_, 0, :, 0:T], tt_.ds, reads=tt_.b)
                S.barrier()
            if stop_after == "A":
                break

            with ExitStack() as esG:
                sbG = lambda name, shape, dt, nb=1, dsem=False: TT(S, esG, f"{name}_{s}", shape, dt, nb, dsem, semname=name)
                Sst = sbG("Sst", [128, 2, 4, 256], F32, nb=8)
                Sbf = sbG("Sbf", [128, 3, 4, 2, 256], BF16, nb=24)
                qdl = [sbG(f"qdl{i}", [128, 4, 128], BF16, dsem=True) for i in range(2)]
                kil = [sbG(f"kil{i}", [128, 4, 128], BF16, dsem=True) for i in range(2)]
                kel = [sbG(f"kel{i}", [128, 512], BF16, dsem=True) for i in range(2)]
                keh = [sbG(f"keh{i}", [128, 512], BF16, dsem=True) for i in range(2)]
                vl = [sbG(f"vl{i}", [128, 1024], BF16, dsem=True) for i in range(2)]
                Am = [sbG(f"Am{i}", [128, 4, 128], BF16, nb=4) for i in range(2)]
                ost = [sbG(f"ost{i}", [128, 8, 128], F32, dsem=True) for i in range(2)]
                ofl = [sbG(f"ofl{i}", [128, 8, 128], F32, dsem=True) for i in range(2)]
                cgen = None
                if s == 0 and deferred:
                    ost_g = [sbG(f"costg{i}", [128, 8448], BF16, dsem=True) for i in range(2)]
                    fx = xin.t[:].rearrange("p u d -> p (u d)")
                    fh = hT.t[:].rearrange("p c t -> p (c t)")
                    stg_g = [View(f_[:, i * 1024:(i + 1) * 1024], [Buf(f"cs{k}{i}")], S.dsem(f"cs{k}{i}"))
                             for k, f_ in enumerate((fx, fh)) for i in range(4)]
                    cgen = conv_gen(deferred, stg_g, ost_g, ("pool", "act"), "pool")

                def conv_step(k):
                    nonlocal_c = cgen
                    if nonlocal_c is None:
                        return
                    for _ in range(k):
                        if next(nonlocal_c, "done") == "done":
                            break
                for i in range(2):
                    S.op("dve", lambda e: e.memset(kel[i].t[:], 0.0), writes=kel[i].b)
                    S.op("dve", lambda e: e.memset(keh[i].t[:], 0.0), writes=keh[i].b)
                for d_ in range(2):
                    order = SUBS if d_ == 0 else SUBS[::-1]
                    mk = maskf if d_ == 0 else maskb
                    S.op("dve", lambda e: e.memset(Sst.t[:], 0.0), writes=Sst.b)
                    scur = [0, 0, 0, 0]
                    for h in range(4):
                        S.op("dve", lambda e: e.memset(Sbf.t[:, 0, h, 0, :], 0.0), writes=[Sbf.b[h * 2]])
                    def g_loads(it):
                        t0, n = order[it]
                        par = it % 2
                        S.dma("sp", qdl[par].t[:, :, 0:n], QD[s][d_, :, t0 // 128, :, 0:n], qdl[par].ds, writes=qdl[par].b)
                        S.dma("sp", kil[par].t[:, :, 0:n], KI[s][d_, :, t0 // 128, :, 0:n], kil[par].ds, writes=kil[par].b)
                        if n == 128:
                            S.dma("sp", kel[par].t[0:64, :], KE[s][d_, t0:t0 + 64, :], kel[par].ds, writes=kel[par].b)
                            S.dma("sp", keh[par].t[64:128, :], KE[s][d_, t0 + 64:t0 + 128, :], keh[par].ds, writes=keh[par].b)
                        else:
                            S.dma("sp", kel[par].t[0:n, :], KE[s][d_, t0:t0 + n, :], kel[par].ds, writes=kel[par].b)
                        S.dma("sp", vl[par].t[0:n, :], VV[s][t0:t0 + n, :], vl[par].ds, writes=vl[par].b)
                        if d_ == 1:
                            S.dma("sp", ofl[par].t[:, :, 0:n], OO[s][0, :, t0 // 128, :, 0:n], ofl[par].ds, writes=ofl[par].b)

                    gst = {}

                    def g_ctx(it):
                        t0, n = order[it]
                        par = it % 2
                        sbi = lambda p_, h_, k_: p_ * 8 + h_ * 2 + k_
                        p3 = it % 3
                        if n == 128:
                            chunks = [(0, 64, kel[par], t0 // 64), (64, 64, keh[par], t0 // 64 + 1)]
                            if d_ == 1:
                                chunks = chunks[::-1]
                            Kc = 128
                        else:
                            chunks = [(0, n, kel[par], 64)]
                            Kc = n
                        return t0, n, par, p3, sbi, chunks, Kc

                    def g_stage1(it):
                        t0, n, par, p3, sbi, chunks, Kc = g_ctx(it)
                        pkvs, pas = [], []
                        for h in range(4):
                            pkv, pkvb = psum()
                            for k, (co, cl, ket, ci) in enumerate(chunks):
                                S.op("pe", lambda e: e.matmul(pkv[:, k * 256:(k + 1) * 256], ket.t[0:Kc, h * 128:(h + 1) * 128],
                                                              vl[par].t[0:Kc, h * 256:(h + 1) * 256], start=True, stop=True),
                                     reads=[ket.b[0], vl[par].b[0]], writes=[pkvb])
                            pkvs.append((pkv, pkvb))
                        pa, pab = psum()
                        for h in range(4):
                            S.op("pe", lambda e: e.matmul(pa[0:n, h * 128:h * 128 + n], kil[par].t[:, h, 0:n], qdl[par].t[:, h, 0:n], start=True, stop=True),
                                 reads=[kil[par].b[0], qdl[par].b[0]], writes=[pab], inc=(h == 3))
                        for k, (co, cl, ket, ci) in enumerate(chunks):
                            for h in range(4):
                                pkv, pkvb = pkvs[h]
                                a_, b_ = scur[h], 1 - scur[h]
                                scur[h] = b_
                                S.op("dve", lambda e: e.scalar_tensor_tensor(out=Sst.t[:, b_, h, :], in0=Sst.t[:, a_, h, :],
                                                                             scalar=dec.t[:, d_, h, ci:ci + 1],
                                                                             in1=pkv[:, k * 256:(k + 1) * 256],
                                                                             op0=ALU.mult, op1=ALU.add),
                                     reads=[Sst.b[a_ * 4 + h], dec.b[0], pkvb], writes=[Sst.b[b_ * 4 + h]])
                                if k < len(chunks) - 1:
                                    tp, tk = p3, 1
                                else:
                                    tp, tk = (p3 + 1) % 3, 0
                                S.op("act", lambda e: e.copy(Sbf.t[:, tp, h, tk, :], Sst.t[:, b_, h, :]),
                                     reads=[Sst.b[b_ * 4 + h]], writes=[Sbf.b[sbi(tp, h, tk)]])
                            if k == 0:
                                for h in range(4):
                                    S.op("dve", lambda e: e.tensor_tensor(out=Am[par].t[0:n, h, 0:n], in0=pa[0:n, h * 128:h * 128 + n], in1=mk.t[0:n, 0:n], op=ALU.mult),
                                         reads=[pab, mk.b[0]], writes=[Am[par].b[h]])

                    def g_stage2(it):
                        t0, n, par, p3, sbi, chunks, Kc = g_ctx(it)
                        for hb in range(2):
                            po, pob = psum()
                            for h in (2 * hb, 2 * hb + 1):
                                for dvc in range(2):
                                    c0 = ((h % 2) * 2 + dvc) * 128
                                    S.op("pe", lambda e: e.matmul(po[:, c0:c0 + n], vl[par].t[0:n, h * 256 + dvc * 128:h * 256 + (dvc + 1) * 128],
                                                                  Am[par].t[0:n, h, 0:n], start=True, stop=False),
                                         reads=[vl[par].b[0], Am[par].b[h]], writes=[pob], inc=False)
                                    for k, (co, cl, ket, ci) in enumerate(chunks):
                                        last = (k == len(chunks) - 1)
                                        S.op("pe", lambda e: e.matmul(po[:, c0 + co:c0 + co + cl], Sbf.t[:, p3, h, k, dvc * 128:(dvc + 1) * 128],
                                                                      qdl[par].t[:, h, co:co + cl], start=False, stop=last),
                                             reads=[Sbf.b[sbi(p3, h, k)], qdl[par].b[0]], writes=[pob], inc=last)
                            for h in (2 * hb, 2 * hb + 1):
                                for dvc in range(2):
                                    c0 = ((h % 2) * 2 + dvc) * 128
                                    if d_ == 0:
                                        evac(ost[par].t[:, h * 2 + dvc, 0:n], po[:, c0:c0 + n], [pob], ost[par].b)
                                    else:
                                        S.op("dve", lambda e: e.tensor_tensor(out=ost[par].t[:, h * 2 + dvc, 0:n], in0=po[:, c0:c0 + n],
                                                                              in1=ofl[par].t[:, h * 2 + dvc, 0:n], op=ALU.add),
                                             reads=[pob, ofl[par].b[0]], writes=ost[par].b)
                        S.dma("pool", OO[s][d_, :, t0 // 128, :, 0:n], ost[par].t[:, :, 0:n], ost[par].ds, reads=ost[par].b)
                        if it + 2 < len(order):
                            g_loads(it + 2)

                    g_loads(0)
                    if len(order) > 1:
                        g_loads(1)
                    g_stage1(0)
                    for it in range(len(order)):
                        if it + 1 < len(order):
                            g_stage1(it + 1)
                        g_stage2(it)
                        conv_step(7)
                    if d_ == 1:
                        conv_step(10 ** 6)
                    S.barrier()
                S.barrier()
            if stop_after == "G":
                break

            esKV = ExitStack()
            KT = TT(S, esKV, f"KT_{s}", [128, 2, NT], BF16, 2)
            VR = TT(S, esKV, f"VR_{s}", [128, 33, 256], BF16, 1)
            with ExitStack() as esB:
                sbB = lambda name, shape, dt, nb=1, dsem=False: TT(S, esB, f"{name}_{s}", shape, dt, nb, dsem, semname=name)
                srt = sbB("srt", [128, 8, 512], F32, nb=8, dsem=True)
                oft = xin
                qkt = []
                for i in range(3):
                    xs_ = sbB(f"xs{i}", [128, 512], F32)
                    ms_ = sbB(f"msq{i}", [128, 512], F32)
                    qkt.append(dict(xs=xs_, r2=xs_, xn=sbB(f"xn{i}", [128, 512], F32), r1=sbB(f"r1{i}", [128, 512], F32),
                                    sqh=sbB(f"sqh{i}", [128, 512], BF16), ms=ms_, rstd=ms_))
                qki = [0]
                Ct = sbB("Ct", [128, 512], F32, dsem=True); St = sbB("St", [128, 512], F32, dsem=True)
                q2st = sbB("q2st", [128, 8, 512], BF16, nb=8, dsem=True)
                def load_os(ti_):
                    t0_, T_ = TILES[ti_]
                    if T_ >= 128:
                        S.dma("sp", xin.t[:, :, :], OO[s][1, :, t0_ // 128:t0_ // 128 + 4, :, :].rearrange("p u c j -> p u (c j)"), xin.ds, writes=xin.b)
                    else:
                        S.dma("sp", xin.t[:, 0, :].rearrange("p (c j) -> p c j", j=128)[:, :, 0:T_], OO[s][1, :, 32, :, 0:T_], xin.ds, writes=xin.b)
                    S.dma("sp", srt.t[:, :, :T_], SR[s].rearrange("(c p) t -> p c t", p=128)[:, :, t0_:t0_ + T_], srt.ds, writes=srt.b)

                for ti, (t0, T) in enumerate(TILES):
                    subs = subs_of(T)
                    if ti == 0:
                        load_os(0)
                    load_hT(T, t0, H1[s])
                    S.dma("sp", Ct.t[:, :T], c_cos[:, t0:t0 + T], Ct.ds, writes=Ct.b)
                    S.dma("sp", St.t[:, :T], c_sin[:, t0:t0 + T], St.ds, writes=St.b)
                    wid = min(128, T)
                    nsb = len(subs)
                    ofc = lambda c: xin.t[:, 0:nsb, c * 128:c * 128 + wid]
                    u3 = lambda ap: ap.rearrange("p (u j) -> p u j", j=wid)
                    for h in range(4):
                        ps, pb = psum()
                        for dvc in range(2):
                            c = 2 * h + dvc
                            S.op("act", lambda e: e.activation(out=u3(act.t[:, c, :T]), in_=ofc(c), func=AF.Square),
                                 reads=[xin.b[c]], writes=[act.b[c]])
                            S.op("pe", lambda e: e.matmul(ps[:, :T], ones_b.t[:], act.t[:, c, :T], start=(dvc == 0), stop=(dvc == 1)),
                                 reads=[ones_b.b[0], act.b[c]], writes=[pb], inc=(dvc == 1))
                        S.op("act", lambda e: e.activation(out=ms.t[:, :T], in_=ps[:, :T], func=AF.Ln, scale=1.0 / 256, bias=EPS), reads=[pb], writes=ms.b)
                        S.op("act", lambda e: e.activation(out=rstd.t[:, :T], in_=ms.t[:, :T], func=AF.Exp, scale=-0.5), reads=ms.b, writes=rstd.b)
                        for dvc in range(2):
                            c = 2 * h + dvc
                            S.op("dve", lambda e: e.scalar_tensor_tensor(out=ofc(c), in0=ofc(c),
                                                                         scalar=hncol.t[:, dvc:dvc + 1], in1=u3(rstd.t[:, :T]),
                                                                         op0=ALU.mult, op1=ALU.mult),
                                 reads=[xin.b[c], hncol.b[0], rstd.b[0]], writes=[xin.b[c]])
                            S.op("dve", lambda e: e.tensor_tensor(out=u3(hn.t[:, c, :T]), in0=ofc(c), in1=u3(srt.t[:, c, :T]), op=ALU.mult),
                                 reads=[xin.b[c], srt.b[c]], writes=[hn.b[c]])
                    if ti + 1 < len(TILES):
                        load_os(ti + 1)
                    for j in range(2):
                        sl = wget(("m512", "gla_w_out", j))
                        proj_fm(T, sl, 4, 0, lambda i, ps, pb: S.op(
                            "dve", lambda e: e.tensor_tensor(out=hT.t[:, j * 4 + i, :T], in0=ps[:, :T], in1=hT.t[:, j * 4 + i, :T], op=ALU.add),
                            reads=[pb, hT.b[j * 4 + i]], writes=[hT.b[j * 4 + i]]))
                    ffn(T, 0, 2)
                    ffn(T, 1, 1)
                    store_hT(T, t0, H4[s])
                    rmsnorm(T, 4)

                    pend = []

                    def qk_stage0(ps, pb, gi, dst_ap, dst_b):
                        q_ = qkt[qki[0] % 3]
                        qki[0] += 1
                        S.op("act", lambda e: e.copy(q_["xs"].t[:, :T], ps[:, :T]), reads=[pb], writes=q_["xs"].b)
                        S.op("act", lambda e: e.activation(out=q_["sqh"].t[:, :T], in_=ps[:, :T], func=AF.Square), reads=[pb], writes=q_["sqh"].b)
                        pend.append([q_, gi, dst_ap, dst_b, 0])
                        qk_advance(2)

                    def qk_stage1(it_):
                        q_, gi = it_[0], it_[1]
                        xs, xn, sqh, ms_, rs_ = q_["xs"], q_["xn"], q_["sqh"], q_["ms"], q_["rstd"]
                        p2, p2b = psum()
                        S.op("pe", lambda e: e.matmul(p2[:, :T], ones_b.t[:], sqh.t[:, :T], start=True, stop=True),
                             reads=[ones_b.b[0], sqh.b[0]], writes=[p2b])
                        S.op("act", lambda e: e.activation(out=ms_.t[:, :T], in_=p2[:, :T], func=AF.Ln, scale=1.0 / 128, bias=EPS), reads=[p2b], writes=ms_.b)
                        S.op("act", lambda e: e.activation(out=rs_.t[:, :T], in_=ms_.t[:, :T], func=AF.Exp, scale=-0.5), reads=ms_.b, writes=rs_.b)
                        S.op("dve", lambda e: e.scalar_tensor_tensor(out=xn.t[:, :T], in0=xs.t[:, :T], scalar=qkcol.t[:, gi:gi + 1],
                                                                     in1=rs_.t[:, :T], op0=ALU.mult, op1=ALU.mult),
                             reads=[xs.b[0], qkcol.b[0], rs_.b[0]], writes=xn.b)

                    def qk_stage2(it_):
                        q_, gi, dst_ap, dst_b = it_[0], it_[1], it_[2], it_[3]
                        xn, r1, r2 = q_["xn"], q_["r1"], q_["r2"]
                        p3, p3b = psum()
                        S.op("pe", lambda e: e.matmul(p3[:, :T], pmt.t[:], xn.t[:, :T], start=True, stop=True),
                             reads=[pmt.b[0], xn.b[0]], writes=[p3b])
                        S.op("pool", lambda e: e.tensor_tensor(out=r1.t[:, :T], in0=xn.t[:, :T], in1=Ct.t[:, :T], op=ALU.mult),
                             reads=[xn.b[0], Ct.b[0]], writes=r1.b)
                        S.op("dve", lambda e: e.tensor_tensor(out=r2.t[:, :T], in0=p3[:, :T], in1=St.t[:, :T], op=ALU.mult),
                             reads=[p3b, St.b[0]], writes=r2.b)
                        S.op("dve", lambda e: e.tensor_tensor(out=dst_ap, in0=r1.t[:, :T], in1=r2.t[:, :T], op=ALU.add),
                             reads=[r1.b[0], r2.b[0]], writes=dst_b)

                    def qk_advance(lag):
                        n_ = len(pend)
                        for k_, it_ in enumerate(pend):
                            age = n_ - 1 - k_
                            if it_[4] == 0 and age >= lag - 1:
                                qk_stage1(it_)
                                it_[4] = 1
                            elif it_[4] == 1 and age >= lag:
                                qk_stage2(it_)
                                it_[4] = 2
                        while pend and pend[0][4] == 2:
                            pend.pop(0)

                    def qk_flush():
                        while pend:
                            for it_ in pend:
                                if it_[4] == 0:
                                    qk_stage1(it_)
                                    it_[4] = 1
                                elif it_[4] == 1:
                                    qk_stage2(it_)
                                    it_[4] = 2
                            while pend and pend[0][4] == 2:
                                pend.pop(0)

                    for j in range(2):
                        sl = wget(("m512", "attn_w_in", j))
                        proj_fm(T, sl, 4, 0, lambda i, ps, pb: qk_stage0(ps, pb, 0, q2st.t[:, j * 4 + i, :T], [q2st.b[j * 4 + i]]))
                    sl = wget(("m512", "attn_w_in", 2))
                    proj_fm(T, sl, 2, 0, lambda i, ps, pb: qk_stage0(ps, pb, 1, KT.t[:, i, t0:t0 + T], [KT.b[i]]))
                    v = sl.t[:].rearrange("p (k c) -> p k c", k=8)
                    for u, (o, n) in enumerate(subs):
                        ps, pb = psum()
                        for kc in range(8):
                            S.op("pe", lambda e: e.matmul(ps[0:n, 0:256], hn.t[:, kc, o:o + n], v[:, kc, 256:512],
                                                          start=(kc == 0), stop=(kc == 7)),
                                 reads=[sl.b[0], hn.b[kc]], writes=[pb], inc=(kc == 7))
                        evac(VR.t[0:n, (t0 + o) // 128, :], ps[0:n, 0:256], [pb], VR.b)
                        if u == 0:
                            qk_advance(1)
                    qk_flush()
                    S.dma("pool", Q2[s].rearrange("(c p) t -> p c t", p=128)[:, :, t0:t0 + T], q2st.t[:, :, :T], q2st.ds, reads=q2st.b)
                S.barrier()
            if stop_after == "B":
                esKV.close()
                break

            with ExitStack() as esD:
                sbD = lambda name, shape, dt, nb=1, dsem=False: TT(S, esD, f"{name}_{s}", shape, dt, nb, dsem, semname=name)
                qts = [sbD(f"qt{i}", [128, 8, 512], BF16, dsem=True) for i in range(2)]
                pT = [sbD(f"pT{i}", [128, 512], BF16) for i in range(6)]
                rden = [sbD(f"rden{i}", [128, 512], F32) for i in range(2)]
                dacc = [[sbD(f"dacc{i}{j}", [128, 512], F32) for j in range(3)] for i in range(2)]
                yf = sbD("yf", [128, 8, 512], F32, nb=8)
                pst[1] = 4
                pti = 0
                def load_q(ti_):
                    t0_, T_ = TILES[1:][ti_]
                    q_ = qts[ti_ % 2]
                    S.dma("sp", q_.t[:, :, :T_], Q2[s].rearrange("(c p) t -> p c t", p=128)[:, :, t0_:t0_ + T_], q_.ds, writes=q_.b)

                load_q(0)
                for ti, (t0, T) in enumerate(TILES[1:]):
                    subs = subs_of(T)
                    qt = qts[ti % 2]
                    if ti + 1 < len(TILES) - 1:
                        load_q(ti + 1)
                    load_hT(T, t0, H4[s])
                    steps = [(head, blk) for head in range(8) for blk in range(33)]
                    LOOK = 3
                    sps = {}

                    def emit_s(i):
                        head, blk = steps[i]
                        kvh = head // 4
                        K_ = 128 if blk < 32 else NMETA
                        ps, pb = psum()
                        S.op("pe", lambda e: e.matmul(ps[0:K_, :T], KT.t[:, kvh, blk * 128:blk * 128 + K_], qt.t[:, head, :T],
                                                      start=True, stop=True), reads=[KT.b[kvh], qt.b[0]], writes=[pb])
                        sps[i] = (ps, pb)

                    for i in range(LOOK):
                        emit_s(i)
                    for i, (head, blk) in enumerate(steps):
                        kvh = head // 4
                        K_ = 128 if blk < 32 else NMETA
                        ps, pb = sps.pop(i)
                        a0 = 4 + 2 * (head % 2)
                        po, pob, pd, pdb = psb[a0], psB[a0], psb[a0 + 1], psB[a0 + 1]
                        p_ = pT[pti % 6]
                        pti += 1
                        S.op("act", lambda e: e.activation(out=p_.t[0:K_, :T], in_=ps[0:K_, :T], func=AF.Exp),
                             reads=[pb], writes=p_.b)
                        S.op("pe", lambda e: e.matmul(po[:, :T], VR.t[0:K_, blk, kvh * 128:(kvh + 1) * 128], p_.t[0:K_, :T],
                                                      start=(blk == 0), stop=(blk == 32)),
                             reads=[VR.b[0], p_.b[0]], writes=[pob], inc=True)
                        ai = blk % 3
                        if ai == 2:
                            S.op("pe", lambda e: e.matmul(pd[:, :T], ones_b.t[0:K_, :], p_.t[0:K_, :T], start=(blk == 2), stop=False),
                                 reads=[ones_b.b[0], p_.b[0]], writes=[pdb], inc=True)
                        else:
                            acc = dacc[head % 2][ai]
                            if blk < 2:
                                S.op("dve", lambda e: e.tensor_copy(acc.t[:, :T], p_.t[:, :T]), reads=p_.b, writes=acc.b)
                            else:
                                S.op("dve", lambda e: e.tensor_tensor(out=acc.t[0:K_, :T], in0=acc.t[0:K_, :T], in1=p_.t[0:K_, :T], op=ALU.add),
                                     reads=[acc.b[0], p_.b[0]], writes=acc.b)
                        if i + LOOK < len(steps):
                            emit_s(i + LOOK)
                        if blk == 32:
                            a_ = dacc[head % 2]
                            S.op("dve", lambda e: e.tensor_tensor(out=a_[0].t[:, :T], in0=a_[0].t[:, :T], in1=a_[1].t[:, :T], op=ALU.add),
                                 reads=[a_[0].b[0], a_[1].b[0]], writes=a_[0].b)
                            S.op("pe", lambda e: e.matmul(pd[:, :T], ones_f.t[:], a_[0].t[:, :T], start=False, stop=True),
                                 reads=[ones_f.b[0], a_[0].b[0]], writes=[pdb])
                            rd = rden[head % 2]
                            S.op("act", lambda e: e.activation(out=rd.t[:, :T], in_=pd[:, :T], func=AF.Ln), reads=[pdb], writes=rd.b)
                            S.op("act", lambda e: e.activation(out=rd.t[:, :T], in_=rd.t[:, :T], func=AF.Exp, scale=-1.0), reads=rd.b, writes=rd.b)
                            S.op("dve", lambda e: e.tensor_tensor(out=hn.t[:, head, :T], in0=po[:, :T], in1=rd.t[:, :T], op=ALU.mult),
                                 reads=[pob, rd.b[0]], writes=[hn.b[head]])
                    for j in range(2):
                        sl = wget(("m512", "attn_w_out", j))
                        proj_fm(T, sl, 4, 0, lambda i, ps, pb: S.op(
                            "dve", lambda e: e.tensor_tensor(out=hT.t[:, j * 4 + i, :T], in0=ps[:, :T], in1=hT.t[:, j * 4 + i, :T], op=ALU.add),
                            reads=[pb, hT.b[j * 4 + i]], writes=[hT.b[j * 4 + i]]))
                    ffn(T, 1, 2)
                    for c in range(8):
                        S.op("act", lambda e: e.activation(out=act.t[:, c, :T], in_=hT.t[:, c, :T], func=AF.Square),
                             reads=[hT.b[c]], writes=[act.b[c]])
                    ps, pb = psum()
                    for c in range(8):
                        S.op("pe", lambda e: e.matmul(ps[:, :T], ones_b.t[:], act.t[:, c, :T], start=(c == 0), stop=(c == 7)),
                             reads=[ones_b.b[0], act.b[c]], writes=[pb], inc=(c == 7))
                    S.op("act", lambda e: e.activation(out=ms.t[:, :T], in_=ps[:, :T], func=AF.Ln, scale=1.0 / D, bias=EPS), reads=[pb], writes=ms.b)
                    S.op("act", lambda e: e.activation(out=rstd.t[:, :T], in_=ms.t[:, :T], func=AF.Exp, scale=-0.5), reads=ms.b, writes=rstd.b)
                    for c in range(8):
                        S.op("dve", lambda e: e.scalar_tensor_tensor(out=yf.t[:, c, :T], in0=hT.t[:, c, :T], scalar=gcol.t[:, 6, c:c + 1],
                                                                     in1=rstd.t[:, :T], op0=ALU.mult, op1=ALU.mult),
                             reads=[hT.b[c], gcol.b[0], rstd.b[0]], writes=[yf.b[c]])
                    for u, (o, n) in enumerate(subs):
                        for cg in range(2):
                            ps, pb = psum()
                            for c4 in range(4):
                                c = cg * 4 + c4
                                S.op("pe", lambda e: e.transpose(ps[0:n, c4 * 128:(c4 + 1) * 128], yf.t[:, c, o:o + n], ident_f.t[:, :]),
                                     reads=[yf.b[c], ident_f.b[0]], writes=[pb], inc=(c4 == 3))
                            evac(xin.t[0:n, u, cg * 512:(cg + 1) * 512], ps[0:n, :], [pb], xin.b)
                    S.dma("pool", out[s, t0:t0 + T, :].rearrange("(u p) d -> p u d", p=128), xin.t[:, :, :], xst_ds, reads=xin.b)
                pst[1] = 8
                S.barrier()
            esKV.close()

        S.barrier()
        print("instructions:", S.n_ins, "weights consumed", wst["next"], "/", len(wkeys))
    return nc


def make_consts():
    c = {}
    c["c_ident"] = np.eye(128, dtype=np.float32)
    s_ = np.arange(128)[:, None]
    t_ = np.arange(128)[None, :]
    same = (s_ // 64) == (t_ // 64)
    c["c_maskf"] = (same & (s_ <= t_)).astype(np.float32)
    c["c_maskb"] = (same & (s_ >= t_)).astype(np.float32)
    m = np.ones((128, 512), np.float32)
    m[:, ::64] = 0.0
    c["c_scan"] = m
    pm = np.zeros((128, 128), np.float32)
    for d in range(128):
        if d % 64 < 32:
            pm[d, d + 32] = -1.0
        else:
            pm[d, d - 32] = 1.0
    c["c_pmt"] = np.ascontiguousarray(pm.T)
    inv = (10000.0 ** (-np.arange(0, 64, 2, dtype=np.float32) / 64.0)).astype(np.float32)
    pos = np.arange(SEQ)
    trow = np.concatenate([pos // 64, np.zeros(NMETA, np.int64)]).astype(np.float32)
    tcol = np.concatenate([pos % 64, np.zeros(NMETA, np.int64)]).astype(np.float32)
    ang = np.zeros((128, NT), np.float32)
    for d in range(128):
        ang[d] = (trow if d < 64 else tcol) * inv[d % 32]
    c["c_cos"] = np.cos(ang).astype(np.float32)
    c["c_sin"] = np.sin(ang).astype(np.float32)
    return c


_NC_CACHE = {}


def kernel(**inputs):
    n_cores = 8
    n_seq = 2
    if "nc" not in _NC_CACHE:
        _NC_CACHE["nc"] = build(n_seq=n_seq)
    nc = _NC_CACHE["nc"]
    consts = make_consts()
    x = np.ascontiguousarray(np.asarray(inputs["x"], dtype=np.float32))
    shared = {k: np.ascontiguousarray(np.asarray(v, dtype=np.float32)) for k, v in inputs.items() if k != "x"}
    shared.update(consts)
    in_maps = []
    for c in range(n_cores):
        m = dict(shared)
        m["x"] = x[c * n_seq:(c + 1) * n_seq]
        in_maps.append(m)
    res = run_bass_kernel_spmd(nc, in_maps, core_ids=list(range(n_cores)))
    return np.concatenate([r["out"] for r in res.results], axis=0).astype(np.float32)
```

```python
import numpy as np
import concourse.bass as bass
import concourse.mybir as mybir
from concourse.bass_utils import run_bass_kernel_spmd
from contextlib import ExitStack

F32 = mybir.dt.float32
BF16 = mybir.dt.bfloat16
AF = mybir.ActivationFunctionType
ALU = mybir.AluOpType

D = 1024
DFF = 2816
SEQ = 4096
NMETA = 16
NT = SEQ + NMETA
EPS = 1e-6
NSLOT = 4
TILES = [(SEQ, NMETA)] + [(i * 512, 512) for i in range(8)]
SUBS = [(SEQ, NMETA)] + [(i * 128, 128) for i in range(32)]


class Buf:
    __slots__ = ("name", "w", "r")

    def __init__(self, name):
        self.name = name
        self.w = None
        self.r = []


class DSem:
    __slots__ = ("sem", "cnt")

    def __init__(self, sem):
        self.sem = sem
        self.cnt = 0


class Sched:
    def __init__(self, nc, es):
        self.nc = nc
        self.es = es
        self.eng = {"pe": nc.tensor, "act": nc.scalar, "dve": nc.vector,
                    "pool": nc.gpsimd, "sp": nc.sync}
        self.esem = {k: es.enter_context(nc.semaphore("e_" + k)) for k in self.eng}
        self.ecnt = {k: 0 for k in self.eng}
        self.seen = {k: {} for k in self.eng}
        self.dsems = []
        self.n_ins = 0

    def dsem(self, name):
        d = DSem(self.es.enter_context(self.nc.semaphore("d_" + name)))
        self.dsems.append(d)
        return d

    def _wait(self, e, deps):
        best = {}
        for (sem, val) in deps:
            k = sem.num
            if k not in best or best[k][1] < val:
                best[k] = (sem, val)
        for k, (sem, val) in best.items():
            if e == "pe" and k == self.esem["pe"].num:
                continue
            if self.seen[e].get(k, 0) < val:
                self.eng[e].wait_ge(sem, val)
                self.seen[e][k] = val
                self.n_ins += 1

    @staticmethod
    def _deps(reads, writes):
        deps = []
        for b in reads:
            if b.w is not None:
                deps.append(b.w)
        for b in writes:
            if b.w is not None:
                deps.append(b.w)
            deps.extend(b.r)
        return deps

    @staticmethod
    def _compact(ticks):
        best = {}
        for (sem, val) in ticks:
            k = sem.num
            if k not in best or best[k][1] < val:
                best[k] = (sem, val)
        return list(best.values())

    def _mark(self, tick, reads, writes):
        for b in reads:
            b.r.append(tick)
            if len(b.r) > 32:
                b.r = self._compact(b.r)
        for b in writes:
            b.w = tick
            b.r = []

    def op(self, e, fn, reads=(), writes=(), inc=True):
        self._wait(e, self._deps(reads, writes))
        ins = fn(self.eng[e])
        self.n_ins += 1
        if inc:
            self.ecnt[e] += 1
            ins.then_inc(self.esem[e], 1)
            tick = (self.esem[e], self.ecnt[e])
        else:
            tick = (self.esem[e], self.ecnt[e] + 1)
        self._mark(tick, reads, writes)
        return ins

    def acquire(self, e, reads=(), writes=()):
        self._wait(e, self._deps(reads, writes))

    def dma(self, q, out, in_, ds, reads=(), writes=()):
        self._wait(q, self._deps(reads, writes))
        ins = self.eng[q].dma_start(out=out, in_=in_)
        ds.cnt += 16
        ins.then_inc(ds.sem, 16)
        self.n_ins += 1
        self._mark((ds.sem, ds.cnt), reads, writes)
        return ins

    def barrier(self):
        for e in self.eng:
            deps = [(self.esem[k], self.ecnt[k]) for k in self.eng if self.ecnt[k] > 0]
            deps += [(d.sem, d.cnt) for d in self.dsems if d.cnt > 0]
            for (sem, val) in deps:
                if self.seen[e].get(sem.num, 0) < val:
                    self.eng[e].wait_ge(sem, val)
                    self.seen[e][sem.num] = val
                    self.n_ins += 1


class TT:
    _cache = {}

    def __init__(self, S, es, name, shape, dt, nb=1, dsem=False, semname=None):
        self.t = es.enter_context(S.nc.sbuf_tensor(name, shape, dt))
        self.b = [Buf(f"{name}{i}") for i in range(nb)]
        semname = semname or name
        nsem = 0 if not dsem else (nb if dsem == "per" else 1)
        self.dsl = []
        for i in range(nsem):
            key = (id(S), f"{semname}{i}")
            if key not in TT._cache:
                TT._cache[key] = S.dsem(f"{semname}{i}")
            self.dsl.append(TT._cache[key])
        self.ds = self.dsl[0] if self.dsl else None
        self.ps = 1
        for s in shape[1:]:
            self.ps *= s


def weight_keys(n_seq):
    ks = []

    def ffn(l, f):
        for j in range(11):
            ks.append(("gu", l, f, j))
        for m in range(8):
            ks.append(("dn", l, f, m))

    for s in range(n_seq):
        for _ in TILES:
            ffn(0, 1)
            for j in range(6):
                ks.append(("m512", "gla_w_in", j))
        for _ in TILES:
            for j in range(2):
                ks.append(("m512", "gla_w_out", j))
            ffn(0, 2)
            ffn(1, 1)
            for j in range(3):
                ks.append(("m512", "attn_w_in", j))
        for _ in TILES[1:]:
            for j in range(2):
                ks.append(("m512", "attn_w_out", j))
            ffn(1, 2)
    return ks


def build(n_seq=2, stop_after="D", dbg=False):
    nc = bass.Bass("TRN2", target_bir_lowering=False)

    def din(name, shape):
        return nc.dram_tensor(name, list(shape), F32, kind="ExternalInput").ap()

    x = din("x", [n_seq, SEQ, D])
    meta_tokens = din("meta_tokens", [NMETA, D])
    norm_ffn1 = din("norm_ffn1", [2, D]); norm_mix = din("norm_mix", [2, D]); norm_ffn2 = din("norm_ffn2", [2, D])
    norm_final = din("norm_final", [D])
    Wg = {1: din("ffn1_w_gate", [2, D, DFF]), 2: din("ffn2_w_gate", [2, D, DFF])}
    Wu = {1: din("ffn1_w_up", [2, D, DFF]), 2: din("ffn2_w_up", [2, D, DFF])}
    Wd = {1: din("ffn1_w_down", [2, DFF, D]), 2: din("ffn2_w_down", [2, DFF, D])}
    Wm = {"gla_w_in": din("gla_w_in", [1, D, 3072])[0], "gla_w_out": din("gla_w_out", [1, D, D])[0],
          "attn_w_in": din("attn_w_in", [1, D, 1536])[0], "attn_w_out": din("attn_w_out", [1, D, D])[0]}
    gate_w1 = din("gla_gate_w1", [1, 2, D, 16])[0]
    gate_w2 = din("gla_gate_w2", [1, 2, 16, 512])[0]
    gate_b = din("gla_gate_b", [1, 2, 512])[0]
    head_norm = din("gla_head_norm", [1, 256])[0]
    q_norm = din("attn_q_norm", [1, 128])[0]
    k_norm = din("attn_k_norm", [1, 128])[0]
    c_ident = din("c_ident", [128, 128]); c_maskf = din("c_maskf", [128, 128]); c_maskb = din("c_maskb", [128, 128])
    c_scan = din("c_scan", [128, 512]); c_pmt = din("c_pmt", [128, 128])
    c_cos = din("c_cos", [128, NT]); c_sin = din("c_sin", [128, NT])

    out = nc.dram_tensor("out", [n_seq, SEQ, D], F32, kind="ExternalOutput").ap()
    WGU = {(l, f): nc.dram_tensor(f"wgu_{l}_{f}", [11, 128, 4096], BF16, kind="Internal").ap() for l in range(2) for f in (1, 2)}
    WDN = {(l, f): nc.dram_tensor(f"wdn_{l}_{f}", [8, 128, 2816], BF16, kind="Internal").ap() for l in range(2) for f in (1, 2)}
    WMB = {k: nc.dram_tensor(f"wmb_{k}", [v.shape[1] // 512, 128, 4096], BF16, kind="Internal").ap() for k, v in Wm.items()}
    wjob = {}
    skind = "ExternalOutput" if dbg else "Internal"

    def scr(name, shape, dt):
        return [nc.dram_tensor(f"{name}{s}", list(shape), dt, kind=skind).ap() for s in range(n_seq)]

    H1 = scr("H1_", [D, NT], F32); QD = scr("QD_", [2, 128, 33, 4, 128], BF16); KI = scr("KI_", [2, 128, 33, 4, 128], BF16)
    KE = scr("KE_", [2, NT, 512], BF16); VV = scr("VV_", [NT, 1024], BF16); SR = scr("SR_", [D, NT], F32)
    OO = scr("OO_", [2, 128, 33, 8, 128], F32)
    Q2 = scr("Q2_", [D, NT], BF16); H4 = scr("H4_", [D, NT], F32)

    with ExitStack() as es, nc.allow_non_contiguous_dma(reason="small param vectors"):
        S = Sched(nc, es)
        sb = lambda name, shape, dt, nb=1, dsem=False: TT(S, es, name, shape, dt, nb, dsem)

        ident_f = sb("ident_f", [128, 128], F32, dsem=True)
        ident_b = sb("ident_b", [128, 128], BF16)
        ones_b = sb("ones_b", [128, 128], BF16)
        maskf = sb("maskf", [128, 128], F32, dsem=True); maskb = sb("maskb", [128, 128], F32, dsem=True)
        scanm = sb("scanm", [128, 512], F32, dsem=True)
        pmt = sb("pmt", [128, 128], F32, dsem=True)
        neghalf = sb("neghalf", [128, 512], F32)
        ones_f = sb("ones_f", [128, 128], F32)
        gcol = sb("gcol", [128, 7, 8], F32, dsem=True)
        hncol = sb("hncol", [128, 2], F32, dsem=True)
        qkcol = sb("qkcol", [128, 2], F32, dsem=True)
        ngb = sb("ngb", [128, 2, 4], F32, dsem=True)
        w1 = sb("w1", [128, 2, 8, 16], BF16, dsem=True)
        w2 = sb("w2", [16, 2, 512], BF16, dsem=True)
        dec = sb("dec", [128, 2, 4, 65], F32)
        cst = [ident_f.b[0], ident_b.b[0], ones_b.b[0], maskf.b[0], maskb.b[0], scanm.b[0], pmt.b[0],
               neghalf.b[0], gcol.b[0], hncol.b[0], qkcol.b[0], ngb.b[0], w1.b[0], w2.b[0]]

        S.dma("sp", ident_f.t[:], c_ident, ident_f.ds, writes=ident_f.b)
        S.dma("sp", maskf.t[:], c_maskf, maskf.ds, writes=maskf.b)
        S.dma("sp", maskb.t[:], c_maskb, maskb.ds, writes=maskb.b)
        S.dma("sp", scanm.t[:], c_scan, scanm.ds, writes=scanm.b)
        S.dma("sp", pmt.t[:], c_pmt, pmt.ds, writes=pmt.b)
        gl = [norm_ffn1[0], norm_mix[0], norm_ffn2[0], norm_ffn1[1], norm_mix[1], norm_ffn2[1], norm_final]
        for i, g in enumerate(gl):
            S.dma("sp", gcol.t[:, i, :], g.rearrange("(c p) -> p c", p=128), gcol.ds, writes=gcol.b)
        S.dma("sp", hncol.t[:], head_norm.rearrange("(c p) -> p c", p=128), hncol.ds, writes=hncol.b)
        S.dma("sp", qkcol.t[:, 0:1], q_norm.rearrange("(c p) -> p c", p=128), qkcol.ds, writes=qkcol.b)
        S.dma("sp", qkcol.t[:, 1:2], k_norm.rearrange("(c p) -> p c", p=128), qkcol.ds, writes=qkcol.b)
        for n in range(2):
            S.dma("sp", ngb.t[:, n, :], gate_b[n].rearrange("(h p) -> p h", p=128), ngb.ds, writes=ngb.b)
            S.dma("pool", w1.t[:, n], gate_w1[n].rearrange("(kc p) r -> p kc r", p=128), w1.ds, writes=w1.b)
            S.dma("pool", w2.t[:, n, :], gate_w2[n], w2.ds, writes=w2.b)
        S.op("dve", lambda e: e.tensor_copy(ident_b.t[:], ident_f.t[:]), reads=ident_f.b, writes=ident_b.b)
        S.op("dve", lambda e: e.memset(ones_b.t[:], 1.0), writes=ones_b.b)
        S.op("dve", lambda e: e.memset(neghalf.t[:], -0.5), writes=neghalf.b)
        S.op("dve", lambda e: e.memset(ones_f.t[:], 1.0), writes=ones_f.b)
        S.op("dve", lambda e: e.tensor_scalar(out=ngb.t[:], in0=ngb.t[:], scalar1=-1.0, scalar2=None, op0=ALU.mult),
             reads=ngb.b, writes=ngb.b)
        S.op("dve", lambda e: e.tensor_scalar(out=qkcol.t[:, 0:1], in0=qkcol.t[:, 0:1], scalar1=128.0 ** -0.5,
                                              scalar2=None, op0=ALU.mult), reads=qkcol.b, writes=qkcol.b)

        class View:
            def __init__(self, t, b, ds):
                self.t, self.b, self.ds = t, b, ds

        def make_jobs(spec, gu_n, dn_n, m_n):
            jobs = []
            for it_ in spec:
                if it_[0] == "gu":
                    _, l, f = it_
                    for j0 in range(0, 11, gu_n):
                        nj = min(gu_n, 11 - j0)
                        jobs.append(dict(
                            keys=[("gu", l, f, j) for j in range(j0, j0 + nj)], rows=[(g, kc) for g in range(2) for kc in range(8)], ncols=nj * 256,
                            src=lambda r, l=l, f=f, j0=j0, nj=nj: (Wg[f][l] if r[0] == 0 else Wu[f][l])[r[1] * 128:(r[1] + 1) * 128, j0 * 256:(j0 + nj) * 256],
                            outv=lambda o, r, nj=nj: o.t[:, 0:nj * 4096].rearrange("p (j e) -> p j e", e=4096)[:, :, r[0] * 2048 + r[1] * 256:r[0] * 2048 + (r[1] + 1) * 256],
                            inv=lambda st, nj=nj: st.t[:, 0:nj * 256].rearrange("p (j c) -> p j c", c=256),
                            dst=WGU[(l, f)][j0:j0 + nj].rearrange("j p e -> p j e"), esz=(nj, 4096)))
                elif it_[0] == "dn":
                    _, l, f = it_
                    for m0 in range(0, 8, dn_n):
                        nm = min(dn_n, 8 - m0)
                        jobs.append(dict(
                            keys=[("dn", l, f, m) for m in range(m0, m0 + nm)], rows=list(range(22)), ncols=nm * 128,
                            src=lambda r, l=l, f=f, m0=m0, nm=nm: Wd[f][l][r * 128:(r + 1) * 128, m0 * 128:(m0 + nm) * 128],
                            outv=lambda o, r, nm=nm: o.t[:, 0:nm * 2816].rearrange("p (m e) -> p m e", e=2816)[:, :, r * 128:(r + 1) * 128],
                            inv=lambda st, nm=nm: st.t[:, 0:nm * 128].rearrange("p (m c) -> p m c", c=128),
                            dst=WDN[(l, f)][m0:m0 + nm].rearrange("m p e -> p m e"), esz=(nm, 2816)))
                else:
                    _, name = it_
                    npc_all = Wm[name].shape[1] // 512
                    for j0 in range(0, npc_all, m_n):
                        nj = min(m_n, npc_all - j0)
                        jobs.append(dict(
                            keys=[("m512", name, j) for j in range(j0, j0 + nj)], rows=list(range(8)), ncols=nj * 512,
                            src=lambda r, name=name, j0=j0, nj=nj: Wm[name][r * 128:(r + 1) * 128, j0 * 512:(j0 + nj) * 512],
                            outv=lambda o, r, nj=nj: o.t[:, 0:nj * 4096].rearrange("p (j e) -> p j e", e=4096)[:, :, r * 512:(r + 1) * 512],
                            inv=lambda st, nj=nj: st.t[:, 0:nj * 512].rearrange("p (j c) -> p j c", c=512),
                            dst=WMB[name][j0:j0 + nj].rearrange("j p e -> p j e"), esz=(nj, 4096)))
            return jobs

        def conv_gen(jobs, stages, outs, engs, store_q):
            items = [(ji, r) for ji, job in enumerate(jobs) for r in job["rows"]]
            ns = len(stages)
            ce = [0]

            def load(i):
                ji, r = items[i]
                st = stages[i % ns]
                S.dma("sp", st.t[:, 0:jobs[ji]["ncols"]], jobs[ji]["src"](r), st.ds, writes=st.b)

            for i in range(min(ns, len(items))):
                load(i)
            for i, (ji, r) in enumerate(items):
                job = jobs[ji]
                st = stages[i % ns]
                o = outs[ji % len(outs)]
                e = engs[ce[0] % len(engs)]
                ce[0] += 1
                if e == "act":
                    S.op("act", lambda en: en.copy(job["outv"](o, r), job["inv"](st)), reads=st.b, writes=o.b)
                else:
                    S.op(e, lambda en: en.tensor_copy(job["outv"](o, r), job["inv"](st)), reads=st.b, writes=o.b)
                if i + ns < len(items):
                    load(i + ns)
                if r == job["rows"][-1]:
                    jb = Buf("job")
                    nj, e_ = job["esz"]
                    S.dma(store_q, job["dst"], o.t[:, 0:nj * e_].rearrange("p (j e) -> p j e", e=e_), o.ds, reads=o.b, writes=[jb])
                    for k in job["keys"]:
                        wjob[k] = jb
                yield

        with ExitStack() as esP:
            stg = [TT(S, esP, f"cstg{i}", [128, 3072], F32, 1, True) for i in range(4)]
            ost_p = [TT(S, esP, f"cost{i}", [128, 24576], BF16, 1, True) for i in range(2)]
            for _ in conv_gen(make_jobs([("gu", 0, 1), ("dn", 0, 1), ("m", "gla_w_in"), ("m", "gla_w_out"), ("gu", 0, 2), ("dn", 0, 2),
                                         ("gu", 1, 1), ("dn", 1, 1), ("m", "attn_w_in"), ("m", "attn_w_out"), ("gu", 1, 2), ("dn", 1, 2)], 6, 8, 6),
                              stg, ost_p, ("act", "dve", "pool"), "pool"):
                pass
            S.barrier()
        deferred = []

        psb = [es.enter_context(nc.psum_tensor(f"ps{i}", [128, 512], F32)) for i in range(8)]
        psB = [Buf(f"ps{i}") for i in range(8)]
        pst = [0, 8]

        def psum():
            i = pst[0] % pst[1]
            pst[0] += 1
            return psb[i], psB[i]

        def psum_peek(n):
            return [psB[(pst[0] + k) % pst[1]] for k in range(min(n, pst[1]))]

        slots = [sb(f"wslot{i}", [128, 4096], BF16, dsem=True) for i in range(NSLOT)]
        wkeys = weight_keys(n_seq)
        wst = {"issued": 0, "next": 0}

        def issue_piece(i):
            key = wkeys[i]
            sl = slots[i % NSLOT]
            if key[0] == "gu":
                _, l, f, j = key
                S.dma("sp", sl.t[:, :], WGU[(l, f)][j], sl.ds, reads=[wjob[key]], writes=sl.b)
            elif key[0] == "dn":
                _, l, f, m = key
                S.dma("sp", sl.t[:, 0:2816], WDN[(l, f)][m], sl.ds, reads=[wjob[key]], writes=sl.b)
            else:
                _, name, j = key
                S.dma("sp", sl.t[:, :], WMB[name][j], sl.ds, reads=[wjob[key]], writes=sl.b)

        def wget(key):
            i = wst["next"]
            assert wkeys[i] == key, (i, wkeys[i], key)
            wst["next"] += 1
            while wst["issued"] < min(len(wkeys), i + NSLOT):
                if wkeys[wst["issued"]] not in wjob:
                    assert wst["issued"] > i
                    break
                issue_piece(wst["issued"])
                wst["issued"] += 1
            return slots[i % NSLOT]

        hT = sb("hT", [128, 8, 512], F32, nb=8)
        hn = sb("hn", [128, 8, 512], BF16, nb=8)
        act = sb("act", [128, 22, 512], BF16, nb=22)
        xin = sb("xin", [128, 4, 1024], F32, nb=8, dsem=True)
        ms = sb("ms", [128, 512], F32); rstd = sb("rstd", [128, 512], F32)
        sg = [sb(f"sg{i}", [128, 512], F32) for i in range(2)]
        hst_ds = S.dsem("hTst")
        hld_ds = S.dsem("hTld")
        xst_ds = S.dsem("xinst")
        rr = {"sg": 0, "cp": 0}

        def evac(out_ap, in_ap, reads, writes):
            rr["cp"] += 1
            if rr["cp"] % 2:
                S.op("act", lambda e: e.copy(out_ap, in_ap), reads=reads, writes=writes)
            else:
                S.op("dve", lambda e: e.tensor_copy(out_ap, in_ap), reads=reads, writes=writes)

        def rmsnorm(T, gi):
            for c in range(8):
                S.op("act", lambda e: e.activation(out=act.t[:, c, :T], in_=hT.t[:, c, :T], func=AF.Square),
                     reads=[hT.b[c]], writes=[act.b[c]])
            ps, pb = psum()
            for c in range(8):
                S.op("pe", lambda e: e.matmul(ps[:, :T], ones_b.t[:], act.t[:, c, :T], start=(c == 0), stop=(c == 7)),
                     reads=[ones_b.b[0], act.b[c]], writes=[pb], inc=(c == 7))
            S.op("act", lambda e: e.activation(out=ms.t[:, :T], in_=ps[:, :T], func=AF.Ln, scale=1.0 / D, bias=EPS), reads=[pb], writes=ms.b)
            S.op("act", lambda e: e.activation(out=rstd.t[:, :T], in_=ms.t[:, :T], func=AF.Exp, scale=-0.5), reads=ms.b, writes=rstd.b)
            for c in range(8):
                S.op("dve", lambda e: e.scalar_tensor_tensor(out=hn.t[:, c, :T], in0=hT.t[:, c, :T],
                                                             scalar=gcol.t[:, gi, c:c + 1], in1=rstd.t[:, :T],
                                                             op0=ALU.mult, op1=ALU.mult),
                     reads=[hT.b[c], gcol.b[0], rstd.b[0]], writes=[hn.b[c]])

        def ffn(T, l, f):
            rmsnorm(T, 3 * l + (0 if f == 1 else 2))
            for j in range(11):
                sl = wget(("gu", l, f, j))
                v = sl.t[:].rearrange("p (g k c) -> p g k c", g=2, k=8)
                S.acquire("pe", reads=sl.b, writes=psum_peek(4))
                for half in range(2):
                    fc = 2 * j + half
                    pg, pgb = psum()
                    pu, pub = psum()
                    for g, (pp, ppb) in enumerate(((pg, pgb), (pu, pub))):
                        for kc in range(8):
                            S.op("pe", lambda e: e.matmul(pp[:, :T], v[:, g, kc, half * 128:(half + 1) * 128],
                                                          hn.t[:, kc, :T], start=(kc == 0), stop=(kc == 7)),
                                 reads=[sl.b[0], hn.b[kc]], writes=[ppb], inc=(kc == 7))
                    s_ = sg[rr["sg"] % 2]
                    rr["sg"] += 1
                    S.op("act", lambda e: e.activation(out=s_.t[:, :T], in_=pg[:, :T], func=AF.Silu),
                         reads=[pgb], writes=s_.b)
                    S.op("dve", lambda e: e.tensor_tensor(out=act.t[:, fc, :T], in0=s_.t[:, :T], in1=pu[:, :T], op=ALU.mult),
                         reads=[s_.b[0], pub], writes=[act.b[fc]])
            for m in range(8):
                sl = wget(("dn", l, f, m))
                v = sl.t[:, 0:22 * 128].rearrange("p (k c) -> p k c", k=22)
                py, pyb = psum()
                for fc in range(22):
                    S.op("pe", lambda e: e.matmul(py[:, :T], v[:, fc, :], act.t[:, fc, :T], start=(fc == 0), stop=(fc == 21)),
                         reads=[sl.b[0], act.b[fc]], writes=[pyb], inc=(fc == 21))
                S.op("dve", lambda e: e.scalar_tensor_tensor(out=hT.t[:, m, :T], in0=py[:, :T], scalar=0.5,
                                                             in1=hT.t[:, m, :T], op0=ALU.mult, op1=ALU.add),
                     reads=[pyb, hT.b[m]], writes=[hT.b[m]])

        def proj_fm(T, sl, ncol, col0, consume):
            v = sl.t[:].rearrange("p (k c) -> p k c", k=8)
            S.acquire("pe", reads=sl.b, writes=psum_peek(ncol))
            for i in range(ncol):
                ps, pb = psum()
                for kc in range(8):
                    S.op("pe", lambda e: e.matmul(ps[:, :T], v[:, kc, col0 + i * 128:col0 + (i + 1) * 128], hn.t[:, kc, :T],
                                                  start=(kc == 0), stop=(kc == 7)),
                         reads=[sl.b[0], hn.b[kc]], writes=[pb], inc=(kc == 7))
                consume(i, ps, pb)

        def store_hT(T, t0, dst):
            S.dma("pool", dst.rearrange("(c p) t -> p c t", p=128)[:, :, t0:t0 + T], hT.t[:, :, :T], hst_ds, reads=hT.b)

        def load_hT(T, t0, src):
            S.dma("sp", hT.t[:, :, :T], src.rearrange("(c p) t -> p c t", p=128)[:, :, t0:t0 + T], hld_ds, writes=hT.b)

        def subs_of(T):
            return [(i * 128, 128) for i in range(T // 128)] if T >= 128 else [(0, T)]

        for s in range(n_seq):
            with ExitStack() as esA:
                sbA = lambda name, shape, dt, nb=1, dsem=False: TT(S, esA, f"{name}_{s}", shape, dt, nb, dsem, semname=name)
                qf = sbA("qf", [128, 4, 512], F32, nb=4); kf = sbA("kf", [128, 4, 512], F32, nb=4)
                ub = sbA("ub", [16, 2, 512], BF16, nb=2)
                tA = sbA("tA", [128, 512], F32); nl = sbA("nl", [128, 2, 512], F32, nb=2)
                Pc = sbA("Pc", [128, 2, 512], F32, nb=2); Ea = sbA("Ea", [128, 512], F32); Eb = sbA("Eb", [128, 512], F32)
                Pe = sbA("Pe", [128, 512], F32); t1 = sbA("t1", [128, 512], F32); t2 = sbA("t2", [128, 512], F32)
                Gs = sbA("Gs", [128, 2, 8], F32, nb=2)
                qdst = sbA("qdst", [128, 2, 4, 4, 128], BF16, nb=1, dsem=True)
                kist = sbA("kist", [128, 2, 4, 4, 128], BF16, nb=1, dsem=True)
                keT = sbA("keT", [128, 2, 2, 512], BF16, nb=2)
                ketok = sbA("ketok", [128, 2, 2, 4, 128], BF16, nb=2, dsem="per")
                vtok = sbA("vtok", [128, 2, 4, 512], BF16, nb=2, dsem="per")
                srst = sbA("srst", [128, 2, 4, 512], F32, nb=2, dsem="per")
                for ti, (t0, T) in enumerate(TILES):
                    subs = subs_of(T)
                    nch = max(1, T // 64)
                    clen = min(64, T)
                    ci0 = t0 // 64
                    def load_x(ti_):
                        t0_, T_ = TILES[ti_]
                        if T_ == NMETA:
                            S.dma("sp", xin.t[0:T_, 0, :], meta_tokens, xin.ds, writes=xin.b)
                        else:
                            S.dma("sp", xin.t[:, :, :], x[s, t0_:t0_ + T_, :].rearrange("(u p) d -> p u d", p=128), xin.ds,
                                  writes=xin.b)
                    if ti == 0:
                        load_x(0)
                    for c in range(8):
                        ps, pb = psum()
                        for u, (o, n) in enumerate(subs):
                            S.op("pe", lambda e: e.transpose(ps[:, o:o + n], xin.t[0:n, u, c * 128:(c + 1) * 128],
                                                             ident_f.t[0:n, 0:n]),
                                 reads=[xin.b[0], ident_f.b[0]], writes=[pb], inc=(u == len(subs) - 1))
                        evac(hT.t[:, c, :T], ps[:, :T], [pb], [hT.b[c]])
                    if ti + 1 < len(TILES):
                        load_x(ti + 1)
                    ffn(T, 0, 1)
                    store_hT(T, t0, H1[s])
                    rmsnorm(T, 1)
                    for n in range(2):
                        ps, pb = psum()
                        for kc in range(8):
                            S.op("pe", lambda e: e.matmul(ps[0:16, :T], w1.t[:, n, kc, :], hn.t[:, kc, :T],
                                                          start=(kc == 0), stop=(kc == 7)),
                                 reads=[w1.b[0], hn.b[kc]], writes=[pb], inc=(kc == 7))
                        evac(ub.t[0:16, n, :T], ps[0:16, :T], [pb], [ub.b[n]])
                    sl = wget(("m512", "gla_w_in", 0))
                    proj_fm(T, sl, 4, 0, lambda i, ps, pb: S.op(
                        "act", lambda e: e.mul(qf.t[:, i, :T], ps[:, :T], 128.0 ** -0.5), reads=[pb], writes=[qf.b[i]]))
                    sl = wget(("m512", "gla_w_in", 1))
                    proj_fm(T, sl, 4, 0, lambda i, ps, pb: evac(kf.t[:, i, :T], ps[:, :T], [pb], [kf.b[i]]))
                    def proj_piece(k):
                        if k < 2:
                            vp = k
                            sl = wget(("m512", "gla_w_in", 2 + vp))
                            v = sl.t[:].rearrange("p (k c) -> p k c", k=8)
                            for u, (o, n) in enumerate(subs):
                                ps, pb = psum()
                                for kc in range(8):
                                    S.op("pe", lambda e: e.matmul(ps[0:n, :], hn.t[:, kc, o:o + n], v[:, kc, :],
                                                                  start=(kc == 0), stop=(kc == 7)),
                                         reads=[sl.b[0], hn.b[kc]], writes=[pb], inc=(kc == 7))
                                evac(vtok.t[0:n, vp, u, :], ps[0:n, :], [pb], [vtok.b[vp]])
                            dst = VV[s][t0:t0 + T, vp * 512:(vp + 1) * 512]
                            if T >= 128:
                                S.dma("pool", dst.rearrange("(u p) c -> p u c", p=128), vtok.t[:, vp, :, :], vtok.dsl[vp], reads=[vtok.b[vp]])
                            else:
                                S.dma("pool", dst, vtok.t[0:T, vp, 0, :], vtok.dsl[vp], reads=[vtok.b[vp]])

                        else:
                            rp = k - 2
                            sl = wget(("m512", "gla_w_in", 4 + rp))
                            proj_fm(T, sl, 4, 0, lambda i, ps, pb: S.op(
                                "act", lambda e: e.activation(out=srst.t[:, rp, i, :T], in_=ps[:, :T], func=AF.Silu),
                                reads=[pb], writes=[srst.b[rp]]))
                            S.dma("pool", SR[s].rearrange("(c p) t -> p c t", p=128)[:, rp * 4:(rp + 1) * 4, t0:t0 + T],
                                  srst.t[:, rp, :, :T], srst.dsl[rp], reads=[srst.b[rp]])

                    def gate_math(h):
                        par = h % 2

                        def bc(tt, off):
                            return bass.AP(tt.t, off + clen - 1, [[tt.ps, 128], [clen, nch], [0, clen]])

                        def ch(tt, off):
                            return bass.AP(tt.t, off + clen - 1, [[tt.ps, 128], [clen, nch]])

                        def v3(ap):
                            return ap.rearrange("p (c j) -> p c j", j=clen)

                        wid = min(128, T)
                        nsb = len(subs)

                        def d4(tt, d_):
                            return tt.t[:, d_, 0:nsb, h, 0:wid]

                        def u3(ap):
                            return ap.rearrange("p (u j) -> p u j", j=wid)

                        for n in range(2):
                            ps, pb = psum()
                            S.op("pe", lambda e: e.matmul(ps[:, :T], w2.t[0:16, n, h * 128:(h + 1) * 128], ub.t[0:16, n, :T],
                                                          start=True, stop=True), reads=[w2.b[0], ub.b[n]], writes=[pb])
                            S.op("act", lambda e: e.activation(out=tA.t[:, :T], in_=ps[:, :T], func=AF.Exp, scale=-1.0,
                                                               bias=ngb.t[:, n, h:h + 1]),
                                 reads=[pb, ngb.b[0]], writes=tA.b)
                            S.op("act", lambda e: e.activation(out=nl.t[:, n, :T], in_=tA.t[:, :T], func=AF.Ln, bias=1.0),
                                 reads=tA.b, writes=[nl.b[n]])
                            S.op("dve", lambda e: e.tensor_tensor_scan(out=Pc.t[:, n, :T], data0=scanm.t[:, :T],
                                                                       data1=nl.t[:, n, :T], initial=0.0,
                                                                       op0=ALU.mult, op1=ALU.add),
                                 reads=[scanm.b[0], nl.b[n]], writes=[Pc.b[n]])
                        S.op("act", lambda e: e.activation(out=Ea.t[:, :T], in_=Pc.t[:, 0, :T], func=AF.Exp, scale=-1.0 / 16),
                             reads=[Pc.b[0]], writes=Ea.b)
                        S.op("act", lambda e: e.activation(out=Eb.t[:, :T], in_=Pc.t[:, 0, :T], func=AF.Exp, scale=1.0 / 16),
                             reads=[Pc.b[0]], writes=Eb.b)
                        S.op("dve", lambda e: e.tensor_tensor(out=d4(qdst, 0), in0=u3(qf.t[:, h, :T]), in1=u3(Ea.t[:, :T]), op=ALU.mult),
                             reads=[qf.b[h], Ea.b[0]], writes=qdst.b)
                        S.op("dve", lambda e: e.tensor_tensor(out=t1.t[:, :T], in0=kf.t[:, h, :T], in1=Eb.t[:, :T], op=ALU.mult),
                             reads=[kf.b[h], Eb.b[0]], writes=t1.b)
                        S.op("act", lambda e: e.copy(d4(kist, 0), u3(t1.t[:, :T])), reads=t1.b, writes=kist.b)
                        S.op("pool", lambda e: e.tensor_tensor(out=v3(keT.t[:, par, 0, :T]), in0=v3(t1.t[:, :T]), in1=bc(Ea, 0), op=ALU.mult),
                             reads=[t1.b[0], Ea.b[0]], writes=[keT.b[par]])
                        S.op("act", lambda e: e.copy(dec.t[:, 0, h, ci0:ci0 + nch], ch(Ea, 0)), reads=Ea.b, writes=dec.b)
                        S.op("dve", lambda e: e.tensor_tensor(out=Pe.t[:, :T], in0=Pc.t[:, 1, :T], in1=nl.t[:, 1, :T], op=ALU.subtract),
                             reads=[Pc.b[1], nl.b[1]], writes=Pe.b)
                        S.op("act", lambda e: e.activation(out=Ea.t[:, :T], in_=Pe.t[:, :T], func=AF.Exp, scale=-1.0 / 16),
                             reads=Pe.b, writes=Ea.b)
                        S.op("act", lambda e: e.activation(out=Eb.t[:, :T], in_=Pe.t[:, :T], func=AF.Exp, scale=1.0 / 16),
                             reads=Pe.b, writes=Eb.b)
                        S.op("act", lambda e: e.activation(out=Gs.t[:, 0, 0:nch], in_=ch(Pc, 512), func=AF.Exp, scale=-1.0 / 16),
                             reads=[Pc.b[1]], writes=[Gs.b[0]])
                        S.op("act", lambda e: e.activation(out=Gs.t[:, 1, 0:nch], in_=ch(Pc, 512), func=AF.Exp, scale=1.0 / 16),
                             reads=[Pc.b[1]], writes=[Gs.b[1]])
                        g1b = bass.AP(Gs.t, 0, [[16, 128], [1, nch], [0, clen]])
                        g2b = bass.AP(Gs.t, 8, [[16, 128], [1, nch], [0, clen]])
                        S.op("dve", lambda e: e.tensor_tensor(out=t1.t[:, :T], in0=qf.t[:, h, :T], in1=Eb.t[:, :T], op=ALU.mult),
                             reads=[qf.b[h], Eb.b[0]], writes=t1.b)
                        S.op("pool", lambda e: e.tensor_tensor(out=d4(qdst, 1), in0=v3(t1.t[:, :T]), in1=g1b, op=ALU.mult),
                             reads=[t1.b[0], Gs.b[0]], writes=qdst.b)
                        S.op("dve", lambda e: e.tensor_tensor(out=t2.t[:, :T], in0=kf.t[:, h, :T], in1=Ea.t[:, :T], op=ALU.mult),
                             reads=[kf.b[h], Ea.b[0]], writes=t2.b)
                        S.op("act", lambda e: e.copy(keT.t[:, par, 1, :T], t2.t[:, :T]), reads=t2.b, writes=[keT.b[par]])
                        S.op("pool", lambda e: e.tensor_tensor(out=d4(kist, 1), in0=v3(t2.t[:, :T]), in1=g2b, op=ALU.mult),
                             reads=[t2.b[0], Gs.b[1]], writes=kist.b)
                        S.op("act", lambda e: e.copy(dec.t[:, 1, h, ci0:ci0 + nch], Gs.t[:, 0, 0:nch]), reads=[Gs.b[0]], writes=dec.b)
                    def ke_transposes(h):
                        par = h % 2
                        for d_ in range(2):
                            ps, pb = psum()
                            pbf = ps[:].bitcast(BF16)
                            for u, (o, n) in enumerate(subs):
                                S.op("pe", lambda e: e.transpose(pbf[0:n, u * 128:(u + 1) * 128], keT.t[:, par, d_, o:o + n], ident_b.t[:, :]),
                                     reads=[keT.b[par], ident_b.b[0]], writes=[pb], inc=(u == len(subs) - 1))
                            nu = len(subs)
                            n0 = subs[0][1]
                            evac(ketok.t[0:n0, par, d_, 0:nu, :], pbf[0:n0, 0:nu * 128].rearrange("p (u c) -> p u c", c=128),
                                 [pb], [ketok.b[par]])
                            dst = KE[s][d_, t0:t0 + T, h * 128:(h + 1) * 128]
                            if T >= 128:
                                S.dma("pool", dst.rearrange("(u p) c -> p u c", p=128), ketok.t[:, par, d_, :, :], ketok.dsl[par], reads=[ketok.b[par]])
                            else:
                                S.dma("pool", dst, ketok.t[0:T, par, d_, 0, :], ketok.dsl[par], reads=[ketok.b[par]])
                    for h in range(4):
                        gate_math(h)
                        proj_piece(h)
                        if h >= 1:
                            ke_transposes(h - 1)
                    ke_transposes(3)
                    for d_ in range(2):
                        for tt_, dst_ in ((qdst, QD), (kist, KI)):
                            if T >= 128:
                                S.dma("pool", dst_[s][d_, :, t0 // 128:t0 // 128 + 4, :, :], tt_.t[:, d_, :, :, :], tt_.ds, reads=tt_.b)
                            else:
                                S.dma("pool", dst_[s][d_, :, 32, :, 0:T], tt_.t[:, d_, 0, :, 0:T], tt_.ds, reads=tt_.b)
                S.barrier()
            if stop_after == "A":
                break

            with ExitStack() as esG:
                sbG = lambda name, shape, dt, nb=1, dsem=False: TT(S, esG, f"{name}_{s}", shape, dt, nb, dsem, semname=name)
                Sst = sbG("Sst", [128, 2, 4, 256], F32, nb=8)
                Sbf = sbG("Sbf", [128, 3, 4, 2, 256], BF16, nb=24)
                qdl = [sbG(f"qdl{i}", [128, 4, 128], BF16, dsem=True) for i in range(2)]
                kil = [sbG(f"kil{i}", [128, 4, 128], BF16, dsem=True) for i in range(2)]
                kel = [sbG(f"kel{i}", [128, 512], BF16, dsem=True) for i in range(2)]
                keh = [sbG(f"keh{i}", [128, 512], BF16, dsem=True) for i in range(2)]
                vl = [sbG(f"vl{i}", [128, 1024], BF16, dsem=True) for i in range(2)]
                Am = [sbG(f"Am{i}", [128, 4, 128], BF16, nb=4) for i in range(2)]
                ost = [sbG(f"ost{i}", [128, 8, 128], F32, dsem=True) for i in range(2)]
                ofl = [sbG(f"ofl{i}", [128, 8, 128], F32, dsem=True) for i in range(2)]
                kvsb = [sbG(f"kvsb{i}", [128, 4, 512], F32, nb=4) for i in range(2)]
                gps = {"kv": [0, 1, 2, 3], "a": [4, 5], "o": [6, 7]}
                gpc = {"kv": 0, "a": 0, "o": 0}

                def psum_g(role):
                    i = gps[role][gpc[role] % len(gps[role])]
                    gpc[role] += 1
                    return psb[i], psB[i]
                cgen = None
                if s == 0 and deferred:
                    ost_g = [sbG(f"costg{i}", [128, 8448], BF16, dsem=True) for i in range(2)]
                    fx = xin.t[:].rearrange("p u d -> p (u d)")
                    fh = hT.t[:].rearrange("p c t -> p (c t)")
                    stg_g = [View(f_[:, i * 1024:(i + 1) * 1024], [Buf(f"cs{k}{i}")], S.dsem(f"cs{k}{i}"))
                             for k, f_ in enumerate((fx, fh)) for i in range(4)]
                    cgen = conv_gen(deferred, stg_g, ost_g, ("pool", "act"), "pool")

                def conv_step(k):
                    nonlocal_c = cgen
                    if nonlocal_c is None:
                        return
                    for _ in range(k):
                        if next(nonlocal_c, "done") == "done":
                            break
                for i in range(2):
                    S.op("dve", lambda e: e.memset(kel[i].t[:], 0.0), writes=kel[i].b)
                    S.op("dve", lambda e: e.memset(keh[i].t[:], 0.0), writes=keh[i].b)
                for d_ in range(2):
                    order = SUBS if d_ == 0 else SUBS[::-1]
                    mk = maskf if d_ == 0 else maskb
                    S.op("dve", lambda e: e.memset(Sst.t[:], 0.0), writes=Sst.b)
                    scur = [0, 0, 0, 0]
                    for h in range(4):
                        S.op("dve", lambda e: e.memset(Sbf.t[:, 0, h, 0, :], 0.0), writes=[Sbf.b[h * 2]])
                    def g_loads(it):
                        t0, n = order[it]
                        par = it % 2
                        S.dma("sp", qdl[par].t[:, :, 0:n], QD[s][d_, :, t0 // 128, :, 0:n], qdl[par].ds, writes=qdl[par].b)
                        S.dma("sp", kil[par].t[:, :, 0:n], KI[s][d_, :, t0 // 128, :, 0:n], kil[par].ds, writes=kil[par].b)
                        if n == 128:
                            S.dma("sp", kel[par].t[0:64, :], KE[s][d_, t0:t0 + 64, :], kel[par].ds, writes=kel[par].b)
                            S.dma("sp", keh[par].t[64:128, :], KE[s][d_, t0 + 64:t0 + 128, :], keh[par].ds, writes=keh[par].b)
                        else:
                            S.dma("sp", kel[par].t[0:n, :], KE[s][d_, t0:t0 + n, :], kel[par].ds, writes=kel[par].b)
                        S.dma("sp", vl[par].t[0:n, :], VV[s][t0:t0 + n, :], vl[par].ds, writes=vl[par].b)
                        if d_ == 1:
                            S.dma("sp", ofl[par].t[:, :, 0:n], OO[s][0, :, t0 // 128, :, 0:n], ofl[par].ds, writes=ofl[par].b)

                    gst = {}

                    def g_ctx(it):
                        t0, n = order[it]
                        par = it % 2
                        sbi = lambda p_, h_, k_: p_ * 8 + h_ * 2 + k_
                        p3 = it % 3
                        if n == 128:
                            chunks = [(0, 64, kel[par], t0 // 64), (64, 64, keh[par], t0 // 64 + 1)]
                            if d_ == 1:
                                chunks = chunks[::-1]
                            Kc = 128
                        else:
                            chunks = [(0, n, kel[par], 64)]
                            Kc = n
                        return t0, n, par, p3, sbi, chunks, Kc

                    def g_stage1(it):
                        t0, n, par, p3, sbi, chunks, Kc = g_ctx(it)
                        pkvs, pas = [], []
                        nk = len(chunks)
                        for h in range(4):
                            pkv, pkvb = psum_g("kv")
                            for k, (co, cl, ket, ci) in enumerate(chunks):
                                S.op("pe", lambda e: e.matmul(pkv[:, k * 256:(k + 1) * 256], ket.t[0:Kc, h * 128:(h + 1) * 128],
                                                              vl[par].t[0:Kc, h * 256:(h + 1) * 256], start=True, stop=True),
                                     reads=[ket.b[0], vl[par].b[0]], writes=[pkvb], inc=(k == nk - 1))
                            S.op("act", lambda e: e.copy(kvsb[par].t[:, h, 0:nk * 256], pkv[:, 0:nk * 256]), reads=[pkvb], writes=[kvsb[par].b[h]])
                        pa, pab = psum_g("a")
                        for h in range(4):
                            S.op("pe", lambda e: e.matmul(pa[0:n, h * 128:h * 128 + n], kil[par].t[:, h, 0:n], qdl[par].t[:, h, 0:n], start=True, stop=True),
                                 reads=[kil[par].b[0], qdl[par].b[0]], writes=[pab], inc=(h == 3))
                        for k, (co, cl, ket, ci) in enumerate(chunks):
                            for h in range(4):
                                a_, b_ = scur[h], 1 - scur[h]
                                scur[h] = b_
                                S.op("dve", lambda e: e.scalar_tensor_tensor(out=Sst.t[:, b_, h, :], in0=Sst.t[:, a_, h, :],
                                                                             scalar=dec.t[:, d_, h, ci:ci + 1],
                                                                             in1=kvsb[par].t[:, h, k * 256:(k + 1) * 256],
                                                                             op0=ALU.mult, op1=ALU.add),
                                     reads=[Sst.b[a_ * 4 + h], dec.b[0], kvsb[par].b[h]], writes=[Sst.b[b_ * 4 + h]])
                                if k < len(chunks) - 1:
                                    tp, tk = p3, 1
                                else:
                                    tp, tk = (p3 + 1) % 3, 0
                                S.op("act", lambda e: e.copy(Sbf.t[:, tp, h, tk, :], Sst.t[:, b_, h, :]),
                                     reads=[Sst.b[b_ * 4 + h]], writes=[Sbf.b[sbi(tp, h, tk)]])
                            if k == 0:
                                for h in range(4):
                                    S.op("dve", lambda e: e.tensor_tensor(out=Am[par].t[0:n, h, 0:n], in0=pa[0:n, h * 128:h * 128 + n], in1=mk.t[0:n, 0:n], op=ALU.mult),
                                         reads=[pab, mk.b[0]], writes=[Am[par].b[h]])

                    def g_stage2(it):
                        t0, n, par, p3, sbi, chunks, Kc = g_ctx(it)
                        for hb in range(2):
                            po, pob = psum_g("o")
                            for h in (2 * hb, 2 * hb + 1):
                                for dvc in range(2):
                                    c0 = ((h % 2) * 2 + dvc) * 128
                                    S.op("pe", lambda e: e.matmul(po[:, c0:c0 + n], vl[par].t[0:n, h * 256 + dvc * 128:h * 256 + (dvc + 1) * 128],
                                                                  Am[par].t[0:n, h, 0:n], start=True, stop=False),
                                         reads=[vl[par].b[0], Am[par].b[h]], writes=[pob], inc=False)
                                    for k, (co, cl, ket, ci) in enumerate(chunks):
                                        last = (k == len(chunks) - 1)
                                        S.op("pe", lambda e: e.matmul(po[:, c0 + co:c0 + co + cl], Sbf.t[:, p3, h, k, dvc * 128:(dvc + 1) * 128],
                                                                      qdl[par].t[:, h, co:co + cl], start=False, stop=last),
                                             reads=[Sbf.b[sbi(p3, h, k)], qdl[par].b[0]], writes=[pob], inc=last)
                            for h in (2 * hb, 2 * hb + 1):
                                for dvc in range(2):
                                    c0 = ((h % 2) * 2 + dvc) * 128
                                    if d_ == 0:
                                        evac(ost[par].t[:, h * 2 + dvc, 0:n], po[:, c0:c0 + n], [pob], ost[par].b)
                                    else:
                                        S.op("dve", lambda e: e.tensor_tensor(out=ost[par].t[:, h * 2 + dvc, 0:n], in0=po[:, c0:c0 + n],
                                                                              in1=ofl[par].t[:, h * 2 + dvc, 0:n], op=ALU.add),
                                             reads=[pob, ofl[par].b[0]], writes=ost[par].b)
                        S.dma("pool", OO[s][d_, :, t0 // 128, :, 0:n], ost[par].t[:, :, 0:n], ost[par].ds, reads=ost[par].b)
                        if it + 2 < len(order):
                            g_loads(it + 2)

                    g_loads(0)
                    if len(order) > 1:
                        g_loads(1)
                    g_stage1(0)
                    for it in range(len(order)):
                        if it + 1 < len(order):
                            g_stage1(it + 1)
                        g_stage2(it)
                        conv_step(7)
                    if d_ == 1:
                        conv_step(10 ** 6)
                    S.barrier()
                S.barrier()
            if stop_after == "G":
                break

            esKV = ExitStack()
            KT = TT(S, esKV, f"KT_{s}", [128, 2, NT], BF16, 2)
            VR = TT(S, esKV, f"VR_{s}", [128, 33, 256], BF16, 1)
            with ExitStack() as esB:
                sbB = lambda name, shape, dt, nb=1, dsem=False: TT(S, esB, f"{name}_{s}", shape, dt, nb, dsem, semname=name)
                srt = sbB("srt", [128, 8, 512], F32, nb=8, dsem=True)
                oft = xin
                qkt = []
                for i in range(3):
                    xs_ = sbB(f"xs{i}", [128, 512], F32)
                    ms_ = sbB(f"msq{i}", [128, 512], F32)
                    qkt.append(dict(xs=xs_, r2=xs_, xn=sbB(f"xn{i}", [128, 512], F32), r1=sbB(f"r1{i}", [128, 512], F32),
                                    sqh=sbB(f"sqh{i}", [128, 512], BF16), ms=ms_, rstd=ms_))
                qki = [0]
                Ct = sbB("Ct", [128, 512], F32, dsem=True); St = sbB("St", [128, 512], F32, dsem=True)
                q2st = sbB("q2st", [128, 8, 512], BF16, nb=8, dsem=True)
                def load_os(ti_):
                    t0_, T_ = TILES[ti_]
                    if T_ >= 128:
                        S.dma("sp", xin.t[:, :, :], OO[s][1, :, t0_ // 128:t0_ // 128 + 4, :, :].rearrange("p u c j -> p u (c j)"), xin.ds, writes=xin.b)
                    else:
                        S.dma("sp", xin.t[:, 0, :].rearrange("p (c j) -> p c j", j=128)[:, :, 0:T_], OO[s][1, :, 32, :, 0:T_], xin.ds, writes=xin.b)
                    S.dma("sp", srt.t[:, :, :T_], SR[s].rearrange("(c p) t -> p c t", p=128)[:, :, t0_:t0_ + T_], srt.ds, writes=srt.b)

                for ti, (t0, T) in enumerate(TILES):
                    subs = subs_of(T)
                    if ti == 0:
                        load_os(0)
                    load_hT(T, t0, H1[s])
                    S.dma("sp", Ct.t[:, :T], c_cos[:, t0:t0 + T], Ct.ds, writes=Ct.b)
                    S.dma("sp", St.t[:, :T], c_sin[:, t0:t0 + T], St.ds, writes=St.b)
                    wid = min(128, T)
                    nsb = len(subs)
                    ofc = lambda c: xin.t[:, 0:nsb, c * 128:c * 128 + wid]
                    u3 = lambda ap: ap.rearrange("p (u j) -> p u j", j=wid)
                    for h in range(4):
                        ps, pb = psum()
                        for dvc in range(2):
                            c = 2 * h + dvc
                            S.op("act", lambda e: e.activation(out=u3(act.t[:, c, :T]), in_=ofc(c), func=AF.Square),
                                 reads=[xin.b[c]], writes=[act.b[c]])
                            S.op("pe", lambda e: e.matmul(ps[:, :T], ones_b.t[:], act.t[:, c, :T], start=(dvc == 0), stop=(dvc == 1)),
                                 reads=[ones_b.b[0], act.b[c]], writes=[pb], inc=(dvc == 1))
                        S.op("act", lambda e: e.activation(out=ms.t[:, :T], in_=ps[:, :T], func=AF.Ln, scale=1.0 / 256, bias=EPS), reads=[pb], writes=ms.b)
                        S.op("act", lambda e: e.activation(out=rstd.t[:, :T], in_=ms.t[:, :T], func=AF.Exp, scale=-0.5), reads=ms.b, writes=rstd.b)
                        for dvc in range(2):
                            c = 2 * h + dvc
                            S.op("dve", lambda e: e.scalar_tensor_tensor(out=ofc(c), in0=ofc(c),
                                                                         scalar=hncol.t[:, dvc:dvc + 1], in1=u3(rstd.t[:, :T]),
                                                                         op0=ALU.mult, op1=ALU.mult),
                                 reads=[xin.b[c], hncol.b[0], rstd.b[0]], writes=[xin.b[c]])
                            S.op("dve", lambda e: e.tensor_tensor(out=u3(hn.t[:, c, :T]), in0=ofc(c), in1=u3(srt.t[:, c, :T]), op=ALU.mult),
                                 reads=[xin.b[c], srt.b[c]], writes=[hn.b[c]])
                    if ti + 1 < len(TILES):
                        load_os(ti + 1)
                    for j in range(2):
                        sl = wget(("m512", "gla_w_out", j))
                        proj_fm(T, sl, 4, 0, lambda i, ps, pb: S.op(
                            "dve", lambda e: e.tensor_tensor(out=hT.t[:, j * 4 + i, :T], in0=ps[:, :T], in1=hT.t[:, j * 4 + i, :T], op=ALU.add),
                            reads=[pb, hT.b[j * 4 + i]], writes=[hT.b[j * 4 + i]]))
                    ffn(T, 0, 2)
                    ffn(T, 1, 1)
                    store_hT(T, t0, H4[s])
                    rmsnorm(T, 4)

                    pend = []

                    def qk_stage0(ps, pb, gi, dst_ap, dst_b):
                        q_ = qkt[qki[0] % 3]
                        qki[0] += 1
                        S.op("act", lambda e: e.copy(q_["xs"].t[:, :T], ps[:, :T]), reads=[pb], writes=q_["xs"].b)
                        S.op("act", lambda e: e.activation(out=q_["sqh"].t[:, :T], in_=ps[:, :T], func=AF.Square), reads=[pb], writes=q_["sqh"].b)
                        pend.append([q_, gi, dst_ap, dst_b, 0])
                        qk_advance(2)

                    def qk_stage1(it_):
                        q_, gi = it_[0], it_[1]
                        xs, xn, sqh, ms_, rs_ = q_["xs"], q_["xn"], q_["sqh"], q_["ms"], q_["rstd"]
                        p2, p2b = psum()
                        S.op("pe", lambda e: e.matmul(p2[:, :T], ones_b.t[:], sqh.t[:, :T], start=True, stop=True),
                             reads=[ones_b.b[0], sqh.b[0]], writes=[p2b])
                        S.op("act", lambda e: e.activation(out=ms_.t[:, :T], in_=p2[:, :T], func=AF.Ln, scale=1.0 / 128, bias=EPS), reads=[p2b], writes=ms_.b)
                        S.op("act", lambda e: e.activation(out=rs_.t[:, :T], in_=ms_.t[:, :T], func=AF.Exp, scale=-0.5), reads=ms_.b, writes=rs_.b)
                        S.op("dve", lambda e: e.scalar_tensor_tensor(out=xn.t[:, :T], in0=xs.t[:, :T], scalar=qkcol.t[:, gi:gi + 1],
                                                                     in1=rs_.t[:, :T], op0=ALU.mult, op1=ALU.mult),
                             reads=[xs.b[0], qkcol.b[0], rs_.b[0]], writes=xn.b)

                    def qk_stage2(it_):
                        q_, gi, dst_ap, dst_b = it_[0], it_[1], it_[2], it_[3]
                        xn, r1, r2 = q_["xn"], q_["r1"], q_["r2"]
                        p3, p3b = psum()
                        S.op("pe", lambda e: e.matmul(p3[:, :T], pmt.t[:], xn.t[:, :T], start=True, stop=True),
                             reads=[pmt.b[0], xn.b[0]], writes=[p3b])
                        S.op("pool", lambda e: e.tensor_tensor(out=r1.t[:, :T], in0=xn.t[:, :T], in1=Ct.t[:, :T], op=ALU.mult),
                             reads=[xn.b[0], Ct.b[0]], writes=r1.b)
                        S.op("dve", lambda e: e.tensor_tensor(out=r2.t[:, :T], in0=p3[:, :T], in1=St.t[:, :T], op=ALU.mult),
                             reads=[p3b, St.b[0]], writes=r2.b)
                        S.op("dve", lambda e: e.tensor_tensor(out=dst_ap, in0=r1.t[:, :T], in1=r2.t[:, :T], op=ALU.add),
                             reads=[r1.b[0], r2.b[0]], writes=dst_b)

                    def qk_advance(lag):
                        n_ = len(pend)
                        for k_, it_ in enumerate(pend):
                            age = n_ - 1 - k_
                            if it_[4] == 0 and age >= lag - 1:
                                qk_stage1(it_)
                                it_[4] = 1
                            elif it_[4] == 1 and age >= lag:
                                qk_stage2(it_)
                                it_[4] = 2
                        while pend and pend[0][4] == 2:
                            pend.pop(0)

                    def qk_flush():
                        while pend:
                            for it_ in pend:
                                if it_[4] == 0:
                                    qk_stage1(it_)
                                    it_[4] = 1
                                elif it_[4] == 1:
                                    qk_stage2(it_)
                                    it_[4] = 2
                            while pend and pend[0][4] == 2:
                                pend.pop(0)

                    for j in range(2):
                        sl = wget(("m512", "attn_w_in", j))
                        proj_fm(T, sl, 4, 0, lambda i, ps, pb: qk_stage0(ps, pb, 0, q2st.t[:, j * 4 + i, :T], [q2st.b[j * 4 + i]]))
                    sl = wget(("m512", "attn_w_in", 2))
                    proj_fm(T, sl, 2, 0, lambda i, ps, pb: qk_stage0(ps, pb, 1, KT.t[:, i, t0:t0 + T], [KT.b[i]]))
                    v = sl.t[:].rearrange("p (k c) -> p k c", k=8)
                    for u, (o, n) in enumerate(subs):
                        ps, pb = psum()
                        for kc in range(8):
                            S.op("pe", lambda e: e.matmul(ps[0:n, 0:256], hn.t[:, kc, o:o + n], v[:, kc, 256:512],
                                                          start=(kc == 0), stop=(kc == 7)),
                                 reads=[sl.b[0], hn.b[kc]], writes=[pb], inc=(kc == 7))
                        evac(VR.t[0:n, (t0 + o) // 128, :], ps[0:n, 0:256], [pb], VR.b)
                        if u == 0:
                            qk_advance(1)
                    qk_flush()
                    S.dma("pool", Q2[s].rearrange("(c p) t -> p c t", p=128)[:, :, t0:t0 + T], q2st.t[:, :, :T], q2st.ds, reads=q2st.b)
                S.barrier()
            if stop_after == "B":
                esKV.close()
                break

            with ExitStack() as esD:
                sbD = lambda name, shape, dt, nb=1, dsem=False: TT(S, esD, f"{name}_{s}", shape, dt, nb, dsem, semname=name)
                qts = [sbD(f"qt{i}", [128, 8, 512], BF16, dsem=True) for i in range(2)]
                pT = [sbD(f"pT{i}", [128, 512], BF16) for i in range(6)]
                rden = [sbD(f"rden{i}", [128, 512], F32) for i in range(2)]
                dacc = [[sbD(f"dacc{i}{j}", [128, 512], F32) for j in range(3)] for i in range(2)]
                yf = sbD("yf", [128, 8, 512], F32, nb=8)
                pst[1] = 4
                pti = 0
                def load_q(ti_):
                    t0_, T_ = TILES[1:][ti_]
                    q_ = qts[ti_ % 2]
                    S.dma("sp", q_.t[:, :, :T_], Q2[s].rearrange("(c p) t -> p c t", p=128)[:, :, t0_:t0_ + T_], q_.ds, writes=q_.b)

                load_q(0)
                for ti, (t0, T) in enumerate(TILES[1:]):
                    subs = subs_of(T)
                    qt = qts[ti % 2]
                    if ti + 1 < len(TILES) - 1:
                        load_q(ti + 1)
                    load_hT(T, t0, H4[s])
                    steps = [(head, blk) for head in range(8) for blk in range(33)]
                    LOOK = 3
                    sps = {}

                    def emit_s(i):
                        head, blk = steps[i]
                        kvh = head // 4
                        K_ = 128 if blk < 32 else NMETA
                        ps, pb = psum()
                        S.op("pe", lambda e: e.matmul(ps[0:K_, :T], KT.t[:, kvh, blk * 128:blk * 128 + K_], qt.t[:, head, :T],
                                                      start=True, stop=True), reads=[KT.b[kvh], qt.b[0]], writes=[pb])
                        sps[i] = (ps, pb)

                    for i in range(LOOK):
                        emit_s(i)
                    for i, (head, blk) in enumerate(steps):
                        kvh = head // 4
                        K_ = 128 if blk < 32 else NMETA
                        ps, pb = sps.pop(i)
                        a0 = 4 + 2 * (head % 2)
                        po, pob, pd, pdb = psb[a0], psB[a0], psb[a0 + 1], psB[a0 + 1]
                        p_ = pT[pti % 6]
                        pti += 1
                        S.op("act", lambda e: e.activation(out=p_.t[0:K_, :T], in_=ps[0:K_, :T], func=AF.Exp),
                             reads=[pb], writes=p_.b)
                        S.op("pe", lambda e: e.matmul(po[:, :T], VR.t[0:K_, blk, kvh * 128:(kvh + 1) * 128], p_.t[0:K_, :T],
                                                      start=(blk == 0), stop=(blk == 32)),
                             reads=[VR.b[0], p_.b[0]], writes=[pob], inc=True)
                        ai = blk % 3
                        if ai == 2:
                            S.op("pe", lambda e: e.matmul(pd[:, :T], ones_b.t[0:K_, :], p_.t[0:K_, :T], start=(blk == 2), stop=False),
                                 reads=[ones_b.b[0], p_.b[0]], writes=[pdb], inc=True)
                        else:
                            acc = dacc[head % 2][ai]
                            if blk < 2:
                                S.op("dve", lambda e: e.tensor_copy(acc.t[:, :T], p_.t[:, :T]), reads=p_.b, writes=acc.b)
                            else:
                                S.op("dve", lambda e: e.tensor_tensor(out=acc.t[0:K_, :T], in0=acc.t[0:K_, :T], in1=p_.t[0:K_, :T], op=ALU.add),
                                     reads=[acc.b[0], p_.b[0]], writes=acc.b)
                        if i + LOOK < len(steps):
                            emit_s(i + LOOK)
                        if blk == 32:
                            a_ = dacc[head % 2]
                            S.op("dve", lambda e: e.tensor_tensor(out=a_[0].t[:, :T], in0=a_[0].t[:, :T], in1=a_[1].t[:, :T], op=ALU.add),
                                 reads=[a_[0].b[0], a_[1].b[0]], writes=a_[0].b)
                            S.op("pe", lambda e: e.matmul(pd[:, :T], ones_f.t[:], a_[0].t[:, :T], start=False, stop=True),
                                 reads=[ones_f.b[0], a_[0].b[0]], writes=[pdb])
                            rd = rden[head % 2]
                            S.op("act", lambda e: e.activation(out=rd.t[:, :T], in_=pd[:, :T], func=AF.Ln), reads=[pdb], writes=rd.b)
                            S.op("act", lambda e: e.activation(out=rd.t[:, :T], in_=rd.t[:, :T], func=AF.Exp, scale=-1.0), reads=rd.b, writes=rd.b)
                            S.op("dve", lambda e: e.tensor_tensor(out=hn.t[:, head, :T], in0=po[:, :T], in1=rd.t[:, :T], op=ALU.mult),
                                 reads=[pob, rd.b[0]], writes=[hn.b[head]])
                    for j in range(2):
                        sl = wget(("m512", "attn_w_out", j))
                        proj_fm(T, sl, 4, 0, lambda i, ps, pb: S.op(
                            "dve", lambda e: e.tensor_tensor(out=hT.t[:, j * 4 + i, :T], in0=ps[:, :T], in1=hT.t[:, j * 4 + i, :T], op=ALU.add),
                            reads=[pb, hT.b[j * 4 + i]], writes=[hT.b[j * 4 + i]]))
                    ffn(T, 1, 2)
                    for c in range(8):
                        S.op("act", lambda e: e.activation(out=act.t[:, c, :T], in_=hT.t[:, c, :T], func=AF.Square),
                             reads=[hT.b[c]], writes=[act.b[c]])
                    ps, pb = psum()
                    for c in range(8):
                        S.op("pe", lambda e: e.matmul(ps[:, :T], ones_b.t[:], act.t[:, c, :T], start=(c == 0), stop=(c == 7)),
                             reads=[ones_b.b[0], act.b[c]], writes=[pb], inc=(c == 7))
                    S.op("act", lambda e: e.activation(out=ms.t[:, :T], in_=ps[:, :T], func=AF.Ln, scale=1.0 / D, bias=EPS), reads=[pb], writes=ms.b)
                    S.op("act", lambda e: e.activation(out=rstd.t[:, :T], in_=ms.t[:, :T], func=AF.Exp, scale=-0.5), reads=ms.b, writes=rstd.b)
                    for c in range(8):
                        S.op("dve", lambda e: e.scalar_tensor_tensor(out=yf.t[:, c, :T], in0=hT.t[:, c, :T], scalar=gcol.t[:, 6, c:c + 1],
                                                                     in1=rstd.t[:, :T], op0=ALU.mult, op1=ALU.mult),
                             reads=[hT.b[c], gcol.b[0], rstd.b[0]], writes=[yf.b[c]])
                    for u, (o, n) in enumerate(subs):
                        for cg in range(2):
                            ps, pb = psum()
                            for c4 in range(4):
                                c = cg * 4 + c4
                                S.op("pe", lambda e: e.transpose(ps[0:n, c4 * 128:(c4 + 1) * 128], yf.t[:, c, o:o + n], ident_f.t[:, :]),
                                     reads=[yf.b[c], ident_f.b[0]], writes=[pb], inc=(c4 == 3))
                            evac(xin.t[0:n, u, cg * 512:(cg + 1) * 512], ps[0:n, :], [pb], xin.b)
                    S.dma("pool", out[s, t0:t0 + T, :].rearrange("(u p) d -> p u d", p=128), xin.t[:, :, :], xst_ds, reads=xin.b)
                pst[1] = 8
                S.barrier()
            esKV.close()

        S.barrier()
        print("instructions:", S.n_ins, "weights consumed", wst["next"], "/", len(wkeys))
    return nc


def make_consts():
    c = {}
    c["c_ident"] = np.eye(128, dtype=np.float32)
    s_ = np.arange(128)[:, None]
    t_ = np.arange(128)[None, :]
    same = (s_ // 64) == (t_ // 64)
    c["c_maskf"] = (same & (s_ <= t_)).astype(np.float32)
    c["c_maskb"] = (same & (s_ >= t_)).astype(np.float32)
    m = np.ones((128, 512), np.float32)
    m[:, ::64] = 0.0
    c["c_scan"] = m
    pm = np.zeros((128, 128), np.float32)
    for d in range(128):
        if d % 64 < 32:
            pm[d, d + 32] = -1.0
        else:
            pm[d, d - 32] = 1.0
    c["c_pmt"] = np.ascontiguousarray(pm.T)
    inv = (10000.0 ** (-np.arange(0, 64, 2, dtype=np.float32) / 64.0)).astype(np.float32)
    pos = np.arange(SEQ)
    trow = np.concatenate([pos // 64, np.zeros(NMETA, np.int64)]).astype(np.float32)
    tcol = np.concatenate([pos % 64, np.zeros(NMETA, np.int64)]).astype(np.float32)
    ang = np.zeros((128, NT), np.float32)
    for d in range(128):
        ang[d] = (trow if d < 64 else tcol) * inv[d % 32]
    c["c_cos"] = np.cos(ang).astype(np.float32)
    c["c_sin"] = np.sin(ang).astype(np.float32)
    return c


_NC_CACHE = {}


def kernel(**inputs):
    n_cores = 8
    n_seq = 2
    if "nc" not in _NC_CACHE:
        _NC_CACHE["nc"] = build(n_seq=n_seq)
    nc = _NC_CACHE["nc"]
    consts = make_consts()
    x = np.ascontiguousarray(np.asarray(inputs["x"], dtype=np.float32))
    shared = {k: np.ascontiguousarray(np.asarray(v, dtype=np.float32)) for k, v in inputs.items() if k != "x"}
    shared.update(consts)
    in_maps = []
    for c in range(n_cores):
        m = dict(shared)
        m["x"] = x[c * n_seq:(c + 1) * n_seq]
        in_maps.append(m)
    res = run_bass_kernel_spmd(nc, in_maps, core_ids=list(range(n_cores)))
    return np.concatenate([r["out"] for r in res.results], axis=0).astype(np.float32)
```

```python
import numpy as np
import concourse.bass as bass
import concourse.mybir as mybir
from concourse.bass_utils import run_bass_kernel_spmd
from contextlib import ExitStack

F32 = mybir.dt.float32
BF16 = mybir.dt.bfloat16
AF = mybir.ActivationFunctionType
ALU = mybir.AluOpType

D = 1024
DFF = 2816
SEQ = 4096
NMETA = 16
NT = SEQ + NMETA
EPS = 1e-6
NSLOT = 4
TILES = [(SEQ, NMETA)] + [(i * 512, 512) for i in range(8)]
SUBS = [(SEQ, NMETA)] + [(i * 128, 128) for i in range(32)]


class Buf:
    __slots__ = ("name", "w", "r")

    def __init__(self, name):
        self.name = name
        self.w = None
        self.r = []


class DSem:
    __slots__ = ("sem", "cnt")

    def __init__(self, sem):
        self.sem = sem
        self.cnt = 0


class Sched:
    def __init__(self, nc, es):
        self.nc = nc
        self.es = es
        self.eng = {"pe": nc.tensor, "act": nc.scalar, "dve": nc.vector,
                    "pool": nc.gpsimd, "sp": nc.sync}
        self.esem = {k: es.enter_context(nc.semaphore("e_" + k)) for k in self.eng}
        self.ecnt = {k: 0 for k in self.eng}
        self.seen = {k: {} for k in self.eng}
        self.dsems = []
        self.n_ins = 0

    def dsem(self, name):
        d = DSem(self.es.enter_context(self.nc.semaphore("d_" + name)))
        self.dsems.append(d)
        return d

    def _wait(self, e, deps):
        best = {}
        for (sem, val) in deps:
            k = sem.num
            if k not in best or best[k][1] < val:
                best[k] = (sem, val)
        for k, (sem, val) in best.items():
            if e == "pe" and k == self.esem["pe"].num:
                continue
            if self.seen[e].get(k, 0) < val:
                self.eng[e].wait_ge(sem, val)
                self.seen[e][k] = val
                self.n_ins += 1

    @staticmethod
    def _deps(reads, writes):
        deps = []
        for b in reads:
            if b.w is not None:
                deps.append(b.w)
        for b in writes:
            if b.w is not None:
                deps.append(b.w)
            deps.extend(b.r)
        return deps

    @staticmethod
    def _compact(ticks):
        best = {}
        for (sem, val) in ticks:
            k = sem.num
            if k not in best or best[k][1] < val:
                best[k] = (sem, val)
        return list(best.values())

    def _mark(self, tick, reads, writes):
        for b in reads:
            b.r.append(tick)
            if len(b.r) > 32:
                b.r = self._compact(b.r)
        for b in writes:
            b.w = tick
            b.r = []

    def op(self, e, fn, reads=(), writes=(), inc=True):
        self._wait(e, self._deps(reads, writes))
        ins = fn(self.eng[e])
        self.n_ins += 1
        if inc:
            self.ecnt[e] += 1
            ins.then_inc(self.esem[e], 1)
            tick = (self.esem[e], self.ecnt[e])
        else:
            tick = (self.esem[e], self.ecnt[e] + 1)
        self._mark(tick, reads, writes)
        return ins

    def acquire(self, e, reads=(), writes=()):
        self._wait(e, self._deps(reads, writes))

    def dma(self, q, out, in_, ds, reads=(), writes=()):
        self._wait(q, self._deps(reads, writes))
        ins = self.eng[q].dma_start(out=out, in_=in_)
        ds.cnt += 16
        ins.then_inc(ds.sem, 16)
        self.n_ins += 1
        self._mark((ds.sem, ds.cnt), reads, writes)
        return ins

    def barrier(self):
        for e in self.eng:
            deps = [(self.esem[k], self.ecnt[k]) for k in self.eng if self.ecnt[k] > 0]
            deps += [(d.sem, d.cnt) for d in self.dsems if d.cnt > 0]
            for (sem, val) in deps:
                if self.seen[e].get(sem.num, 0) < val:
                    self.eng[e].wait_ge(sem, val)
                    self.seen[e][sem.num] = val
                    self.n_ins += 1


class TT:
    _cache = {}

    def __init__(self, S, es, name, shape, dt, nb=1, dsem=False, semname=None):
        self.t = es.enter_context(S.nc.sbuf_tensor(name, shape, dt))
        self.b = [Buf(f"{name}{i}") for i in range(nb)]
        semname = semname or name
        nsem = 0 if not dsem else (nb if dsem == "per" else 1)
        self.dsl = []
        for i in range(nsem):
            key = (id(S), f"{semname}{i}")
            if key not in TT._cache:
                TT._cache[key] = S.dsem(f"{semname}{i}")
            self.dsl.append(TT._cache[key])
        self.ds = self.dsl[0] if self.dsl else None
        self.ps = 1
        for s in shape[1:]:
            self.ps *= s


def weight_keys(n_seq):
    ks = []

    def ffn(l, f):
        for j in range(11):
            ks.append(("gu", l, f, j))
        for m in range(8):
            ks.append(("dn", l, f, m))

    for s in range(n_seq):
        for _ in TILES:
            ffn(0, 1)
            for j in range(6):
                ks.append(("m512", "gla_w_in", j))
        for _ in TILES:
            for j in range(2):
                ks.append(("m512", "gla_w_out", j))
            ffn(0, 2)
            ffn(1, 1)
            for j in range(3):
                ks.append(("m512", "attn_w_in", j))
        for _ in TILES[1:]:
            for j in range(2):
                ks.append(("m512", "attn_w_out", j))
            ffn(1, 2)
    return ks


def build(n_seq=2, stop_after="D", dbg=False):
    nc = bass.Bass("TRN2", target_bir_lowering=False)

    def din(name, shape):
        return nc.dram_tensor(name, list(shape), F32, kind="ExternalInput").ap()

    x = din("x", [n_seq, SEQ, D])
    meta_tokens = din("meta_tokens", [NMETA, D])
    norm_ffn1 = din("norm_ffn1", [2, D]); norm_mix = din("norm_mix", [2, D]); norm_ffn2 = din("norm_ffn2", [2, D])
    norm_final = din("norm_final", [D])
    Wg = {1: din("ffn1_w_gate", [2, D, DFF]), 2: din("ffn2_w_gate", [2, D, DFF])}
    Wu = {1: din("ffn1_w_up", [2, D, DFF]), 2: din("ffn2_w_up", [2, D, DFF])}
    Wd = {1: din("ffn1_w_down", [2, DFF, D]), 2: din("ffn2_w_down", [2, DFF, D])}
    Wm = {"gla_w_in": din("gla_w_in", [1, D, 3072])[0], "gla_w_out": din("gla_w_out", [1, D, D])[0],
          "attn_w_in": din("attn_w_in", [1, D, 1536])[0], "attn_w_out": din("attn_w_out", [1, D, D])[0]}
    gate_w1 = din("gla_gate_w1", [1, 2, D, 16])[0]
    gate_w2 = din("gla_gate_w2", [1, 2, 16, 512])[0]
    gate_b = din("gla_gate_b", [1, 2, 512])[0]
    head_norm = din("gla_head_norm", [1, 256])[0]
    q_norm = din("attn_q_norm", [1, 128])[0]
    k_norm = din("attn_k_norm", [1, 128])[0]
    c_ident = din("c_ident", [128, 128]); c_maskf = din("c_maskf", [128, 128]); c_maskb = din("c_maskb", [128, 128])
    c_scan = din("c_scan", [128, 512]); c_pmt = din("c_pmt", [128, 128])
    c_cos = din("c_cos", [128, NT]); c_sin = din("c_sin", [128, NT])

    out = nc.dram_tensor("out", [n_seq, SEQ, D], F32, kind="ExternalOutput").ap()
    WGU = {(l, f): nc.dram_tensor(f"wgu_{l}_{f}", [11, 128, 4096], BF16, kind="Internal").ap() for l in range(2) for f in (1, 2)}
    WDN = {(l, f): nc.dram_tensor(f"wdn_{l}_{f}", [8, 128, 2816], BF16, kind="Internal").ap() for l in range(2) for f in (1, 2)}
    WMB = {k: nc.dram_tensor(f"wmb_{k}", [v.shape[1] // 512, 128, 4096], BF16, kind="Internal").ap() for k, v in Wm.items()}
    wjob = {}
    skind = "ExternalOutput" if dbg else "Internal"

    def scr(name, shape, dt):
        return [nc.dram_tensor(f"{name}{s}", list(shape), dt, kind=skind).ap() for s in range(n_seq)]

    H1 = scr("H1_", [D, NT], F32); QD = scr("QD_", [2, 128, 33, 4, 128], BF16); KI = scr("KI_", [2, 128, 33, 4, 128], BF16)
    KE = scr("KE_", [2, NT, 512], BF16); VV = scr("VV_", [NT, 1024], BF16); SR = scr("SR_", [D, NT], F32)
    OO = scr("OO_", [2, 128, 33, 8, 128], F32)
    Q2 = scr("Q2_", [D, NT], BF16); H4 = scr("H4_", [D, NT], F32)

    with ExitStack() as es, nc.allow_non_contiguous_dma(reason="small param vectors"):
        S = Sched(nc, es)
        sb = lambda name, shape, dt, nb=1, dsem=False: TT(S, es, name, shape, dt, nb, dsem)

        ident_f = sb("ident_f", [128, 128], F32, dsem=True)
        ident_b = sb("ident_b", [128, 128], BF16)
        ones_b = sb("ones_b", [128, 128], BF16)
        maskf = sb("maskf", [128, 128], F32, dsem=True); maskb = sb("maskb", [128, 128], F32, dsem=True)
        scanm = sb("scanm", [128, 512], F32, dsem=True)
        pmt = sb("pmt", [128, 128], F32, dsem=True)
        neghalf = sb("neghalf", [128, 512], F32)
        ones_f = sb("ones_f", [128, 128], F32)
        gcol = sb("gcol", [128, 7, 8], F32, dsem=True)
        hncol = sb("hncol", [128, 2], F32, dsem=True)
        qkcol = sb("qkcol", [128, 2], F32, dsem=True)
        ngb = sb("ngb", [128, 2, 4], F32, dsem=True)
        w1 = sb("w1", [128, 2, 8, 16], BF16, dsem=True)
        w2 = sb("w2", [16, 2, 512], BF16, dsem=True)
        dec = sb("dec", [128, 2, 4, 65], F32)
        cst = [ident_f.b[0], ident_b.b[0], ones_b.b[0], maskf.b[0], maskb.b[0], scanm.b[0], pmt.b[0],
               neghalf.b[0], gcol.b[0], hncol.b[0], qkcol.b[0], ngb.b[0], w1.b[0], w2.b[0]]

        S.dma("sp", ident_f.t[:], c_ident, ident_f.ds, writes=ident_f.b)
        S.dma("sp", maskf.t[:], c_maskf, maskf.ds, writes=maskf.b)
        S.dma("sp", maskb.t[:], c_maskb, maskb.ds, writes=maskb.b)
        S.dma("sp", scanm.t[:], c_scan, scanm.ds, writes=scanm.b)
        S.dma("sp", pmt.t[:], c_pmt, pmt.ds, writes=pmt.b)
        gl = [norm_ffn1[0], norm_mix[0], norm_ffn2[0], norm_ffn1[1], norm_mix[1], norm_ffn2[1], norm_final]
        for i, g in enumerate(gl):
            S.dma("sp", gcol.t[:, i, :], g.rearrange("(c p) -> p c", p=128), gcol.ds, writes=gcol.b)
        S.dma("sp", hncol.t[:], head_norm.rearrange("(c p) -> p c", p=128), hncol.ds, writes=hncol.b)
        S.dma("sp", qkcol.t[:, 0:1], q_norm.rearrange("(c p) -> p c", p=128), qkcol.ds, writes=qkcol.b)
        S.dma("sp", qkcol.t[:, 1:2], k_norm.rearrange("(c p) -> p c", p=128), qkcol.ds, writes=qkcol.b)
        for n in range(2):
            S.dma("sp", ngb.t[:, n, :], gate_b[n].rearrange("(h p) -> p h", p=128), ngb.ds, writes=ngb.b)
            S.dma("pool", w1.t[:, n], gate_w1[n].rearrange("(kc p) r -> p kc r", p=128), w1.ds, writes=w1.b)
            S.dma("pool", w2.t[:, n, :], gate_w2[n], w2.ds, writes=w2.b)
        S.op("dve", lambda e: e.tensor_copy(ident_b.t[:], ident_f.t[:]), reads=ident_f.b, writes=ident_b.b)
        S.op("dve", lambda e: e.memset(ones_b.t[:], 1.0), writes=ones_b.b)
        S.op("dve", lambda e: e.memset(neghalf.t[:], -0.5), writes=neghalf.b)
        S.op("dve", lambda e: e.memset(ones_f.t[:], 1.0), writes=ones_f.b)
        S.op("dve", lambda e: e.tensor_scalar(out=ngb.t[:], in0=ngb.t[:], scalar1=-1.0, scalar2=None, op0=ALU.mult),
             reads=ngb.b, writes=ngb.b)
        S.op("dve", lambda e: e.tensor_scalar(out=qkcol.t[:, 0:1], in0=qkcol.t[:, 0:1], scalar1=128.0 ** -0.5,
                                              scalar2=None, op0=ALU.mult), reads=qkcol.b, writes=qkcol.b)

        class View:
            def __init__(self, t, b, ds):
                self.t, self.b, self.ds = t, b, ds

        def make_jobs(spec, gu_n, dn_n, m_n):
            jobs = []
            for it_ in spec:
                if it_[0] == "gu":
                    _, l, f = it_
                    for j0 in range(0, 11, gu_n):
                        nj = min(gu_n, 11 - j0)
                        jobs.append(dict(
                            keys=[("gu", l, f, j) for j in range(j0, j0 + nj)], rows=[(g, kc) for g in range(2) for kc in range(8)], ncols=nj * 256,
                            src=lambda r, l=l, f=f, j0=j0, nj=nj: (Wg[f][l] if r[0] == 0 else Wu[f][l])[r[1] * 128:(r[1] + 1) * 128, j0 * 256:(j0 + nj) * 256],
                            outv=lambda o, r, nj=nj: o.t[:, 0:nj * 4096].rearrange("p (j e) -> p j e", e=4096)[:, :, r[0] * 2048 + r[1] * 256:r[0] * 2048 + (r[1] + 1) * 256],
                            inv=lambda st, nj=nj: st.t[:, 0:nj * 256].rearrange("p (j c) -> p j c", c=256),
                            dst=WGU[(l, f)][j0:j0 + nj].rearrange("j p e -> p j e"), esz=(nj, 4096)))
                elif it_[0] == "dn":
                    _, l, f = it_
                    for m0 in range(0, 8, dn_n):
                        nm = min(dn_n, 8 - m0)
                        jobs.append(dict(
                            keys=[("dn", l, f, m) for m in range(m0, m0 + nm)], rows=list(range(22)), ncols=nm * 128,
                            src=lambda r, l=l, f=f, m0=m0, nm=nm: Wd[f][l][r * 128:(r + 1) * 128, m0 * 128:(m0 + nm) * 128],
                            outv=lambda o, r, nm=nm: o.t[:, 0:nm * 2816].rearrange("p (m e) -> p m e", e=2816)[:, :, r * 128:(r + 1) * 128],
                            inv=lambda st, nm=nm: st.t[:, 0:nm * 128].rearrange("p (m c) -> p m c", c=128),
                            dst=WDN[(l, f)][m0:m0 + nm].rearrange("m p e -> p m e"), esz=(nm, 2816)))
                else:
                    _, name = it_
                    npc_all = Wm[name].shape[1] // 512
                    for j0 in range(0, npc_all, m_n):
                        nj = min(m_n, npc_all - j0)
                        jobs.append(dict(
                            keys=[("m512", name, j) for j in range(j0, j0 + nj)], rows=list(range(8)), ncols=nj * 512,
                            src=lambda r, name=name, j0=j0, nj=nj: Wm[name][r * 128:(r + 1) * 128, j0 * 512:(j0 + nj) * 512],
                            outv=lambda o, r, nj=nj: o.t[:, 0:nj * 4096].rearrange("p (j e) -> p j e", e=4096)[:, :, r * 512:(r + 1) * 512],
                            inv=lambda st, nj=nj: st.t[:, 0:nj * 512].rearrange("p (j c) -> p j c", c=512),
                            dst=WMB[name][j0:j0 + nj].rearrange("j p e -> p j e"), esz=(nj, 4096)))
            return jobs

        def conv_gen(jobs, stages, outs, engs, store_q):
            items = [(ji, r) for ji, job in enumerate(jobs) for r in job["rows"]]
            ns = len(stages)
            ce = [0]

            def load(i):
                ji, r = items[i]
                st = stages[i % ns]
                S.dma("sp", st.t[:, 0:jobs[ji]["ncols"]], jobs[ji]["src"](r), st.ds, writes=st.b)

            for i in range(min(ns, len(items))):
                load(i)
            for i, (ji, r) in enumerate(items):
                job = jobs[ji]
                st = stages[i % ns]
                o = outs[ji % len(outs)]
                e = engs[ce[0] % len(engs)]
                ce[0] += 1
                if e == "act":
                    S.op("act", lambda en: en.copy(job["outv"](o, r), job["inv"](st)), reads=st.b, writes=o.b)
                else:
                    S.op(e, lambda en: en.tensor_copy(job["outv"](o, r), job["inv"](st)), reads=st.b, writes=o.b)
                if i + ns < len(items):
                    load(i + ns)
                if r == job["rows"][-1]:
                    jb = Buf("job")
                    nj, e_ = job["esz"]
                    S.dma(store_q, job["dst"], o.t[:, 0:nj * e_].rearrange("p (j e) -> p j e", e=e_), o.ds, reads=o.b, writes=[jb])
                    for k in job["keys"]:
                        wjob[k] = jb
                yield

        with ExitStack() as esP:
            stg = [TT(S, esP, f"cstg{i}", [128, 3072], F32, 1, True) for i in range(4)]
            ost_p = [TT(S, esP, f"cost{i}", [128, 24576], BF16, 1, True) for i in range(2)]
            for _ in conv_gen(make_jobs([("gu", 0, 1), ("dn", 0, 1), ("m", "gla_w_in"), ("m", "gla_w_out"), ("gu", 0, 2), ("dn", 0, 2),
                                         ("gu", 1, 1), ("dn", 1, 1), ("m", "attn_w_in"), ("m", "attn_w_out"), ("gu", 1, 2), ("dn", 1, 2)], 6, 8, 6),
                              stg, ost_p, ("act", "dve", "pool"), "pool"):
                pass
            S.barrier()
        deferred = []

        psb = [es.enter_context(nc.psum_tensor(f"ps{i}", [128, 512], F32)) for i in range(8)]
        psB = [Buf(f"ps{i}") for i in range(8)]
        pst = [0, 8]

        def psum():
            i = pst[0] % pst[1]
            pst[0] += 1
            return psb[i], psB[i]

        def psum_peek(n):
            return [psB[(pst[0] + k) % pst[1]] for k in range(min(n, pst[1]))]

        slots = [sb(f"wslot{i}", [128, 4096], BF16, dsem=True) for i in range(NSLOT)]
        wkeys = weight_keys(n_seq)
        wst = {"issued": 0, "next": 0}

        def issue_piece(i):
            key = wkeys[i]
            sl = slots[i % NSLOT]
            if key[0] == "gu":
                _, l, f, j = key
                S.dma("sp", sl.t[:, :], WGU[(l, f)][j], sl.ds, reads=[wjob[key]], writes=sl.b)
            elif key[0] == "dn":
                _, l, f, m = key
                S.dma("sp", sl.t[:, 0:2816], WDN[(l, f)][m], sl.ds, reads=[wjob[key]], writes=sl.b)
            else:
                _, name, j = key
                S.dma("sp", sl.t[:, :], WMB[name][j], sl.ds, reads=[wjob[key]], writes=sl.b)

        def wget(key):
            i = wst["next"]
            assert wkeys[i] == key, (i, wkeys[i], key)
            wst["next"] += 1
            while wst["issued"] < min(len(wkeys), i + NSLOT):
                if wkeys[wst["issued"]] not in wjob:
                    assert wst["issued"] > i
                    break
                issue_piece(wst["issued"])
                wst["issued"] += 1
            return slots[i % NSLOT]

        hT = sb("hT", [128, 8, 512], F32, nb=8)
        hn = sb("hn", [128, 8, 512], BF16, nb=8)
        act = sb("act", [128, 22, 512], BF16, nb=22)
        xin = sb("xin", [128, 4, 1024], F32, nb=8, dsem=True)
        ms = sb("ms", [128, 512], F32); rstd = sb("rstd", [128, 512], F32)
        sg = [sb(f"sg{i}", [128, 512], F32) for i in range(2)]
        hst_ds = S.dsem("hTst")
        hld_ds = S.dsem("hTld")
        xst_ds = S.dsem("xinst")
        rr = {"sg": 0, "cp": 0}

        def evac(out_ap, in_ap, reads, writes):
            rr["cp"] += 1
            if rr["cp"] % 2:
                S.op("act", lambda e: e.copy(out_ap, in_ap), reads=reads, writes=writes)
            else:
                S.op("dve", lambda e: e.tensor_copy(out_ap, in_ap), reads=reads, writes=writes)

        def rmsnorm(T, gi):
            for c in range(8):
                S.op("act", lambda e: e.activation(out=act.t[:, c, :T], in_=hT.t[:, c, :T], func=AF.Square),
                     reads=[hT.b[c]], writes=[act.b[c]])
            ps, pb = psum()
            for c in range(8):
                S.op("pe", lambda e: e.matmul(ps[:, :T], ones_b.t[:], act.t[:, c, :T], start=(c == 0), stop=(c == 7)),
                     reads=[ones_b.b[0], act.b[c]], writes=[pb], inc=(c == 7))
            S.op("act", lambda e: e.activation(out=ms.t[:, :T], in_=ps[:, :T], func=AF.Ln, scale=1.0 / D, bias=EPS), reads=[pb], writes=ms.b)
            S.op("act", lambda e: e.activation(out=rstd.t[:, :T], in_=ms.t[:, :T], func=AF.Exp, scale=-0.5), reads=ms.b, writes=rstd.b)
            for c in range(8):
                S.op("dve", lambda e: e.scalar_tensor_tensor(out=hn.t[:, c, :T], in0=hT.t[:, c, :T],
                                                             scalar=gcol.t[:, gi, c:c + 1], in1=rstd.t[:, :T],
                                                             op0=ALU.mult, op1=ALU.mult),
                     reads=[hT.b[c], gcol.b[0], rstd.b[0]], writes=[hn.b[c]])

        def ffn(T, l, f):
            rmsnorm(T, 3 * l + (0 if f == 1 else 2))
            for j in range(11):
                sl = wget(("gu", l, f, j))
                v = sl.t[:].rearrange("p (g k c) -> p g k c", g=2, k=8)
                S.acquire("pe", reads=sl.b, writes=psum_peek(4))
                for half in range(2):
                    fc = 2 * j + half
                    pg, pgb = psum()
                    pu, pub = psum()
                    for g, (pp, ppb) in enumerate(((pg, pgb), (pu, pub))):
                        for kc in range(8):
                            S.op("pe", lambda e: e.matmul(pp[:, :T], v[:, g, kc, half * 128:(half + 1) * 128],
                                                          hn.t[:, kc, :T], start=(kc == 0), stop=(kc == 7)),
                                 reads=[sl.b[0], hn.b[kc]], writes=[ppb], inc=(kc == 7))
                    s_ = sg[rr["sg"] % 2]
                    rr["sg"] += 1
                    S.op("act", lambda e: e.activation(out=s_.t[:, :T], in_=pg[:, :T], func=AF.Silu),
                         reads=[pgb], writes=s_.b)
                    S.op("dve", lambda e: e.tensor_tensor(out=act.t[:, fc, :T], in0=s_.t[:, :T], in1=pu[:, :T], op=ALU.mult),
                         reads=[s_.b[0], pub], writes=[act.b[fc]])
            for m in range(8):
                sl = wget(("dn", l, f, m))
                v = sl.t[:, 0:22 * 128].rearrange("p (k c) -> p k c", k=22)
                py, pyb = psum()
                for fc in range(22):
                    S.op("pe", lambda e: e.matmul(py[:, :T], v[:, fc, :], act.t[:, fc, :T], start=(fc == 0), stop=(fc == 21)),
                         reads=[sl.b[0], act.b[fc]], writes=[pyb], inc=(fc == 21))
                S.op("dve", lambda e: e.scalar_tensor_tensor(out=hT.t[:, m, :T], in0=py[:, :T], scalar=0.5,
                                                             in1=hT.t[:, m, :T], op0=ALU.mult, op1=ALU.add),
                     reads=[pyb, hT.b[m]], writes=[hT.b[m]])

        def proj_fm(T, sl, ncol, col0, consume):
            v = sl.t[:].rearrange("p (k c) -> p k c", k=8)
            S.acquire("pe", reads=sl.b, writes=psum_peek(ncol))
            for i in range(ncol):
                ps, pb = psum()
                for kc in range(8):
                    S.op("pe", lambda e: e.matmul(ps[:, :T], v[:, kc, col0 + i * 128:col0 + (i + 1) * 128], hn.t[:, kc, :T],
                                                  start=(kc == 0), stop=(kc == 7)),
                         reads=[sl.b[0], hn.b[kc]], writes=[pb], inc=(kc == 7))
                consume(i, ps, pb)

        def store_hT(T, t0, dst):
            S.dma("pool", dst.rearrange("(c p) t -> p c t", p=128)[:, :, t0:t0 + T], hT.t[:, :, :T], hst_ds, reads=hT.b)

        def load_hT(T, t0, src):
            S.dma("sp", hT.t[:, :, :T], src.rearrange("(c p) t -> p c t", p=128)[:, :, t0:t0 + T], hld_ds, writes=hT.b)

        def subs_of(T):
            return [(i * 128, 128) for i in range(T // 128)] if T >= 128 else [(0, T)]

        for s in range(n_seq):
            with ExitStack() as esA:
                sbA = lambda name, shape, dt, nb=1, dsem=False: TT(S, esA, f"{name}_{s}", shape, dt, nb, dsem, semname=name)
                qf = sbA("qf", [128, 4, 512], F32, nb=4); kf = sbA("kf", [128, 4, 512], F32, nb=4)
                ub = sbA("ub", [16, 2, 512], BF16, nb=2)
                tA = sbA("tA", [128, 512], F32); nl = sbA("nl", [128, 2, 512], F32, nb=2)
                Pc = sbA("Pc", [128, 2, 512], F32, nb=2); Ea = sbA("Ea", [128, 512], F32); Eb = sbA("Eb", [128, 512], F32)
                Pe = sbA("Pe", [128, 512], F32); t1 = sbA("t1", [128, 512], F32); t2 = sbA("t2", [128, 512], F32)
                Gs = sbA("Gs", [128, 2, 8], F32, nb=2)
                qdst = sbA("qdst", [128, 2, 4, 4, 128], BF16, nb=1, dsem=True)
                kist = sbA("kist", [128, 2, 4, 4, 128], BF16, nb=1, dsem=True)
                keT = sbA("keT", [128, 2, 2, 512], BF16, nb=2)
                ketok = sbA("ketok", [128, 2, 2, 4, 128], BF16, nb=2, dsem="per")
                vtok = sbA("vtok", [128, 2, 4, 512], BF16, nb=2, dsem="per")
                srst = sbA("srst", [128, 2, 4, 512], F32, nb=2, dsem="per")
                for ti, (t0, T) in enumerate(TILES):
                    subs = subs_of(T)
                    nch = max(1, T // 64)
                    clen = min(64, T)
                    ci0 = t0 // 64
                    def load_x(ti_):
                        t0_, T_ = TILES[ti_]
                        if T_ == NMETA:
                            S.dma("sp", xin.t[0:T_, 0, :], meta_tokens, xin.ds, writes=xin.b)
                        else:
                            S.dma("sp", xin.t[:, :, :], x[s, t0_:t0_ + T_, :].rearrange("(u p) d -> p u d", p=128), xin.ds,
                                  writes=xin.b)
                    if ti == 0:
                        load_x(0)
                    for c in range(8):
                        ps, pb = psum()
                        for u, (o, n) in enumerate(subs):
                            S.op("pe", lambda e: e.transpose(ps[:, o:o + n], xin.t[0:n, u, c * 128:(c + 1) * 128],
                                                             ident_f.t[0:n, 0:n]),
                                 reads=[xin.b[0], ident_f.b[0]], writes=[pb], inc=(u == len(subs) - 1))
                        evac(hT.t[:, c, :T], ps[:, :T], [pb], [hT.b[c]])
                    if ti + 1 < len(TILES):
                        load_x(ti + 1)
                    ffn(T, 0, 1)
                    store_hT(T, t0, H1[s])
                    rmsnorm(T, 1)
                    for n in range(2):
                        ps, pb = psum()
                        for kc in range(8):
                            S.op("pe", lambda e: e.matmul(ps[0:16, :T], w1.t[:, n, kc, :], hn.t[:, kc, :T],
                                                          start=(kc == 0), stop=(kc == 7)),
                                 reads=[w1.b[0], hn.b[kc]], writes=[pb], inc=(kc == 7))
                        evac(ub.t[0:16, n, :T], ps[0:16, :T], [pb], [ub.b[n]])
                    sl = wget(("m512", "gla_w_in", 0))
                    proj_fm(T, sl, 4, 0, lambda i, ps, pb: S.op(
                        "act", lambda e: e.mul(qf.t[:, i, :T], ps[:, :T], 128.0 ** -0.5), reads=[pb], writes=[qf.b[i]]))
                    sl = wget(("m512", "gla_w_in", 1))
                    proj_fm(T, sl, 4, 0, lambda i, ps, pb: evac(kf.t[:, i, :T], ps[:, :T], [pb], [kf.b[i]]))
                    def proj_piece(k):
                        if k < 2:
                            vp = k
                            sl = wget(("m512", "gla_w_in", 2 + vp))
                            v = sl.t[:].rearrange("p (k c) -> p k c", k=8)
                            for u, (o, n) in enumerate(subs):
                                ps, pb = psum()
                                for kc in range(8):
                                    S.op("pe", lambda e: e.matmul(ps[0:n, :], hn.t[:, kc, o:o + n], v[:, kc, :],
                                                                  start=(kc == 0), stop=(kc == 7)),
                                         reads=[sl.b[0], hn.b[kc]], writes=[pb], inc=(kc == 7))
                                evac(vtok.t[0:n, vp, u, :], ps[0:n, :], [pb], [vtok.b[vp]])
                            dst = VV[s][t0:t0 + T, vp * 512:(vp + 1) * 512]
                            if T >= 128:
                                S.dma("pool", dst.rearrange("(u p) c -> p u c", p=128), vtok.t[:, vp, :, :], vtok.dsl[vp], reads=[vtok.b[vp]])
                            else:
                                S.dma("pool", dst, vtok.t[0:T, vp, 0, :], vtok.dsl[vp], reads=[vtok.b[vp]])

                        else:
                            rp = k - 2
                            sl = wget(("m512", "gla_w_in", 4 + rp))
                            proj_fm(T, sl, 4, 0, lambda i, ps, pb: S.op(
                                "act", lambda e: e.activation(out=srst.t[:, rp, i, :T], in_=ps[:, :T], func=AF.Silu),
                                reads=[pb], writes=[srst.b[rp]]))
                            S.dma("pool", SR[s].rearrange("(c p) t -> p c t", p=128)[:, rp * 4:(rp + 1) * 4, t0:t0 + T],
                                  srst.t[:, rp, :, :T], srst.dsl[rp], reads=[srst.b[rp]])

                    def gate_math(h):
                        par = h % 2

                        def bc(tt, off):
                            return bass.AP(tt.t, off + clen - 1, [[tt.ps, 128], [clen, nch], [0, clen]])

                        def ch(tt, off):
                            return bass.AP(tt.t, off + clen - 1, [[tt.ps, 128], [clen, nch]])

                        def v3(ap):
                            return ap.rearrange("p (c j) -> p c j", j=clen)

                        wid = min(128, T)
                        nsb = len(subs)

                        def d4(tt, d_):
                            return tt.t[:, d_, 0:nsb, h, 0:wid]

                        def u3(ap):
                            return ap.rearrange("p (u j) -> p u j", j=wid)

                        for n in range(2):
                            ps, pb = psum()
                            S.op("pe", lambda e: e.matmul(ps[:, :T], w2.t[0:16, n, h * 128:(h + 1) * 128], ub.t[0:16, n, :T],
                                                          start=True, stop=True), reads=[w2.b[0], ub.b[n]], writes=[pb])
                            S.op("act", lambda e: e.activation(out=tA.t[:, :T], in_=ps[:, :T], func=AF.Exp, scale=-1.0,
                                                               bias=ngb.t[:, n, h:h + 1]),
                                 reads=[pb, ngb.b[0]], writes=tA.b)
                            S.op("act", lambda e: e.activation(out=nl.t[:, n, :T], in_=tA.t[:, :T], func=AF.Ln, bias=1.0),
                                 reads=tA.b, writes=[nl.b[n]])
                            S.op("dve", lambda e: e.tensor_tensor_scan(out=Pc.t[:, n, :T], data0=scanm.t[:, :T],
                                                                       data1=nl.t[:, n, :T], initial=0.0,
                                                                       op0=ALU.mult, op1=ALU.add),
                                 reads=[scanm.b[0], nl.b[n]], writes=[Pc.b[n]])
                        S.op("act", lambda e: e.activation(out=Ea.t[:, :T], in_=Pc.t[:, 0, :T], func=AF.Exp, scale=-1.0 / 16),
                             reads=[Pc.b[0]], writes=Ea.b)
                        S.op("act", lambda e: e.activation(out=Eb.t[:, :T], in_=Pc.t[:, 0, :T], func=AF.Exp, scale=1.0 / 16),
                             reads=[Pc.b[0]], writes=Eb.b)
                        S.op("dve", lambda e: e.tensor_tensor(out=d4(qdst, 0), in0=u3(qf.t[:, h, :T]), in1=u3(Ea.t[:, :T]), op=ALU.mult),
                             reads=[qf.b[h], Ea.b[0]], writes=qdst.b)
                        S.op("dve", lambda e: e.tensor_tensor(out=t1.t[:, :T], in0=kf.t[:, h, :T], in1=Eb.t[:, :T], op=ALU.mult),
                             reads=[kf.b[h], Eb.b[0]], writes=t1.b)
                        S.op("act", lambda e: e.copy(d4(kist, 0), u3(t1.t[:, :T])), reads=t1.b, writes=kist.b)
                        S.op("pool", lambda e: e.tensor_tensor(out=v3(keT.t[:, par, 0, :T]), in0=v3(t1.t[:, :T]), in1=bc(Ea, 0), op=ALU.mult),
                             reads=[t1.b[0], Ea.b[0]], writes=[keT.b[par]])
                        S.op("act", lambda e: e.copy(dec.t[:, 0, h, ci0:ci0 + nch], ch(Ea, 0)), reads=Ea.b, writes=dec.b)
                        S.op("dve", lambda e: e.tensor_tensor(out=Pe.t[:, :T], in0=Pc.t[:, 1, :T], in1=nl.t[:, 1, :T], op=ALU.subtract),
                             reads=[Pc.b[1], nl.b[1]], writes=Pe.b)
                        S.op("act", lambda e: e.activation(out=Ea.t[:, :T], in_=Pe.t[:, :T], func=AF.Exp, scale=-1.0 / 16),
                             reads=Pe.b, writes=Ea.b)
                        S.op("act", lambda e: e.activation(out=Eb.t[:, :T], in_=Pe.t[:, :T], func=AF.Exp, scale=1.0 / 16),
                             reads=Pe.b, writes=Eb.b)
                        S.op("act", lambda e: e.activation(out=Gs.t[:, 0, 0:nch], in_=ch(Pc, 512), func=AF.Exp, scale=-1.0 / 16),
                             reads=[Pc.b[1]], writes=[Gs.b[0]])
                        S.op("act", lambda e: e.activation(out=Gs.t[:, 1, 0:nch], in_=ch(Pc, 512), func=AF.Exp, scale=1.0 / 16),
                             reads=[Pc.b[1]], writes=[Gs.b[1]])
                        g1b = bass.AP(Gs.t, 0, [[16, 128], [1, nch], [0, clen]])
                        g2b = bass.AP(Gs.t, 8, [[16, 128], [1, nch], [0, clen]])
                        S.op("dve", lambda e: e.tensor_tensor(out=t1.t[:, :T], in0=qf.t[:, h, :T], in1=Eb.t[:, :T], op=ALU.mult),
                             reads=[qf.b[h], Eb.b[0]], writes=t1.b)
                        S.op("pool", lambda e: e.tensor_tensor(out=d4(qdst, 1), in0=v3(t1.t[:, :T]), in1=g1b, op=ALU.mult),
                             reads=[t1.b[0], Gs.b[0]], writes=qdst.b)
                        S.op("dve", lambda e: e.tensor_tensor(out=t2.t[:, :T], in0=kf.t[:, h, :T], in1=Ea.t[:, :T], op=ALU.mult),
                             reads=[kf.b[h], Ea.b[0]], writes=t2.b)
                        S.op("act", lambda e: e.copy(keT.t[:, par, 1, :T], t2.t[:, :T]), reads=t2.b, writes=[keT.b[par]])
                        S.op("pool", lambda e: e.tensor_tensor(out=d4(kist, 1), in0=v3(t2.t[:, :T]), in1=g2b, op=ALU.mult),
                             reads=[t2.b[0], Gs.b[1]], writes=kist.b)
                        S.op("act", lambda e: e.copy(dec.t[:, 1, h, ci0:ci0 + nch], Gs.t[:, 0, 0:nch]), reads=[Gs.b[0]], writes=dec.b)
                    def ke_transposes(h):
                        par = h % 2
                        for d_ in range(2):
                            ps, pb = psum()
                            pbf = ps[:].bitcast(BF16)
                            for u, (o, n) in enumerate(subs):
                                S.op("pe", lambda e: e.transpose(pbf[0:n, u * 128:(u + 1) * 128], keT.t[:, par, d_, o:o + n], ident_b.t[:, :]),
                                     reads=[keT.b[par], ident_b.b[0]], writes=[pb], inc=(u == len(subs) - 1))
                            nu = len(subs)
                            n0 = subs[0][1]
                            evac(ketok.t[0:n0, par, d_, 0:nu, :], pbf[0:n0, 0:nu * 128].rearrange("p (u c) -> p u c", c=128),
                                 [pb], [ketok.b[par]])
                            dst = KE[s][d_, t0:t0 + T, h * 128:(h + 1) * 128]
                            if T >= 128:
                                S.dma("pool", dst.rearrange("(u p) c -> p u c", p=128), ketok.t[:, par, d_, :, :], ketok.dsl[par], reads=[ketok.b[par]])
                            else:
                                S.dma("pool", dst, ketok.t[0:T, par, d_, 0, :], ketok.dsl[par], reads=[ketok.b[par]])
                    for h in range(4):
                        gate_math(h)
                        proj_piece(h)
                        if h >= 1:
                            ke_transposes(h - 1)
                    ke_transposes(3)
                    for d_ in range(2):
                        for tt_, dst_ in ((qdst, QD), (kist, KI)):
                            if T >= 128:
                                S.dma("pool", dst_[s][d_, :, t0 // 128:t0 // 128 + 4, :, :], tt_.t[:, d_, :, :, :], tt_.ds, reads=tt_.b)
                            else:
                                S.dma("pool", dst_[s][d_, :, 32, :, 0:T], tt_.t[:, d_, 0, :, 0:T], tt_.ds, reads=tt_.b)
                S.barrier()
            if stop_after == "A":
                break

            with ExitStack() as esG:
                sbG = lambda name, shape, dt, nb=1, dsem=False: TT(S, esG, f"{name}_{s}", shape, dt, nb, dsem, semname=name)
                Sst = sbG("Sst", [128, 2, 4, 256], F32, nb=8)
                Sbf = sbG("Sbf", [128, 3, 4, 2, 256], BF16, nb=24)
                qdl = [sbG(f"qdl{i}", [128, 4, 128], BF16, dsem=True) for i in range(3)]
                kil = [sbG(f"kil{i}", [128, 4, 128], BF16, dsem=True) for i in range(3)]
                kel = [sbG(f"kel{i}", [128, 512], BF16, dsem=True) for i in range(3)]
                keh = [sbG(f"keh{i}", [128, 512], BF16, dsem=True) for i in range(3)]
                vl = [sbG(f"vl{i}", [128, 1024], BF16, dsem=True) for i in range(3)]
                Am = [sbG(f"Am{i}", [128, 4, 128], BF16, nb=4) for i in range(2)]
                ost = [sbG(f"ost{i}", [128, 8, 128], F32, dsem=True) for i in range(2)]
                ofl = [sbG(f"ofl{i}", [128, 8, 128], F32, dsem=True) for i in range(3)]
                kvsb = [sbG(f"kvsb{i}", [128, 4, 512], F32, nb=4) for i in range(2)]
                gps = {"kv": [0, 1, 2, 3], "a": [4, 5], "o": [6, 7]}
                gpc = {"kv": 0, "a": 0, "o": 0}

                def psum_g(role):
                    i = gps[role][gpc[role] % len(gps[role])]
                    gpc[role] += 1
                    return psb[i], psB[i]
                cgen = None
                if s == 0 and deferred:
                    ost_g = [sbG(f"costg{i}", [128, 8448], BF16, dsem=True) for i in range(2)]
                    fx = xin.t[:].rearrange("p u d -> p (u d)")
                    fh = hT.t[:].rearrange("p c t -> p (c t)")
                    stg_g = [View(f_[:, i * 1024:(i + 1) * 1024], [Buf(f"cs{k}{i}")], S.dsem(f"cs{k}{i}"))
                             for k, f_ in enumerate((fx, fh)) for i in range(4)]
                    cgen = conv_gen(deferred, stg_g, ost_g, ("pool", "act"), "pool")

                def conv_step(k):
                    nonlocal_c = cgen
                    if nonlocal_c is None:
                        return
                    for _ in range(k):
                        if next(nonlocal_c, "done") == "done":
                            break
                for i in range(3):
                    S.op("dve", lambda e: e.memset(kel[i].t[:], 0.0), writes=kel[i].b)
                    S.op("dve", lambda e: e.memset(keh[i].t[:], 0.0), writes=keh[i].b)
                for d_ in range(2):
                    order = SUBS if d_ == 0 else SUBS[::-1]
                    mk = maskf if d_ == 0 else maskb
                    S.op("dve", lambda e: e.memset(Sst.t[:], 0.0), writes=Sst.b)
                    scur = [0, 0, 0, 0]
                    for h in range(4):
                        S.op("dve", lambda e: e.memset(Sbf.t[:, 0, h, 0, :], 0.0), writes=[Sbf.b[h * 2]])
                    def g_loads(it):
                        t0, n = order[it]
                        lp = it % 3
                        S.dma("sp", qdl[lp].t[:, :, 0:n], QD[s][d_, :, t0 // 128, :, 0:n], qdl[lp].ds, writes=qdl[lp].b)
                        S.dma("sp", kil[lp].t[:, :, 0:n], KI[s][d_, :, t0 // 128, :, 0:n], kil[lp].ds, writes=kil[lp].b)
                        if n == 128:
                            S.dma("sp", kel[lp].t[0:64, :], KE[s][d_, t0:t0 + 64, :], kel[lp].ds, writes=kel[lp].b)
                            S.dma("sp", keh[lp].t[64:128, :], KE[s][d_, t0 + 64:t0 + 128, :], keh[lp].ds, writes=keh[lp].b)
                        else:
                            S.dma("sp", kel[lp].t[0:n, :], KE[s][d_, t0:t0 + n, :], kel[lp].ds, writes=kel[lp].b)
                        S.dma("sp", vl[lp].t[0:n, :], VV[s][t0:t0 + n, :], vl[lp].ds, writes=vl[lp].b)
                        if d_ == 1:
                            S.dma("sp", ofl[lp].t[:, :, 0:n], OO[s][0, :, t0 // 128, :, 0:n], ofl[lp].ds, writes=ofl[lp].b)

                    gst = {}

                    def g_ctx(it):
                        t0, n = order[it]
                        par = it % 2
                        lp = it % 3
                        sbi = lambda p_, h_, k_: p_ * 8 + h_ * 2 + k_
                        p3 = it % 3
                        if n == 128:
                            chunks = [(0, 64, kel[lp], t0 // 64), (64, 64, keh[lp], t0 // 64 + 1)]
                            if d_ == 1:
                                chunks = chunks[::-1]
                            Kc = 128
                        else:
                            chunks = [(0, n, kel[lp], 64)]
                            Kc = n
                        return t0, n, par, lp, p3, sbi, chunks, Kc

                    def g_stage1(it):
                        t0, n, par, lp, p3, sbi, chunks, Kc = g_ctx(it)
                        pkvs, pas = [], []
                        nk = len(chunks)
                        for h in range(4):
                            pkv, pkvb = psum_g("kv")
                            for k, (co, cl, ket, ci) in enumerate(chunks):
                                S.op("pe", lambda e: e.matmul(pkv[:, k * 256:(k + 1) * 256], ket.t[0:Kc, h * 128:(h + 1) * 128],
                                                              vl[lp].t[0:Kc, h * 256:(h + 1) * 256], start=True, stop=True),
                                     reads=[ket.b[0], vl[lp].b[0]], writes=[pkvb], inc=(k == nk - 1))
                            S.op("act", lambda e: e.copy(kvsb[par].t[:, h, 0:nk * 256], pkv[:, 0:nk * 256]), reads=[pkvb], writes=[kvsb[par].b[h]])
                        pa, pab = psum_g("a")
                        for h in range(4):
                            S.op("pe", lambda e: e.matmul(pa[0:n, h * 128:h * 128 + n], kil[lp].t[:, h, 0:n], qdl[lp].t[:, h, 0:n], start=True, stop=True),
                                 reads=[kil[lp].b[0], qdl[lp].b[0]], writes=[pab], inc=(h == 3))
                        for k, (co, cl, ket, ci) in enumerate(chunks):
                            for h in range(4):
                                a_, b_ = scur[h], 1 - scur[h]
                                scur[h] = b_
                                S.op("dve", lambda e: e.scalar_tensor_tensor(out=Sst.t[:, b_, h, :], in0=Sst.t[:, a_, h, :],
                                                                             scalar=dec.t[:, d_, h, ci:ci + 1],
                                                                             in1=kvsb[par].t[:, h, k * 256:(k + 1) * 256],
                                                                             op0=ALU.mult, op1=ALU.add),
                                     reads=[Sst.b[a_ * 4 + h], dec.b[0], kvsb[par].b[h]], writes=[Sst.b[b_ * 4 + h]])
                                if k < len(chunks) - 1:
                                    tp, tk = p3, 1
                                else:
                                    tp, tk = (p3 + 1) % 3, 0
                                S.op("act", lambda e: e.copy(Sbf.t[:, tp, h, tk, :], Sst.t[:, b_, h, :]),
                                     reads=[Sst.b[b_ * 4 + h]], writes=[Sbf.b[sbi(tp, h, tk)]])
                            if k == 0:
                                for h in range(4):
                                    S.op("dve", lambda e: e.tensor_tensor(out=Am[par].t[0:n, h, 0:n], in0=pa[0:n, h * 128:h * 128 + n], in1=mk.t[0:n, 0:n], op=ALU.mult),
                                         reads=[pab, mk.b[0]], writes=[Am[par].b[h]])

                    def g_stage2(it):
                        t0, n, par, lp, p3, sbi, chunks, Kc = g_ctx(it)
                        for hb in range(2):
                            po, pob = psum_g("o")
                            for h in (2 * hb, 2 * hb + 1):
                                for dvc in range(2):
                                    c0 = ((h % 2) * 2 + dvc) * 128
                                    S.op("pe", lambda e: e.matmul(po[:, c0:c0 + n], vl[lp].t[0:n, h * 256 + dvc * 128:h * 256 + (dvc + 1) * 128],
                                                                  Am[par].t[0:n, h, 0:n], start=True, stop=False),
                                         reads=[vl[lp].b[0], Am[par].b[h]], writes=[pob], inc=False)
                                    for k, (co, cl, ket, ci) in enumerate(chunks):
                                        last = (k == len(chunks) - 1)
                                        S.op("pe", lambda e: e.matmul(po[:, c0 + co:c0 + co + cl], Sbf.t[:, p3, h, k, dvc * 128:(dvc + 1) * 128],
                                                                      qdl[lp].t[:, h, co:co + cl], start=False, stop=last),
                                             reads=[Sbf.b[sbi(p3, h, k)], qdl[lp].b[0]], writes=[pob], inc=last)
                            for h in (2 * hb, 2 * hb + 1):
                                for dvc in range(2):
                                    c0 = ((h % 2) * 2 + dvc) * 128
                                    if d_ == 0:
                                        evac(ost[par].t[:, h * 2 + dvc, 0:n], po[:, c0:c0 + n], [pob], ost[par].b)
                                    else:
                                        S.op("dve", lambda e: e.tensor_tensor(out=ost[par].t[:, h * 2 + dvc, 0:n], in0=po[:, c0:c0 + n],
                                                                              in1=ofl[lp].t[:, h * 2 + dvc, 0:n], op=ALU.add),
                                             reads=[pob, ofl[lp].b[0]], writes=ost[par].b)
                        S.dma("pool", OO[s][d_, :, t0 // 128, :, 0:n], ost[par].t[:, :, 0:n], ost[par].ds, reads=ost[par].b)
                        if it + 3 < len(order):
                            g_loads(it + 3)

                    for i_ in range(min(3, len(order))):
                        g_loads(i_)
                    g_stage1(0)
                    for it in range(len(order)):
                        if it + 1 < len(order):
                            g_stage1(it + 1)
                        g_stage2(it)
                        conv_step(7)
                    if d_ == 1:
                        conv_step(10 ** 6)
                    S.barrier()
                S.barrier()
            if stop_after == "G":
                break

            esKV = ExitStack()
            KT = TT(S, esKV, f"KT_{s}", [128, 2, NT], BF16, 2)
            VR = TT(S, esKV, f"VR_{s}", [128, 33, 256], BF16, 1)
            with ExitStack() as esB:
                sbB = lambda name, shape, dt, nb=1, dsem=False: TT(S, esB, f"{name}_{s}", shape, dt, nb, dsem, semname=name)
                srt = sbB("srt", [128, 8, 512], F32, nb=8, dsem=True)
                oft = xin
                qkt = []
                for i in range(3):
                    xs_ = sbB(f"xs{i}", [128, 512], F32)
                    ms_ = sbB(f"msq{i}", [128, 512], F32)
                    qkt.append(dict(xs=xs_, r2=xs_, xn=sbB(f"xn{i}", [128, 512], F32), r1=sbB(f"r1{i}", [128, 512], F32),
                                    sqh=sbB(f"sqh{i}", [128, 512], BF16), ms=ms_, rstd=ms_))
                qki = [0]
                Ct = sbB("Ct", [128, 512], F32, dsem=True); St = sbB("St", [128, 512], F32, dsem=True)
                q2st = sbB("q2st", [128, 8, 512], BF16, nb=8, dsem=True)
                def load_os(ti_):
                    t0_, T_ = TILES[ti_]
                    if T_ >= 128:
                        S.dma("sp", xin.t[:, :, :], OO[s][1, :, t0_ // 128:t0_ // 128 + 4, :, :].rearrange("p u c j -> p u (c j)"), xin.ds, writes=xin.b)
                    else:
                        S.dma("sp", xin.t[:, 0, :].rearrange("p (c j) -> p c j", j=128)[:, :, 0:T_], OO[s][1, :, 32, :, 0:T_], xin.ds, writes=xin.b)
                    S.dma("sp", srt.t[:, :, :T_], SR[s].rearrange("(c p) t -> p c t", p=128)[:, :, t0_:t0_ + T_], srt.ds, writes=srt.b)

                for ti, (t0, T) in enumerate(TILES):
                    subs = subs_of(T)
                    if ti == 0:
                        load_os(0)
                    load_hT(T, t0, H1[s])
                    S.dma("sp", Ct.t[:, :T], c_cos[:, t0:t0 + T], Ct.ds, writes=Ct.b)
                    S.dma("sp", St.t[:, :T], c_sin[:, t0:t0 + T], St.ds, writes=St.b)
                    wid = min(128, T)
                    nsb = len(subs)
                    ofc = lambda c: xin.t[:, 0:nsb, c * 128:c * 128 + wid]
                    u3 = lambda ap: ap.rearrange("p (u j) -> p u j", j=wid)
                    for h in range(4):
                        ps, pb = psum()
                        for dvc in range(2):
                            c = 2 * h + dvc
                            S.op("act", lambda e: e.activation(out=u3(act.t[:, c, :T]), in_=ofc(c), func=AF.Square),
                                 reads=[xin.b[c]], writes=[act.b[c]])
                            S.op("pe", lambda e: e.matmul(ps[:, :T], ones_b.t[:], act.t[:, c, :T], start=(dvc == 0), stop=(dvc == 1)),
                                 reads=[ones_b.b[0], act.b[c]], writes=[pb], inc=(dvc == 1))
                        S.op("act", lambda e: e.activation(out=ms.t[:, :T], in_=ps[:, :T], func=AF.Ln, scale=1.0 / 256, bias=EPS), reads=[pb], writes=ms.b)
                        S.op("act", lambda e: e.activation(out=rstd.t[:, :T], in_=ms.t[:, :T], func=AF.Exp, scale=-0.5), reads=ms.b, writes=rstd.b)
                        for dvc in range(2):
                            c = 2 * h + dvc
                            S.op("dve", lambda e: e.scalar_tensor_tensor(out=ofc(c), in0=ofc(c),
                                                                         scalar=hncol.t[:, dvc:dvc + 1], in1=u3(rstd.t[:, :T]),
                                                                         op0=ALU.mult, op1=ALU.mult),
                                 reads=[xin.b[c], hncol.b[0], rstd.b[0]], writes=[xin.b[c]])
                            S.op("dve", lambda e: e.tensor_tensor(out=u3(hn.t[:, c, :T]), in0=ofc(c), in1=u3(srt.t[:, c, :T]), op=ALU.mult),
                                 reads=[xin.b[c], srt.b[c]], writes=[hn.b[c]])
                    if ti + 1 < len(TILES):
                        load_os(ti + 1)
                    for j in range(2):
                        sl = wget(("m512", "gla_w_out", j))
                        proj_fm(T, sl, 4, 0, lambda i, ps, pb: S.op(
                            "dve", lambda e: e.tensor_tensor(out=hT.t[:, j * 4 + i, :T], in0=ps[:, :T], in1=hT.t[:, j * 4 + i, :T], op=ALU.add),
                            reads=[pb, hT.b[j * 4 + i]], writes=[hT.b[j * 4 + i]]))
                    ffn(T, 0, 2)
                    ffn(T, 1, 1)
                    store_hT(T, t0, H4[s])
                    rmsnorm(T, 4)

                    pend = []

                    def qk_stage0(ps, pb, gi, dst_ap, dst_b):
                        q_ = qkt[qki[0] % 3]
                        qki[0] += 1
                        S.op("act", lambda e: e.copy(q_["xs"].t[:, :T], ps[:, :T]), reads=[pb], writes=q_["xs"].b)
                        S.op("act", lambda e: e.activation(out=q_["sqh"].t[:, :T], in_=ps[:, :T], func=AF.Square), reads=[pb], writes=q_["sqh"].b)
                        pend.append([q_, gi, dst_ap, dst_b, 0])
                        qk_advance(2)

                    def qk_stage1(it_):
                        q_, gi = it_[0], it_[1]
                        xs, xn, sqh, ms_, rs_ = q_["xs"], q_["xn"], q_["sqh"], q_["ms"], q_["rstd"]
                        p2, p2b = psum()
                        S.op("pe", lambda e: e.matmul(p2[:, :T], ones_b.t[:], sqh.t[:, :T], start=True, stop=True),
                             reads=[ones_b.b[0], sqh.b[0]], writes=[p2b])
                        S.op("act", lambda e: e.activation(out=ms_.t[:, :T], in_=p2[:, :T], func=AF.Ln, scale=1.0 / 128, bias=EPS), reads=[p2b], writes=ms_.b)
                        S.op("act", lambda e: e.activation(out=rs_.t[:, :T], in_=ms_.t[:, :T], func=AF.Exp, scale=-0.5), reads=ms_.b, writes=rs_.b)
                        S.op("dve", lambda e: e.scalar_tensor_tensor(out=xn.t[:, :T], in0=xs.t[:, :T], scalar=qkcol.t[:, gi:gi + 1],
                                                                     in1=rs_.t[:, :T], op0=ALU.mult, op1=ALU.mult),
                             reads=[xs.b[0], qkcol.b[0], rs_.b[0]], writes=xn.b)

                    def qk_stage2(it_):
                        q_, gi, dst_ap, dst_b = it_[0], it_[1], it_[2], it_[3]
                        xn, r1, r2 = q_["xn"], q_["r1"], q_["r2"]
                        p3, p3b = psum()
                        S.op("pe", lambda e: e.matmul(p3[:, :T], pmt.t[:], xn.t[:, :T], start=True, stop=True),
                             reads=[pmt.b[0], xn.b[0]], writes=[p3b])
                        S.op("pool", lambda e: e.tensor_tensor(out=r1.t[:, :T], in0=xn.t[:, :T], in1=Ct.t[:, :T], op=ALU.mult),
                             reads=[xn.b[0], Ct.b[0]], writes=r1.b)
                        S.op("dve", lambda e: e.tensor_tensor(out=r2.t[:, :T], in0=p3[:, :T], in1=St.t[:, :T], op=ALU.mult),
                             reads=[p3b, St.b[0]], writes=r2.b)
                        S.op("dve", lambda e: e.tensor_tensor(out=dst_ap, in0=r1.t[:, :T], in1=r2.t[:, :T], op=ALU.add),
                             reads=[r1.b[0], r2.b[0]], writes=dst_b)

                    def qk_advance(lag):
                        n_ = len(pend)
                        for k_, it_ in enumerate(pend):
                            age = n_ - 1 - k_
                            if it_[4] == 0 and age >= lag - 1:
                                qk_stage1(it_)
                                it_[4] = 1
                            elif it_[4] == 1 and age >= lag:
                                qk_stage2(it_)
                                it_[4] = 2
                        while pend and pend[0][4] == 2:
                            pend.pop(0)

                    def qk_flush():
                        while pend:
                            for it_ in pend:
                                if it_[4] == 0:
                                    qk_stage1(it_)
                                    it_[4] = 1
                                elif it_[4] == 1:
                                    qk_stage2(it_)
                                    it_[4] = 2
                            while pend and pend[0][4] == 2:
                                pend.pop(0)

                    for j in range(2):
                        sl = wget(("m512", "attn_w_in", j))
                        proj_fm(T, sl, 4, 0, lambda i, ps, pb: qk_stage0(ps, pb, 0, q2st.t[:, j * 4 + i, :T], [q2st.b[j * 4 + i]]))
                    sl = wget(("m512", "attn_w_in", 2))
                    proj_fm(T, sl, 2, 0, lambda i, ps, pb: qk_stage0(ps, pb, 1, KT.t[:, i, t0:t0 + T], [KT.b[i]]))
                    v = sl.t[:].rearrange("p (k c) -> p k c", k=8)
                    for u, (o, n) in enumerate(subs):
                        ps, pb = psum()
                        for kc in range(8):
                            S.op("pe", lambda e: e.matmul(ps[0:n, 0:256], hn.t[:, kc, o:o + n], v[:, kc, 256:512],
                                                          start=(kc == 0), stop=(kc == 7)),
                                 reads=[sl.b[0], hn.b[kc]], writes=[pb], inc=(kc == 7))
                        evac(VR.t[0:n, (t0 + o) // 128, :], ps[0:n, 0:256], [pb], VR.b)
                        if u == 0:
                            qk_advance(1)
                    qk_flush()
                    S.dma("pool", Q2[s].rearrange("(c p) t -> p c t", p=128)[:, :, t0:t0 + T], q2st.t[:, :, :T], q2st.ds, reads=q2st.b)
                S.barrier()
            if stop_after == "B":
                esKV.close()
                break

            with ExitStack() as esD:
                sbD = lambda name, shape, dt, nb=1, dsem=False: TT(S, esD, f"{name}_{s}", shape, dt, nb, dsem, semname=name)
                qts = [sbD(f"qt{i}", [128, 8, 512], BF16, dsem=True) for i in range(2)]
                pT = [sbD(f"pT{i}", [128, 512], BF16) for i in range(6)]
                rden = [sbD(f"rden{i}", [128, 512], F32) for i in range(2)]
                dacc = [[sbD(f"dacc{i}{j}", [128, 512], F32) for j in range(3)] for i in range(2)]
                yf = sbD("yf", [128, 8, 512], F32, nb=8)
                pst[1] = 4
                pti = 0
                def load_q(ti_):
                    t0_, T_ = TILES[1:][ti_]
                    q_ = qts[ti_ % 2]
                    S.dma("sp", q_.t[:, :, :T_], Q2[s].rearrange("(c p) t -> p c t", p=128)[:, :, t0_:t0_ + T_], q_.ds, writes=q_.b)

                load_q(0)
                for ti, (t0, T) in enumerate(TILES[1:]):
                    subs = subs_of(T)
                    qt = qts[ti % 2]
                    if ti + 1 < len(TILES) - 1:
                        load_q(ti + 1)
                    load_hT(T, t0, H4[s])
                    steps = [(head, blk) for head in range(8) for blk in range(33)]
                    LOOK = 3
                    sps = {}

                    def emit_s(i):
                        head, blk = steps[i]
                        kvh = head // 4
                        K_ = 128 if blk < 32 else NMETA
                        ps, pb = psum()
                        S.op("pe", lambda e: e.matmul(ps[0:K_, :T], KT.t[:, kvh, blk * 128:blk * 128 + K_], qt.t[:, head, :T],
                                                      start=True, stop=True), reads=[KT.b[kvh], qt.b[0]], writes=[pb])
                        sps[i] = (ps, pb)

                    for i in range(LOOK):
                        emit_s(i)
                    for i, (head, blk) in enumerate(steps):
                        kvh = head // 4
                        K_ = 128 if blk < 32 else NMETA
                        ps, pb = sps.pop(i)
                        a0 = 4 + 2 * (head % 2)
                        po, pob, pd, pdb = psb[a0], psB[a0], psb[a0 + 1], psB[a0 + 1]
                        p_ = pT[pti % 6]
                        pti += 1
                        S.op("act", lambda e: e.activation(out=p_.t[0:K_, :T], in_=ps[0:K_, :T], func=AF.Exp),
                             reads=[pb], writes=p_.b)
                        S.op("pe", lambda e: e.matmul(po[:, :T], VR.t[0:K_, blk, kvh * 128:(kvh + 1) * 128], p_.t[0:K_, :T],
                                                      start=(blk == 0), stop=(blk == 32)),
                             reads=[VR.b[0], p_.b[0]], writes=[pob], inc=True)
                        ai = blk % 3
                        if ai == 2:
                            S.op("pe", lambda e: e.matmul(pd[:, :T], ones_b.t[0:K_, :], p_.t[0:K_, :T], start=(blk == 2), stop=False),
                                 reads=[ones_b.b[0], p_.b[0]], writes=[pdb], inc=True)
                        else:
                            acc = dacc[head % 2][ai]
                            if blk < 2:
                                S.op("dve", lambda e: e.tensor_copy(acc.t[:, :T], p_.t[:, :T]), reads=p_.b, writes=acc.b)
                            else:
                                S.op("dve", lambda e: e.tensor_tensor(out=acc.t[0:K_, :T], in0=acc.t[0:K_, :T], in1=p_.t[0:K_, :T], op=ALU.add),
                                     reads=[acc.b[0], p_.b[0]], writes=acc.b)
                        if i + LOOK < len(steps):
                            emit_s(i + LOOK)
                        if blk == 32:
                            a_ = dacc[head % 2]
                            S.op("dve", lambda e: e.tensor_tensor(out=a_[0].t[:, :T], in0=a_[0].t[:, :T], in1=a_[1].t[:, :T], op=ALU.add),
                                 reads=[a_[0].b[0], a_[1].b[0]], writes=a_[0].b)
                            S.op("pe", lambda e: e.matmul(pd[:, :T], ones_f.t[:], a_[0].t[:, :T], start=False, stop=True),
                                 reads=[ones_f.b[0], a_[0].b[0]], writes=[pdb])
                            rd = rden[head % 2]
                            S.op("act", lambda e: e.activation(out=rd.t[:, :T], in_=pd[:, :T], func=AF.Ln), reads=[pdb], writes=rd.b)
                            S.op("act", lambda e: e.activation(out=rd.t[:, :T], in_=rd.t[:, :T], func=AF.Exp, scale=-1.0), reads=rd.b, writes=rd.b)
                            S.op("dve", lambda e: e.tensor_tensor(out=hn.t[:, head, :T], in0=po[:, :T], in1=rd.t[:, :T], op=ALU.mult),
                                 reads=[pob, rd.b[0]], writes=[hn.b[head]])
                    for j in range(2):
                        sl = wget(("m512", "attn_w_out", j))
                        proj_fm(T, sl, 4, 0, lambda i, ps, pb: S.op(
                            "dve", lambda e: e.tensor_tensor(out=hT.t[:, j * 4 + i, :T], in0=ps[:, :T], in1=hT.t[:, j * 4 + i, :T], op=ALU.add),
                            reads=[pb, hT.b[j * 4 + i]], writes=[hT.b[j * 4 + i]]))
                    ffn(T, 1, 2)
                    for c in range(8):
                        S.op("act", lambda e: e.activation(out=act.t[:, c, :T], in_=hT.t[:, c, :T], func=AF.Square),
                             reads=[hT.b[c]], writes=[act.b[c]])
                    ps, pb = psum()
                    for c in range(8):
                        S.op("pe", lambda e: e.matmul(ps[:, :T], ones_b.t[:], act.t[:, c, :T], start=(c == 0), stop=(c == 7)),
                             reads=[ones_b.b[0], act.b[c]], writes=[pb], inc=(c == 7))
                    S.op("act", lambda e: e.activation(out=ms.t[:, :T], in_=ps[:, :T], func=AF.Ln, scale=1.0 / D, bias=EPS), reads=[pb], writes=ms.b)
                    S.op("act", lambda e: e.activation(out=rstd.t[:, :T], in_=ms.t[:, :T], func=AF.Exp, scale=-0.5), reads=ms.b, writes=rstd.b)
                    for c in range(8):
                        S.op("dve", lambda e: e.scalar_tensor_tensor(out=yf.t[:, c, :T], in0=hT.t[:, c, :T], scalar=gcol.t[:, 6, c:c + 1],
                                                                     in1=rstd.t[:, :T], op0=ALU.mult, op1=ALU.mult),
                             reads=[hT.b[c], gcol.b[0], rstd.b[0]], writes=[yf.b[c]])
                    for u, (o, n) in enumerate(subs):
                        for cg in range(2):
                            ps, pb = psum()
                            for c4 in range(4):
                                c = cg * 4 + c4
                                S.op("pe", lambda e: e.transpose(ps[0:n, c4 * 128:(c4 + 1) * 128], yf.t[:, c, o:o + n], ident_f.t[:, :]),
                                     reads=[yf.b[c], ident_f.b[0]], writes=[pb], inc=(c4 == 3))
                            evac(xin.t[0:n, u, cg * 512:(cg + 1) * 512], ps[0:n, :], [pb], xin.b)
                    S.dma("pool", out[s, t0:t0 + T, :].rearrange("(u p) d -> p u d", p=128), xin.t[:, :, :], xst_ds, reads=xin.b)
                pst[1] = 8
                S.barrier()
            esKV.close()

        S.barrier()
        print("instructions:", S.n_ins, "weights consumed", wst["next"], "/", len(wkeys))
    return nc


def make_consts():
    c = {}
    c["c_ident"] = np.eye(128, dtype=np.float32)
    s_ = np.arange(128)[:, None]
    t_ = np.arange(128)[None, :]
    same = (s_ // 64) == (t_ // 64)
    c["c_maskf"] = (same & (s_ <= t_)).astype(np.float32)
    c["c_maskb"] = (same & (s_ >= t_)).astype(np.float32)
    m = np.ones((128, 512), np.float32)
    m[:, ::64] = 0.0
    c["c_scan"] = m
    pm = np.zeros((128, 128), np.float32)
    for d in range(128):
        if d % 64 < 32:
            pm[d, d + 32] = -1.0
        else:
            pm[d, d - 32] = 1.0
    c["c_pmt"] = np.ascontiguousarray(pm.T)
    inv = (10000.0 ** (-np.arange(0, 64, 2, dtype=np.float32) / 64.0)).astype(np.float32)
    pos = np.arange(SEQ)
    trow = np.concatenate([pos // 64, np.zeros(NMETA, np.int64)]).astype(np.float32)
    tcol = np.concatenate([pos % 64, np.zeros(NMETA, np.int64)]).astype(np.float32)
    ang = np.zeros((128, NT), np.float32)
    for d in range(128):
        ang[d] = (trow if d < 64 else tcol) * inv[d % 32]
    c["c_cos"] = np.cos(ang).astype(np.float32)
    c["c_sin"] = np.sin(ang).astype(np.float32)
    return c


_NC_CACHE = {}


def kernel(**inputs):
    n_cores = 8
    n_seq = 2
    if "nc" not in _NC_CACHE:
        _NC_CACHE["nc"] = build(n_seq=n_seq)
    nc = _NC_CACHE["nc"]
    consts = make_consts()
    x = np.ascontiguousarray(np.asarray(inputs["x"], dtype=np.float32))
    shared = {k: np.ascontiguousarray(np.asarray(v, dtype=np.float32)) for k, v in inputs.items() if k != "x"}
    shared.update(consts)
    in_maps = []
    for c in range(n_cores):
        m = dict(shared)
        m["x"] = x[c * n_seq:(c + 1) * n_seq]
        in_maps.append(m)
    res = run_bass_kernel_spmd(nc, in_maps, core_ids=list(range(n_cores)))
    return np.concatenate([r["out"] for r in res.results], axis=0).astype(np.float32)
```
